# Optimizing a Trainium2 kernel written in Bass

```python
import jax, jax.numpy as jnp
from jax import lax
import numpy as np

D_MODEL = 1024
BATCH = 2
SEQ = 8192
DEPTH = 1

EPS = 1e-6
CONV_WIDTH = D_MODEL
CONV_K = 3
GDN_HEADS = 8
GDN_DK = 128
GDN_DV = 128
GDN_CONV_K = 4
GDN_CHUNK = 64
GDN_QK = GDN_HEADS * GDN_DK
GDN_VW = GDN_HEADS * GDN_DV
D_FF = 256 * ((8 * D_MODEL // 3 + 255) // 256)
FFN_CONV_K = 3
IN_SPLITS = (CONV_WIDTH, CONV_WIDTH, CONV_WIDTH, GDN_QK, GDN_QK, GDN_VW, GDN_VW, GDN_HEADS, GDN_HEADS, D_MODEL, D_MODEL)
IN_WIDTH = sum(IN_SPLITS)

kernel_name = "hybrid_shortconv_gdn_gated_merge_convffn"


def _split_points():
    pts, acc = [], 0
    for w in IN_SPLITS[:-1]:
        acc += w
        pts.append(acc)
    return pts


def rmsnorm(x, g):
    xf = x.astype(jnp.float32)
    y = xf * lax.rsqrt(jnp.mean(xf * xf, axis=-1, keepdims=True) + EPS)
    return (y * g.astype(jnp.float32)).astype(x.dtype)


def l2norm(x):
    return x * lax.rsqrt(jnp.sum(x * x, axis=-1, keepdims=True) + EPS)


def causal_dwconv(x, w):
    K = w.shape[0]
    S = x.shape[1]
    xp = jnp.pad(x, ((0, 0), (K - 1, 0), (0, 0)))
    return sum(xp[:, j:j + S] * w[j] for j in range(K))


def short_conv_branch(bg, cg, xv, conv_w):
    return bg * causal_dwconv(cg * xv, conv_w)


def gated_delta_rule_chunked(q, k, v, g, beta):
    Bsz, S, H, dk = q.shape
    dv = v.shape[-1]
    C = GDN_CHUNK
    N = S // C

    def chunks(t):
        return t.reshape(Bsz, N, C, H, *t.shape[3:]).swapaxes(2, 3)

    q, k, v, g, beta = (chunks(t) for t in (q, k, v, g, beta))
    G = jnp.cumsum(g, axis=-1)
    tril = jnp.tril(jnp.ones((C, C), dtype=bool))
    strict = jnp.tril(jnp.ones((C, C), dtype=bool), -1)
    decay = jnp.exp(jnp.where(tril, G[..., :, None] - G[..., None, :], -jnp.inf))
    k_beta = k * beta[..., None]
    v_beta = v * beta[..., None]
    L = jnp.where(strict, jnp.einsum('bnhid,bnhjd->bnhij', k_beta, k) * decay, 0.0)
    eye = jnp.eye(C, dtype=q.dtype)
    rhs = jnp.concatenate([v_beta, k_beta * jnp.exp(G)[..., None]], axis=-1)
    sol = lax.linalg.triangular_solve(eye + L, rhs, left_side=True, lower=True)
    u, w = sol[..., :dv], sol[..., dv:]
    A_qk = jnp.einsum('bnhid,bnhjd->bnhij', q, k) * decay
    q_dec = q * jnp.exp(G)[..., None]
    G_last = G[..., -1:]
    k_dec = k * jnp.exp(G_last - G)[..., None]

    def step(state, inp):
        q_c, k_c, u_c, w_c, A_c, gl = inp
        v_new = u_c - jnp.einsum('bhcd,bhde->bhce', w_c, state)
        o = jnp.einsum('bhcd,bhde->bhce', q_c, state) + jnp.einsum('bhij,bhje->bhie', A_c, v_new)
        state = state * jnp.exp(gl)[..., None] + jnp.einsum('bhcd,bhce->bhde', k_c, v_new)
        return state, o

    xs = tuple(jnp.moveaxis(t, 1, 0) for t in (q_dec, k_dec, u, w, A_qk, G_last))
    state0 = jnp.zeros((Bsz, H, dk, dv), dtype=q.dtype)
    _, o = lax.scan(step, state0, xs)
    return o.transpose(1, 0, 3, 2, 4).reshape(Bsz, S, H, dv)


def gdn_branch(q, k, v, z, a, b, conv_w, A_log, dt_bias, norm_g):
    Bsz, S, _ = q.shape
    dtype = q.dtype
    f32 = jnp.float32
    qkv = jax.nn.silu(causal_dwconv(jnp.concatenate([q, k, v], axis=-1), conv_w))
    q, k, v = jnp.split(qkv, [GDN_QK, 2 * GDN_QK], axis=-1)
    q = l2norm(q.reshape(Bsz, S, GDN_HEADS, GDN_DK).astype(f32)) * (GDN_DK ** -0.5)
    k = l2norm(k.reshape(Bsz, S, GDN_HEADS, GDN_DK).astype(f32))
    v = v.reshape(Bsz, S, GDN_HEADS, GDN_DV).astype(f32)
    beta = jax.nn.sigmoid(b.astype(f32))
    g = -jnp.exp(A_log.astype(f32)) * jax.nn.softplus(a.astype(f32) + dt_bias.astype(f32))
    o = gated_delta_rule_chunked(q, k, v, g, beta)
    o = o * lax.rsqrt(jnp.mean(o * o, axis=-1, keepdims=True) + EPS) * norm_g.astype(f32)
    o = o * jax.nn.silu(z.reshape(Bsz, S, GDN_HEADS, GDN_DV).astype(f32))
    return o.reshape(Bsz, S, GDN_VW).astype(dtype)


def conv_ffn(h, w_up, conv_w, w_down):
    up = causal_dwconv(h @ w_up, conv_w)
    gate, val = jnp.split(up, [D_FF], axis=-1)
    return (jax.nn.silu(gate) * val) @ w_down


def setup_inputs(seed: int = 0) -> dict:
    key = jax.random.key(seed)
    ks = jax.random.split(key, 16)
    f32 = jnp.float32
    nrm = lambda k, shape, scale: jax.random.normal(k, shape, f32) * scale
    x = jax.random.normal(ks[0], (BATCH, SEQ, D_MODEL), f32)
    norm_mix_g = 1.0 + nrm(ks[1], (DEPTH, D_MODEL), 0.02)
    w_in = nrm(ks[2], (DEPTH, D_MODEL, IN_WIDTH), D_MODEL ** -0.5)
    conv_a_w = nrm(ks[3], (DEPTH, CONV_K, CONV_WIDTH), CONV_K ** -0.5)
    gdn_conv_w = nrm(ks[4], (DEPTH, GDN_CONV_K, 2 * GDN_QK + GDN_VW), GDN_CONV_K ** -0.5)
    gdn_A_log = jnp.log(jax.random.uniform(ks[5], (DEPTH, GDN_HEADS), f32, 1.0, 16.0))
    dt = jnp.exp(jax.random.uniform(ks[6], (DEPTH, GDN_HEADS), f32, np.log(1e-3), np.log(1e-1)))
    gdn_dt_bias = dt + jnp.log(-jnp.expm1(-dt))
    gdn_norm_g = 1.0 + nrm(ks[7], (DEPTH, GDN_DV), 0.02)
    w_a_out = nrm(ks[8], (DEPTH, CONV_WIDTH, D_MODEL), CONV_WIDTH ** -0.5)
    w_b_out = nrm(ks[9], (DEPTH, GDN_VW, D_MODEL), GDN_VW ** -0.5)
    w_o = nrm(ks[10], (DEPTH, D_MODEL, D_MODEL), D_MODEL ** -0.5)
    norm_ffn_g = 1.0 + nrm(ks[11], (DEPTH, D_MODEL), 0.02)
    w_up = nrm(ks[12], (DEPTH, D_MODEL, 2 * D_FF), D_MODEL ** -0.5)
    ffn_conv_w = nrm(ks[13], (DEPTH, FFN_CONV_K, 2 * D_FF), FFN_CONV_K ** -0.5)
    w_down = nrm(ks[14], (DEPTH, D_FF, D_MODEL), D_FF ** -0.5)
    norm_final_g = 1.0 + nrm(ks[15], (D_MODEL,), 0.02)
    return {"x": x, "norm_mix_g": norm_mix_g, "w_in": w_in, "conv_a_w": conv_a_w,
            "gdn_conv_w": gdn_conv_w, "gdn_A_log": gdn_A_log, "gdn_dt_bias": gdn_dt_bias,
            "gdn_norm_g": gdn_norm_g, "w_a_out": w_a_out, "w_b_out": w_b_out, "w_o": w_o,
            "norm_ffn_g": norm_ffn_g, "w_up": w_up, "ffn_conv_w": ffn_conv_w, "w_down": w_down,
            "norm_final_g": norm_final_g}


def reference(x, norm_mix_g, w_in, conv_a_w, gdn_conv_w, gdn_A_log, gdn_dt_bias, gdn_norm_g,
              w_a_out, w_b_out, w_o, norm_ffn_g, w_up, ffn_conv_w, w_down, norm_final_g):
    pts = _split_points()
    for l in range(DEPTH):
        h = rmsnorm(x, norm_mix_g[l])
        proj = h @ w_in[l]
        bg, cg, xv, q, k, v, z, a, b, ga, gb = jnp.split(proj, pts, axis=-1)
        y_a = short_conv_branch(bg, cg, xv, conv_a_w[l]) @ w_a_out[l]
        y_b = gdn_branch(q, k, v, z, a, b, gdn_conv_w[l], gdn_A_log[l], gdn_dt_bias[l],
                         gdn_norm_g[l]) @ w_b_out[l]
        mix = jax.nn.sigmoid(ga) * y_a + jax.nn.sigmoid(gb) * y_b
        x = x + mix @ w_o[l]
        h = rmsnorm(x, norm_ffn_g[l])
        x = x + conv_ffn(h, w_up[l], ffn_conv_w[l], w_down[l])
    return rmsnorm(x, norm_final_g)
```

```python
import numpy as np
from contextlib import ExitStack
import concourse.bass as bass
import concourse.mybir as mybir
from concourse.bass_utils import run_bass_kernel_spmd

F32 = mybir.dt.float32
BF16 = mybir.dt.bfloat16
AF = mybir.ActivationFunctionType
ALU = mybir.AluOpType

D = 1024
NH = 8
DFF = 2816
EPS = 1e-6
NCORES = 8
SEQ = 8192
OWN_FULL = 2048
SAME_SYNC = True
STAGGER = 11
DEBUG_TAGS = False

C_BG, C_CG, C_XV, C_Q, C_K, C_V, C_Z, C_A, C_B, C_GA, C_GB = 0, 1024, 2048, 3072, 4096, 5120, 6144, 7168, 7176, 7184, 8208


class Buf:
    __slots__ = ("name", "w", "rs", "psum")

    def __init__(self, name, psum=False):
        self.name = name
        self.w = None
        self.rs = []
        self.psum = psum


class Eng:
    def __init__(self, name):
        self.name = name
        self.ops = []
        self.count = 0
        self.known = {}


class Prog:
    def __init__(self):
        self.E = {n: Eng(n) for n in ("pe", "act", "dve", "pool", "sp")}
        self.dsem = {}

    def _deps(self, e, reads, writes):
        deps = {}

        def add(dep):
            if dep is None:
                return
            k, v = dep
            if deps.get(k, 0) < v:
                deps[k] = v

        for b in reads:
            add(b.w)
            if b.psum:
                for r in b.rs:
                    if r[0] != e.name:
                        add(r)
        for b in writes:
            add(b.w)
            for r in b.rs:
                add(r)
        waits = []
        for k, v in deps.items():
            if k == e.name and (k == "pe" or not SAME_SYNC):
                continue
            if e.known.get(k, 0) >= v:
                continue
            e.known[k] = v
            waits.append((k, v))
        return waits

    def op(self, eng, fn, reads=(), writes=()):
        e = self.E[eng]
        waits = self._deps(e, reads, writes)
        e.count += 1
        n = e.count
        if DEBUG_TAGS:
            import sys
            f = sys._getframe(1)
            tag = []
            for _ in range(4):
                if f is None:
                    break
                tag.append(str(f.f_lineno))
                f = f.f_back
            fn = (fn, "L" + "<".join(tag))
        e.ops.append((waits, fn, (eng, 1)))
        for b in writes:
            b.w = (eng, n)
            b.rs = []
        for b in reads:
            b.rs.append((eng, n))

    def dma(self, queue, fn, key, reads=(), writes=()):
        e = self.E[queue]
        waits = self._deps(e, reads, writes)
        self.dsem[key] = self.dsem.get(key, 0) + 16
        n = self.dsem[key]
        e.ops.append((waits, fn, (key, 16)))
        for b in writes:
            b.w = (key, n)
            b.rs = []
        for b in reads:
            b.rs.append((key, n))

    def final_wait(self, eng, deps):
        self.E[eng].ops.append((list(deps), None, None))


class _Stop(Exception):
    pass


def build(W=SEQ, OWN=OWN_FULL, dbg=None, stage=None, flags=()):
    def ck(n):
        if stage is not None and stage == n:
            raise _Stop()

    NT = W // 512
    T_OWN0 = (W - OWN) // 512
    NOWN = OWN // 512
    nc = bass.Bass("TRN2", target_bir_lowering=False)
    P = Prog()
    dbg = dbg or []
    dbg_out = {}

    def din(name, shape, dt=F32):
        return nc.dram_tensor(name, shape, dt, kind="ExternalInput").ap()

    xw = din("xw", [W, D])
    w_in = din("w_in", [D, 9232])
    w_a = din("w_a", [D, D])
    w_b = din("w_b", [D, D])
    w_o = din("w_o", [D, D])
    w_up = din("w_up", [D, 2 * DFF])
    w_dn = din("w_dn", [DFF, D])
    gmixB_d = din("gmixB", [128, D])
    gffnB_d = din("gffnB", [128, D])
    gfinB_d = din("gfinB", [128, D])
    cwA_d = din("cwA", [128, 8, 3])
    cwG_d = din("cwG", [128, 24, 4])
    cwF_d = din("cwF", [128, 44, 3])
    alog_d = din("alogB", [128, 32])
    dtb_d = din("dtbB", [128, 32])
    ng_d = din("ngP", [128, 1])
    cst_d = din("cst", [128, 5, 128])
    out_d = nc.dram_tensor("out", [OWN, D], F32, kind="ExternalOutput").ap()
    NUNIT2 = 37
    wsc = nc.dram_tensor("wsc", [8 + NUNIT2, 128, 8, 512], BF16, kind="Internal").ap()
    for name, shape in dbg:
        dbg_out[name] = nc.dram_tensor("dbg_" + name, shape, F32, kind="ExternalOutput").ap()

    es = ExitStack()
    SB_ACC = [0, []]
    build.sb_acc = SB_ACC

    class T:
        def __init__(self, name, shape, dt=F32):
            self.t = es.enter_context(nc.sbuf_tensor("sb_" + name, shape, dt))
            self.b = Buf(name)
            SB_ACC[0] += int(np.prod(shape[1:])) * (2 if dt == BF16 else 4)
            SB_ACC[1].append((name, int(np.prod(shape[1:])) * (2 if dt == BF16 else 4)))

        def __getitem__(self, k):
            return self.t[k]

    class PS:
        def __init__(self, name, shape, dt=F32):
            self.t = es.enter_context(nc.psum_tensor(name, shape, dt))
            self.b = Buf(name, psum=True)

        def __getitem__(self, k):
            return self.t[k]

    class View:
        def __init__(self, ap, name):
            self.ap = ap
            self.b = Buf(name)

        def __getitem__(self, k):
            return self.ap[k]

    arena = T("arena", [128, 11264], BF16)
    arena2 = T("arena2", [128, 12288], BF16)
    arenas = {1: (arena, 11264, [0]), 2: (arena2, 12288, [0])}

    def AV(name, shape, dt=F32, which=1):
        ar, cap, aoff = arenas[which]
        n = int(np.prod(shape[1:]))
        nb = n * (1 if dt == BF16 else 2)
        ap = ar.t[:, aoff[0]:aoff[0] + nb]
        aoff[0] += nb
        assert aoff[0] <= cap
        if dt != BF16:
            ap = ap.bitcast(F32)
        if len(shape) == 3:
            ap = ap.rearrange("p (a b) -> p a b", a=shape[1])
        return View(ap, name)

    def barrier():
        for en, e in P.E.items():
            waits = [(k_, e2.count) for k_, e2 in P.E.items() if k_ != en and e2.count > 0]
            waits += [(k_, v_) for k_, v_ in P.dsem.items() if not k_.startswith("wc")]
            e.ops.append((waits, None, None))
            for k_, v_ in waits:
                e.known[k_] = max(e.known.get(k_, 0), v_)

    def B(xs):
        return [x.b if hasattr(x, "b") else x for x in xs]

    def ACT(out, in_, func, r, w, bias=0.0, scale=1.0, accum=None):
        if accum is None:
            P.op("act", lambda e: e.activation(out=out, in_=in_, func=func, bias=bias, scale=scale), B(r), B(w))
        else:
            P.op("act", lambda e: e.activation(out=out, in_=in_, func=func, bias=bias, scale=scale, accum_out=accum), B(r), B(w))

    def TS(eng, out, in0, s1, s2, op0, op1, r, w):
        if op1 is None:
            P.op(eng, lambda e: e.tensor_scalar(out=out, in0=in0, scalar1=s1, scalar2=None, op0=op0), B(r), B(w))
        else:
            P.op(eng, lambda e: e.tensor_scalar(out=out, in0=in0, scalar1=s1, scalar2=s2, op0=op0, op1=op1), B(r), B(w))

    def STT(eng, out, in0, scalar, in1, op0, op1, r, w):
        P.op(eng, lambda e: e.scalar_tensor_tensor(out=out, in0=in0, scalar=scalar, in1=in1, op0=op0, op1=op1), B(r), B(w))

    def TT(eng, out, in0, in1, op, r, w):
        P.op(eng, lambda e: e.tensor_tensor(out=out, in0=in0, in1=in1, op=op), B(r), B(w))

    def CP(eng, out, in_, r, w):
        if eng == "act":
            P.op("act", lambda e: e.copy(out=out, in_=in_), B(r), B(w))
        else:
            P.op(eng, lambda e: e.tensor_copy(out=out, in_=in_), B(r), B(w))

    def MM(mms, r, w):
        def fn(e):
            ins = None
            for (o, l, rh, st, sp) in mms:
                ins = e.matmul(o, l, rh, start=st, stop=sp)
            return ins
        P.op("pe", fn, B(r), B(w))

    def TR(trs, r, w):
        def fn(e):
            ins = None
            for (o, i, idn) in trs:
                ins = e.transpose(o, i, idn)
            return ins
        P.op("pe", fn, B(r), B(w))

    def DMA(queue, out, in_, key, r, w):
        q = {"sp": "sp", "pool": "pool", "act": "act"}[queue]
        P.dma(q, lambda e: e.dma_start(out=out, in_=in_), key, B(r), B(w))

    cst = T("cst", [128, 5, 128])
    cstb = T("cstb", [128, 128], BF16)
    cstb2 = T("cstb2", [128, 2, 128], BF16)
    gphls = [T(f"gphl{i}", [128, 2, 32], BF16) for i in range(2)]
    gmixB = T("gmixB", [128, D])
    gffnB = T("gffnB", [128, D])
    gfinB = T("gfinB", [128, D])
    cwA = T("cwA", [128, 8, 3])
    cwG = T("cwG", [128, 24, 4])
    cwF = T("cwF", [128, 44, 3])
    negA = T("negA", [128, 32])
    dtb = T("dtb", [128, 32])
    ngP = T("ngP", [128, 1])
    wab = T("wab", [128, 8, 16], BF16)
    IDF = cst[:, 0, :]
    UTI = cst[:, 1, :]
    NEGSU = cst[:, 2, :]
    POSSL = cst[:, 3, :]
    ONES = cst[:, 4, :]
    IDB = cstb[:, :]

    xt = [T(f"xt{i}", [128, D]) for i in range(4)]
    xn = T("xn", [128, D], BF16)
    junk = T("junk", [128, D], BF16)
    st_small = [T(f"st{i}", [128, 4]) for i in range(2)]
    hT = T("hT", [128, 8, 512], BF16)
    h2T = hT
    wslot = [T(f"wslot{i}", [128, 8, 512], BF16) for i in range(3)]
    qkvw = [T(f"qkvw{i}", [128, 8, 384], BF16) for i in range(2)]
    pre = [T(f"pre{i}", [128, 515]) for i in range(2)]
    cacc = T("cacc", [128, 512])
    post = T("post", [128, 512])
    sqb = T("sqb", [128, 512])
    rr = T("rr", [128, 512])
    NSETS = 5
    qTs = [T(f"qT{i}", [128, 512], BF16) for i in range(NSETS)]
    kTs = [T(f"kT{i}", [128, 512], BF16) for i in range(NSETS)]
    vTs = [T(f"vT{i}", [128, 512], BF16) for i in range(NSETS)]
    oTs = [T(f"oT{i}", [128, 512]) for i in range(2)] + [AV("oT2", [128, 512], F32, 2)]
    oT = oTs[0]
    carG = T("carG", [128, 24, 3])
    carA = T("carA", [128, 8, 2])
    carF = T("carF", [128, 44, 2])
    betaPs = [T(f"betaP{i}", [128, 32]) for i in range(2)]
    gPs = [T(f"gP{i}", [128, 32]) for i in range(2)]
    GPs = [T(f"GP{i}", [128, 32]) for i in range(2)]
    kegP = T("kegP", [128, 32])
    s1Ps = [T(f"s1P{i}", [128, 32]) for i in range(2)]
    tmpP = T("tmpP", [128, 32])
    NPIPE = 3
    def PT(p, name, shape, dt=F32):
        return T(name, shape, dt) if p == 0 else AV(name, shape, dt, p)

    Fs = [[PT(p, f"F{p}_{j}", [128, 512]) for j in range(3)] for p in range(NPIPE)]
    gm0_ = T("gm0", [128, 4, 4, 128], BF16)
    gm = [gm0_ for p in range(NPIPE)]
    Vb = [[PT(p, f"V{p}_{j}", [128, 512], BF16) for j in range(2)] for p in range(NPIPE)]
    VTb = [[PT(p, f"VT{p}_{j}", [128, 512], BF16) for j in range(2)] for p in range(NPIPE)]
    Rb = [[PT(p, f"R{p}_{j}", [128, 512], BF16) for j in range(2)] for p in range(NPIPE)]
    ATb = [PT(p, f"AT{p}", [128, 512], BF16) for p in range(NPIPE)]
    QATb = [PT(p, f"QAT{p}", [128, 512], BF16) for p in range(NPIPE)]
    vbb = [PT(p, f"vb{p}", [128, 4, 128], BF16) for p in range(NPIPE)]
    kbgb = [PT(p, f"kbg{p}", [128, 4, 128], BF16) for p in range(NPIPE)]
    kdecb = [PT(p, f"kdec{p}", [128, 4, 128], BF16) for p in range(NPIPE)]
    ub = [PT(p, f"u{p}", [128, 512], BF16) for p in range(NPIPE)]
    wb = [PT(p, f"w{p}", [128, 512], BF16) for p in range(NPIPE)]
    nKWTb = [PT(p, f"nKWT{p}", [128, 512], BF16) for p in range(NPIPE)]
    kbTs = [PT(p, f"kbT{p}", [128, 512], BF16) for p in range(NPIPE)]
    sm4 = [T(f"sm4_{p}", [128, 16]) for p in range(NPIPE)]
    bphls = [T(f"bphl{i}", [128, 2, 32], BF16) for i in range(2)]
    Sf = [T(f"Sf{h}", [128, 128]) for h in range(NH)]
    Sb = [[T(f"Sb{h}_{j}", [128, 128], BF16) for j in range(2)] for h in range(NH)]
    sb_par = [0] * NH
    ob = T("ob", [128, 8, 512], BF16)
    yap = View(arena2.t[:, 0:4096].rearrange("p (f n) -> p f n", f=8), "yap")
    sz = View(arena2.t[:, 4096:8192].rearrange("p (f n) -> p f n", f=8), "sz")
    mixT = View(arena2.t[:, 8192:12288].rearrange("p (f n) -> p f n", f=8), "mixT")
    sga = rr
    sgb = oT
    cgs = cacc
    uw = pre
    actT = View(arena.t[:, :].rearrange("p (f n) -> p f n", f=22), "actT")
    cvg = post
    cvv = sqb
    yout = [T(f"yout{i}", [128, D]) for i in range(1)]

    class Slot:
        def __init__(self, ap, b):
            self.ap = ap
            self.b = b

        def __getitem__(self, k):
            return self.ap[k]

    ps_proj = [PS(f"ps_proj{i}", [128, 512]) for i in range(2)]
    ps_small = ps_proj[0]
    ps_gb = [Slot(ps_small[:, 128:384], ps_small.b) for i in range(2)]
    ps_mm_t = [PS(f"ps_mm{i}", [128, 512]) for i in range(6)]


    mm_slots = [Slot(ps_mm_t[i][:, :], ps_mm_t[i].b) for i in range(6)]
    bank_free = list(mm_slots)
    mm_i = [0]

    def mslot():
        s = mm_slots[mm_i[0] % 3]
        mm_i[0] += 1
        return s

    proj_i = [0]

    def pslot():
        s = ps_proj[proj_i[0] % 2]
        proj_i[0] += 1
        return s

    cl = [(cst, cst_d), (gmixB, gmixB_d), (gffnB, gffnB_d), (gfinB, gfinB_d), (cwA, cwA_d), (cwG, cwG_d),
          (cwF, cwF_d), (negA, alog_d), (dtb, dtb_d), (ngP, ng_d)]
    for t, d_ in cl:
        DMA("sp", t[:], d_, "const", [], [t])
    for t, d_ in cl:
        t.b.w = ("const", P.dsem["const"])
    CP("dve", cstb[:, :], cst[:, 0, :], [cst], [cstb])
    CP("dve", cstb2[:, 0, :], cst[:, 1, :], [cst], [cstb2])
    CP("dve", cstb2[:, 1, :], cst[:, 4, :], [cst], [cstb2])
    ACT(negA[:, :], negA[:, :], AF.Exp, [negA], [negA])
    TS("dve", negA[:, :], negA[:, :], -1.0, None, ALU.mult, None, [negA], [negA])
    for t in (carG, carA, carF):
        P.op("pool", lambda e, t=t: e.memset(t[:], 0.0), [], [t.b])
    for h in range(NH):
        P.op("pool", lambda e, h=h: e.memset(Sf[h][:, :], 0.0), [], [Sf[h].b])
        P.op("pool", lambda e, h=h: e.memset(Sb[h][0][:, :], 0.0), [], [Sb[h][0].b])

    wsc_b = [Buf(f"wsc{u}") for u in range(8 + NUNIT2)]
    stopped = [False]

    uext = {}

    def cast_piece(u, col0, src, r0, nk, c0, width, key="wcast"):
        uext[u] = (nk, max(uext.get(u, (0, 0))[1], col0 + width))
        s = src[r0:r0 + nk * 128, c0:c0 + width].rearrange("(k p) n -> p k n", p=128)
        cast_dma(wsc[u, :, 0:nk, col0:col0 + width], s)

    cast_n = [0]
    dummy = T("dummy", [128, 4])

    def cast_dma(out, in_):
        if "nocast" in flags:
            return
        key = f"wc{cast_n[0] % 6}"
        cast_n[0] += 1
        if P.dsem.get(key, 0) > 0:
            P.E["pool"].ops.append(([(key, P.dsem[key])], None, None))
        DMA("pool", out, in_, key, [], [])

    def cast_barrier(bufs):
        fake = []
        for j in range(6):
            key = f"wc{j}"
            if P.dsem.get(key, 0) > 0:
                fb = Buf("fk")
                fb.w = (key, P.dsem[key])
                fake.append(fb)
        P.op("pool", lambda e: e.memset(dummy[:, :], 0.0), fake, [dummy.b])
        for b_ in bufs:
            b_.w = dummy.b.w

    cast_dma(wab[:, :, :], w_in[:, C_A:C_A + 16].rearrange("(k p) n -> p k n", p=128))
    for h in range(NH):
        for j, cb in enumerate((C_Q, C_K, C_V)):
            cast_piece(h, j * 128, w_in, 0, 8, cb + h * 128, 128, key="wcast0")
    cast_barrier([wab.b] + [wsc_b[h] for h in range(NH)])
    units = []
    u = 8
    U_A = []
    for ct in range(8):
        for j, cb in enumerate((C_BG, C_CG, C_XV)):
            cast_piece(u, j * 128, w_in, 0, 8, cb + ct * 128, 128)
        U_A.append(u); u += 1
    U_Z = []
    for i in range(2):
        cast_piece(u, 0, w_in, 0, 8, C_Z + i * 512, 512)
        U_Z.append(u); u += 1
    U_G = []
    for nt in range(8):
        cast_piece(u, 0, w_in, 0, 8, C_GA + nt * 128, 128)
        cast_piece(u, 128, w_in, 0, 8, C_GB + nt * 128, 128)
        cast_piece(u, 256, w_a, 0, 8, nt * 128, 128)
        cast_piece(u, 384, w_b, 0, 8, nt * 128, 128)
        U_G.append(u); u += 1
    U_O = []
    for i in range(2):
        cast_piece(u, 0, w_o, 0, 8, i * 512, 512)
        U_O.append(u); u += 1
    U_UP = []
    for fp in range(11):
        for j in range(2):
            ft = fp * 2 + j
            cast_piece(u, j * 256, w_up, 0, 8, ft * 128, 128)
            cast_piece(u, j * 256 + 128, w_up, 0, 8, DFF + ft * 128, 128)
        U_UP.append(u); u += 1
    U_D = []
    for nh in range(2):
        row = []
        for kg, (k0, nk) in enumerate(((0, 8), (8, 8), (16, 6))):
            cast_piece(u, 0, w_dn, k0 * 128, nk, nh * 512, 512)
            row.append((u, nk)); u += 1
        U_D.append(row)
    assert u == 8 + NUNIT2
    cast_barrier([wsc_b[uu] for uu in range(8, 8 + NUNIT2)])

    ws_state = {"q": [], "n": 0}

    def ws_issue(unit):
        i = ws_state["n"] % 3
        ws_state["n"] += 1
        nk, ncol = uext[unit]
        DMA("sp", wslot[i][:, 0:nk, 0:ncol], wsc[unit, :, 0:nk, 0:ncol], f"ws{i}", [wsc_b[unit]], [wslot[i]])
        return wslot[i]

    class WStream:
        def __init__(self, seq, ahead=2):
            self.seq = list(seq)
            self.loaded = []
            self.pos = 0
            self.ahead = ahead

        def get(self, ahead=None):
            ahead = self.ahead if ahead is None else ahead
            while len(self.loaded) < min(len(self.seq), self.pos + 1 + ahead):
                self.loaded.append(ws_issue(self.seq[len(self.loaded)]))
            s = self.loaded[self.pos]
            self.pos += 1
            return s

    sidx = [0]

    def norm_to_T(src_tile, gB, dstT, sub, bank=None):
        bank = bank if bank is not None else bank_free[0]
        ps_tr = Slot(bank[:, 0:512].bitcast(BF16), bank.b)
        st = st_small[sidx[0] % 2]
        sidx[0] += 1
        ACT(junk[:, :], src_tile[:, :], AF.Square, [src_tile], [junk, st], accum=st[:, 0:1])
        ACT(st[:, 1:2], st[:, 0:1], AF.Ln, [st], [st], bias=EPS, scale=1.0 / D)
        ACT(st[:, 2:3], st[:, 1:2], AF.Exp, [st], [st], scale=-0.5)
        STT("dve", xn[:, :], src_tile[:, :], st[:, 2:3], gB[:, :], ALU.mult, ALU.mult, [src_tile, st, gB], [xn])
        TR([(ps_tr[:, k * 128:(k + 1) * 128], xn[:, k * 128:(k + 1) * 128], IDB) for k in range(8)], [xn, cstb], [ps_tr])
        CP("act", dstT[:, :, sub * 128:(sub + 1) * 128], ps_tr[:, :].rearrange("p (k n) -> p k n", k=8), [ps_tr], [dstT])
        return st

    def dump(name, ap, r):
        if name in dbg_out:
            DMA("sp", dbg_out[name], ap, "dbg", r, [])

    def dwconv(work, carry_ap, K, wts, n, outp, r_w, w_out, car_t):
        CP("pool", work[:, 0:K - 1], carry_ap, [car_t], [work])
        TS("dve", outp, work[:, 0:n], wts(0), None, ALU.mult, None, [work] + r_w, w_out)
        for j in range(1, K):
            STT("dve", outp, work[:, j:j + n], wts(j), outp, ALU.mult, ALU.add, [work] + r_w + w_out, w_out)
        CP("pool", carry_ap, work[:, n:n + K - 1], [work], [car_t])

    cb_i = [0]
    bc_i = lambda ap4: ap4.unsqueeze(2).broadcast_to([128, 4, 128])
    bc_c = lambda ap: ap.unsqueeze(1).broadcast_to([128, 4, 128])
    v4 = lambda ap: ap.rearrange("p (c i) -> p c i", c=4)
    UTIB = cstb2[:, 0, :]
    ONESB = cstb2[:, 1, :]

    def acquire(n):
        while len(bank_free) < n:
            yield
        return [bank_free.pop(0) for _ in range(n)]

    def release(*bs):
        for b_ in bs:
            bank_free.append(b_)

    def stageA(tt, h, wq, qs):
        qT, kT, vT = qTs[qs], kTs[qs], vTs[qs]
        cc = [cacc, post, sqb]
        for j in range(3):
            ps = pslot()
            MM([(ps[:, :], wq[:, k, j * 128:(j + 1) * 128], hT[:, k, :], k == 0, k == 7) for k in range(8)], [wq, hT], [ps])
            pw = pre[(cb_i[0]) % 2]
            cb_i[0] += 1
            CP("act", pw[:, 3:515], ps[:, :], [ps], [pw])
            ch = j * 8 + h
            dwconv(pw, carG[:, ch, :], 4, lambda jj, ch=ch: cwG[:, ch, jj:jj + 1], 512, cc[j][:, :], [cwG], [cc[j]], carG)
            yield
        ACT(cc[0][:, :], cc[0][:, :], AF.Silu, [cc[0]], [cc[0]])
        ACT(cc[1][:, :], cc[1][:, :], AF.Silu, [cc[1]], [cc[1]])
        ACT(vT[:, :], cc[2][:, :], AF.Silu, [cc[2]], [vT])
        yield
        for src, dst, scale in ((cc[0], qT, 128 ** -0.5), (cc[1], kT, 1.0)):
            ACT(sqb[:, :], src[:, :], AF.Square, [src], [sqb])
            ps2 = pslot()
            CP("pool", junk[:, 0:512], sqb[:, :], [sqb], [junk])
            MM([(ps2[:, :], ONESB, junk[:, 0:512], True, True)], [junk, cstb2], [ps2])
            ACT(rr[:, :], ps2[:, :], AF.Ln, [ps2], [rr], bias=EPS)
            ACT(rr[:, :], rr[:, :], AF.Exp, [rr], [rr], scale=-0.5)
            STT("dve", dst[:, :], src[:, :], scale, rr[:, :], ALU.mult, ALU.mult, [src, rr], [dst])
            yield

    def stageC(tt, h, qs, p=0):
        qT, kT, vT = qTs[qs], kTs[qs], vTs[qs]
        sl = slice(h * 4, h * 4 + 4)
        F0, F1, F2 = Fs[p]
        G, V, VT, R = gm[p], Vb[p], VTb[p], Rb[p]
        AT, QAT, vb, kbg, kdec, u_, w_, nK, sm = ATb[p], QATb[p], vbb[p], kbgb[p], kdecb[p], ub[p], wb[p], nKWTb[p], sm4[p]
        kbT, oT = kbTs[p], oTs[p]
        tp = tt % 2
        betaP, GP, s1P, gphl, bphl = betaPs[tp], GPs[tp], s1Ps[tp], gphls[tp], bphls[tp]
        f2 = lambda t3: t3.rearrange("p c i -> p (c i)")
        cs = lambda c: slice(c * 128, (c + 1) * 128)
        bG, bB = yield from acquire(2)
        TT("pool", G[:, 0], bc_c(UTIB), bc_i(gphl[:, 0, sl]), ALU.mult, [cstb2, gphl], [G])
        TT("dve", G[:, 1], bc_c(UTIB), bc_i(gphl[:, 1, sl]), ALU.mult, [cstb2, gphl], [G])
        TT("pool", G[:, 2], bc_c(IDB), bc_i(bphl[:, 0, sl]), ALU.mult, [cstb, bphl], [G])
        TT("dve", G[:, 3], bc_c(IDB), bc_i(bphl[:, 1, sl]), ALU.mult, [cstb, bphl], [G])
        MM([(bG[:, 0:512], ONESB, f2(G[:, 0]), True, False), (bG[:, 0:512], ONESB, f2(G[:, 1]), False, True)], [G, cstb2], [bG])
        MM([(bB[:, 0:512], ONESB, f2(G[:, 2]), True, False), (bB[:, 0:512], ONESB, f2(G[:, 3]), False, True)], [G, cstb2], [bB])
        yield
        TT("dve", v4(F0[:, :]), v4(bG[:, 0:512]), bc_i(GP[:, sl]), ALU.subtract, [bG, GP], [F0])
        ACT(F2[:, :], bG[:, 0:512], AF.Exp, [bG], [F2])
        CP("act", sm[:, 0:4], v4(bG[:, 0:512])[:, :, 127], [bG], [sm])
        TT("dve", kbT[:, :], kT[:, :], bB[:, 0:512], ALU.mult, [kT, bB], [kbT])
        release(bG, bB)
        yield
        TT("dve", v4(F1[:, :]), v4(F0[:, :]), bc_c(POSSL), ALU.add, [F0, cst], [F1])
        TT("dve", v4(F0[:, :]), v4(F0[:, :]), bc_c(NEGSU), ALU.add, [F0, cst], [F0])
        TT("dve", sm[:, 4:8], sm[:, 0:4], GP[:, sl], ALU.subtract, [sm, GP], [sm])
        yield
        ACT(F1[:, :], F1[:, :], AF.Exp, [F1], [F1], scale=-1.0)
        ACT(F0[:, :], F0[:, :], AF.Exp, [F0], [F0])
        ACT(sm[:, 8:12], sm[:, 4:8], AF.Exp, [sm], [sm])
        ACT(sm[:, 12:16], sm[:, 0:4], AF.Exp, [sm], [sm])
        TT("pool", F2[:, :], qT[:, :], F2[:, :], ALU.mult, [qT, F2], [F2])
        bL, bU, bA = yield from acquire(3)
        MM([(bL[:, cs(c)], kbT[:, cs(c)], kT[:, cs(c)], True, True) for c in range(4)], [kbT, kT], [bL])
        MM([(bU[:, cs(c)], kT[:, cs(c)], kbT[:, cs(c)], True, True) for c in range(4)], [kbT, kT], [bU])
        MM([(bA[:, cs(c)], kT[:, cs(c)], qT[:, cs(c)], True, True) for c in range(4)], [qT, kT], [bA])
        yield
        STT("dve", VT[0][:, :], bL[:, 0:512], -1.0, F1[:, :], ALU.mult, ALU.mult, [bL, F1], [VT[0]])
        STT("dve", V[0][:, :], bU[:, 0:512], -1.0, F0[:, :], ALU.mult, ALU.mult, [bU, F0], [V[0]])
        TT("dve", v4(F0[:, :]), v4(F0[:, :]), bc_c(IDF), ALU.add, [F0, cst], [F0])
        TT("dve", AT[:, :], bA[:, 0:512], F0[:, :], ALU.mult, [bA, F0], [AT])
        TT("pool", v4(R[0][:, :]), v4(V[0][:, :]), bc_c(IDB), ALU.add, [V[0], cstb], [R[0]])
        release(bL, bU, bA)
        (bX,) = yield from acquire(1)
        bXb = bX[:, 0:512].bitcast(BF16)
        TR([(bXb[:, cs(c)], kT[:, cs(c)], IDB) for c in range(4)] +
           [(bXb[:, 512 + c * 128:512 + (c + 1) * 128], vT[:, cs(c)], IDB) for c in range(4)], [kT, vT, cstb], [bX])
        bT, bV = yield from acquire(2)
        MM([(bT[:, cs(c)], V[0][:, cs(c)], VT[0][:, cs(c)], True, True) for c in range(4)], [V[0], VT[0]], [bT])
        MM([(bV[:, cs(c)], VT[0][:, cs(c)], V[0][:, cs(c)], True, True) for c in range(4)], [V[0], VT[0]], [bV])
        yield
        TT("dve", kbg[:, :, :], v4(bXb[:, 0:512]), bc_i(s1P[:, sl]), ALU.mult, [bX, s1P], [kbg])
        TT("dve", kdec[:, :, :], v4(bXb[:, 0:512]), bc_i(sm[:, 8:12]), ALU.mult, [bX, sm], [kdec])
        TT("dve", vb[:, :, :], v4(bXb[:, 512:1024]), bc_i(betaP[:, sl]), ALU.mult, [bX, betaP], [vb])
        release(bX)
        cur = 0
        for m in range(1, 7):
            nxt = 1 - cur
            CP("act", VT[nxt][:, :], bT[:, 0:512], [bT], [VT[nxt]])
            release(bT)
            if m < 6:
                CP("dve", V[nxt][:, :], bV[:, 0:512], [bV], [V[nxt]])
                release(bV)
            if m > 1:
                TT("dve", R[cur][:, :], R[1 - cur][:, :], bR[:, 0:512], ALU.add, [R[1 - cur], bR], [R[cur]])
                release(bR)
            yield
            (bR,) = yield from acquire(1)
            MM([(bR[:, cs(c)], VT[nxt][:, cs(c)], R[cur][:, cs(c)], True, True) for c in range(4)], [R[cur], VT[nxt]], [bR])
            if m < 5:
                bT, bV = yield from acquire(2)
            elif m == 5:
                (bT,) = yield from acquire(1)
            if m < 6:
                MM([(bT[:, cs(c)], V[nxt][:, cs(c)], VT[nxt][:, cs(c)], True, True) for c in range(4)], [V[nxt], VT[nxt]], [bT])
            if m < 5:
                MM([(bV[:, cs(c)], VT[nxt][:, cs(c)], V[nxt][:, cs(c)], True, True) for c in range(4)], [V[nxt], VT[nxt]], [bV])
            cur = nxt
            yield
        TT("dve", R[cur][:, :], R[1 - cur][:, :], bR[:, 0:512], ALU.add, [R[1 - cur], bR], [R[cur]])
        release(bR)
        Rf = R[cur]
        yield
        b0, b1 = yield from acquire(2)
        MM([(b0[:, cs(c)], Rf[:, cs(c)], vb[:, c, :], True, True) for c in range(4)], [Rf, vb], [b0])
        MM([(b1[:, cs(c)], Rf[:, cs(c)], kbg[:, c, :], True, True) for c in range(4)], [Rf, kbg], [b1])
        yield
        CP("act", u_[:, :], b0[:, 0:512], [b0], [u_])
        CP("dve", w_[:, :], b1[:, 0:512], [b1], [w_])
        release(b0, b1)
        yield
        bK, bQ = yield from acquire(2)
        MM([(bK[:, cs(c)], w_[:, cs(c)], kdec[:, c, :], True, True) for c in range(4)], [w_, kdec], [bK])
        MM([(bQ[:, cs(c)], w_[:, cs(c)], AT[:, cs(c)], True, True) for c in range(4)], [w_, AT], [bQ])
        yield
        P.op("act", lambda e: e.mul(out=nK[:, :], in_=bK[:, 0:512], mul=-1.0), B([bK]), B([nK]))
        TT("dve", QAT[:, :], F2[:, :], bQ[:, 0:512], ALU.subtract, [F2, bQ], [QAT])
        release(bK, bQ)
        yield
        for c in range(4):
            so = sb_par[h]
            S_old = Sb[h][so]
            S_new = Sb[h][1 - so]
            sb_par[h] = 1 - so
            (bS,) = yield from acquire(1)
            MM([(bS[:, 128:256], kdec[:, c, :], u_[:, cs(c)], True, False),
                (bS[:, 128:256], nK[:, cs(c)], S_old[:, :], False, True),
                (bS[:, 0:128], S_old[:, :], QAT[:, cs(c)], True, False),
                (bS[:, 0:128], u_[:, cs(c)], AT[:, cs(c)], False, True)],
               [S_old, QAT, u_, AT, kdec, nK], [bS])
            yield
            gam = sm[:, 12 + c:13 + c]
            STT("dve", S_new[:, :], Sf[h][:, :], gam, bS[:, 128:256], ALU.mult, ALU.add, [Sf[h], sm, bS], [S_new])
            STT("dve", Sf[h][:, :], Sf[h][:, :], gam, bS[:, 128:256], ALU.mult, ALU.add, [Sf[h], sm, bS], [Sf[h]])
            CP("dve", oT[:, cs(c)], bS[:, 0:128], [bS], [oT])
            release(bS)
            yield
        ACT(F0[:, :], oT[:, :], AF.Square, [oT], [F0])
        CP("pool", V[0][:, :], F0[:, :], [F0], [V[0]])
        yield
        (b2,) = yield from acquire(1)
        MM([(b2[:, 0:512], ONESB, V[0][:, :], True, True)], [V[0], cstb2], [b2])
        yield
        ACT(F1[:, :], b2[:, 0:512], AF.Ln, [b2], [F1], bias=EPS, scale=1.0 / 128)
        release(b2)
        ACT(F1[:, :], F1[:, :], AF.Exp, [F1], [F1], scale=-0.5)
        STT("dve", ob[:, h, :], oT[:, :], ngP[:, 0:1], F1[:, :], ALU.mult, ALU.mult, [oT, F1, ngP], [ob])
        yield

    def run_il(gens):
        gens = [g for g in gens if g is not None]
        while gens:
            for g in list(gens):
                try:
                    next(g)
                except StopIteration:
                    gens.remove(g)

    def load_wq(h):
        wq = qkvw[h % 2]
        DMA("sp", wq[:, :, :], wsc[h, :, :, 0:384], f"qw{h % 2}", [wsc_b[h]], [wq])
        return wq

    x_loaded = set()

    def load_x(tt):
        if tt in x_loaded or tt >= NT:
            return
        x_loaded.add(tt)
        for sub in range(4):
            r0 = tt * 512 + sub * 128
            DMA("sp", xt[sub][:, :], xw[r0:r0 + 128, :], f"xl{sub}", [], [xt[sub]])

    def tile_start(tt):
        tp = tt % 2
        betaP, gP, GP, s1P, gphl, bphl = betaPs[tp], gPs[tp], GPs[tp], s1Ps[tp], gphls[tp], bphls[tp]
        load_x(tt)
        for sub in range(4):
            (bk,) = yield from acquire(1)
            norm_to_T(xt[sub], gmixB, hT, sub, bk)
            release(bk)
            yield
        if tt < T_OWN0 - 1:
            load_x(tt + 1)
        MM([(ps_small[:, c * 16:(c + 1) * 16], hT[:, k, c * 128:(c + 1) * 128], wab[:, k, :], k == 0, k == 7)
            for c in range(4) for k in range(8)], [hT, wab], [ps_small])
        pa = ps_small[:, 0:64].rearrange("p (c x) -> p c x", c=4)
        vch = lambda t: t[:, :].rearrange("p (c h) -> p c h", c=4)
        vhc = lambda t: t[:, :].rearrange("p (h c) -> p c h", c=4)
        ACT(vhc(betaP), pa[:, :, 8:16], AF.Exp, [ps_small], [betaP], scale=-1.0)
        ACT(betaP[:, :], betaP[:, :], AF.Ln, [betaP], [betaP], bias=1.0)
        ACT(betaP[:, :], betaP[:, :], AF.Exp, [betaP], [betaP], scale=-1.0)
        TT("dve", vch(tmpP), pa[:, :, 0:8], vch(dtb), ALU.add, [ps_small, dtb], [tmpP])
        ACT(tmpP[:, :], tmpP[:, :], AF.Exp, [tmpP], [tmpP])
        ACT(tmpP[:, :], tmpP[:, :], AF.Ln, [tmpP], [tmpP], bias=1.0)
        TT("dve", vhc(gP), vch(tmpP), vch(negA), ALU.mult, [tmpP, negA], [gP])
        CP("pool", gphl[:, 0, :], gP[:, :], [gP], [gphl])
        TT("pool", tmpP[:, :], gP[:, :], gphl[:, 0, :], ALU.subtract, [gP, gphl], [tmpP])
        CP("pool", gphl[:, 1, :], tmpP[:, :], [tmpP], [gphl])
        CP("pool", bphl[:, 0, :], betaP[:, :], [betaP], [bphl])
        TT("pool", tmpP[:, :], betaP[:, :], bphl[:, 0, :], ALU.subtract, [betaP, bphl], [tmpP])
        CP("pool", bphl[:, 1, :], tmpP[:, :], [tmpP], [bphl])
        MM([(ps_small[:, 64:96], cstb2[:, 0, :], gphl[:, 0, :], True, False), (ps_small[:, 64:96], cstb2[:, 0, :], gphl[:, 1, :], False, True)], [cstb2, gphl], [ps_small])
        CP("act", GP[:, :], ps_small[:, 64:96], [ps_small], [GP])
        ACT(kegP[:, :], ps_small[:, 64:96], AF.Exp, [ps_small], [kegP])
        TT("dve", s1P[:, :], kegP[:, :], betaP[:, :], ALU.mult, [kegP, betaP], [s1P])
        yield

    def gdn_tiles(tts):
        tts = list(tts)
        gl = [(tt, h) for tt in tts for h in range(NH)]
        doneA = set()
        doneC = set()

        def chainA():
            for gi, (tt, h) in enumerate(gl):
                while gi >= NSETS and gl[gi - NSETS] not in doneC:
                    yield
                if h == 0:
                    while any((tt - 2, hh) in gl and (tt - 2, hh) not in doneC for hh in range(NH)):
                        yield
                    yield from tile_start(tt)
                yield from stageA(tt, h, load_wq(h), gi % NSETS)
                doneA.add((tt, h))

        def chainC(p, delay):
            for _ in range(delay):
                yield
            for gi, (tt, h) in enumerate(gl):
                if gi % NPIPE != p:
                    continue
                while (tt, h) not in doneA:
                    yield
                while gi >= NH and gl[gi - NH] not in doneC:
                    yield
                yield from stageC(tt, h, gi % NSETS, p)
                doneC.add((tt, h))

        run_il([chainA()] + [chainC(p, p * STAGGER) for p in range(NPIPE)])

    out_sem_total = [0]

    def phase2(tt, c0, c1, store):
        n = c1 - c0
        subs = list(range(c0 // 128, c1 // 128))
        seq = U_A + U_Z + U_G + U_O + U_UP + [uu for row in U_D for (uu, _) in row]
        wsm = WStream(seq)
        hs = hT[:, :, c0:c1]
        for ct in range(8):
            wu = wsm.get()
            psc = pslot()
            MM([(psc[:, 0:n], wu[:, k, 128:256], hT[:, k, c0:c1], k == 0, k == 7) for k in range(8)], [wu, hT], [psc])
            CP("act", cgs[:, 0:n], psc[:, 0:n], [psc], [cgs])
            psx = pslot()
            MM([(psx[:, 0:n], wu[:, k, 256:384], hT[:, k, c0:c1], k == 0, k == 7) for k in range(8)], [wu, hT], [psx])
            w_ = uw[ct % 2]
            TT("dve", w_[:, 2:2 + n], psx[:, 0:n], cgs[:, 0:n], ALU.mult, [psx, cgs], [w_])
            dwconv(w_, carA[:, ct, :], 3, lambda jj, ct=ct: cwA[:, ct, jj:jj + 1], n, cvg[:, 0:n], [cwA], [cvg], carA)
            psb = pslot()
            MM([(psb[:, 0:n], wu[:, k, 0:128], hT[:, k, c0:c1], k == 0, k == 7) for k in range(8)], [wu, hT], [psb])
            TT("dve", yap[:, ct, 0:n], psb[:, 0:n], cvg[:, 0:n], ALU.mult, [psb, cvg], [yap])
        for i2 in range(2):
            wu = wsm.get()
            for j in range(4):
                ct = i2 * 4 + j
                ps = pslot()
                MM([(ps[:, 0:n], wu[:, k, j * 128:(j + 1) * 128], hT[:, k, c0:c1], k == 0, k == 7) for k in range(8)], [wu, hT], [ps])
                ACT(sz[:, ct, 0:n], ps[:, 0:n], AF.Silu, [ps], [sz])
                TT("pool", sz[:, ct, 0:n], sz[:, ct, 0:n], ob[:, ct, c0:c1], ALU.mult, [sz, ob], [sz])
        for nt in range(8):
            wu = wsm.get()
            ps = pslot()
            MM([(ps[:, 0:n], wu[:, k, 0:128], hT[:, k, c0:c1], k == 0, k == 7) for k in range(8)], [wu, hT], [ps])
            ACT(sga[:, 0:n], ps[:, 0:n], AF.Sigmoid, [ps], [sga])
            ps = pslot()
            MM([(ps[:, 0:n], wu[:, k, 128:256], hT[:, k, c0:c1], k == 0, k == 7) for k in range(8)], [wu, hT], [ps])
            ACT(sgb[:, 0:n], ps[:, 0:n], AF.Sigmoid, [ps], [sgb])
            ps = pslot()
            MM([(ps[:, 0:n], wu[:, k, 256:384], yap[:, k, 0:n], k == 0, k == 7) for k in range(8)], [wu, yap], [ps])
            TT("dve", sga[:, 0:n], sga[:, 0:n], ps[:, 0:n], ALU.mult, [sga, ps], [sga])
            ps = pslot()
            MM([(ps[:, 0:n], wu[:, k, 384:512], sz[:, k, 0:n], k == 0, k == 7) for k in range(8)], [wu, sz], [ps])
            TT("dve", sgb[:, 0:n], sgb[:, 0:n], ps[:, 0:n], ALU.mult, [sgb, ps], [sgb])
            TT("pool", mixT[:, nt, 0:n], sga[:, 0:n], sgb[:, 0:n], ALU.add, [sga, sgb], [mixT])
        for nh in range(2):
            wu = wsm.get()
            for si, sub in enumerate(subs):
                ps = pslot()
                MM([(ps[:, :], mixT[:, k, si * 128:(si + 1) * 128], wu[:, k, :], k == 0, k == 7) for k in range(8)], [wu, mixT], [ps])
                TT("dve", xt[sub][:, nh * 512:(nh + 1) * 512], xt[sub][:, nh * 512:(nh + 1) * 512], ps[:, :], ALU.add, [xt[sub], ps], [xt[sub]])
        for si, sub in enumerate(subs):
            norm_to_T(xt[sub], gffnB, h2T, si)
        for fp in range(11):
            wu = wsm.get()
            for j in range(2):
                ft = fp * 2 + j
                psg = pslot()
                MM([(psg[:, 0:n], wu[:, k, j * 256:j * 256 + 128], h2T[:, k, 0:n], k == 0, k == 7) for k in range(8)], [wu, h2T], [psg])
                w0 = uw[0]
                CP("act", w0[:, 2:2 + n], psg[:, 0:n], [psg], [w0])
                dwconv(w0, carF[:, ft, :], 3, lambda jj, ft=ft: cwF[:, ft, jj:jj + 1], n, cvg[:, 0:n], [cwF], [cvg], carF)
                psv = pslot()
                MM([(psv[:, 0:n], wu[:, k, j * 256 + 128:j * 256 + 256], h2T[:, k, 0:n], k == 0, k == 7) for k in range(8)], [wu, h2T], [psv])
                w1 = uw[1]
                CP("act", w1[:, 2:2 + n], psv[:, 0:n], [psv], [w1])
                dwconv(w1, carF[:, 22 + ft, :], 3, lambda jj, ft=ft: cwF[:, 22 + ft, jj:jj + 1], n, cvv[:, 0:n], [cwF], [cvv], carF)
                ACT(cvg[:, 0:n], cvg[:, 0:n], AF.Silu, [cvg], [cvg])
                TT("dve", actT[:, ft, 0:n], cvg[:, 0:n], cvv[:, 0:n], ALU.mult, [cvg, cvv], [actT])
        for nh in range(2):
            wus = [(wsm.get(ahead=0), nk) for (_, nk) in U_D[nh]]
            if not store:
                continue
            for si, sub in enumerate(subs):
                ps = pslot()
                mms = []
                for kg, (wu, nk) in enumerate(wus):
                    for k in range(nk):
                        kk = kg * 8 + k
                        mms.append((ps[:, :], actT[:, kk, si * 128:(si + 1) * 128], wu[:, k, :], kk == 0, kk == 21))
                MM(mms, [actT] + [w for w, _ in wus], [ps])
                TT("dve", xt[sub][:, nh * 512:(nh + 1) * 512], xt[sub][:, nh * 512:(nh + 1) * 512], ps[:, :], ALU.add, [xt[sub], ps], [xt[sub]])
        if store:
            for si, sub in enumerate(subs):
                st = st_small[sidx[0] % 2]
                sidx[0] += 1
                yo = yout[0]
                ACT(junk[:, :], xt[sub][:, :], AF.Square, [xt[sub]], [junk, st], accum=st[:, 0:1])
                ACT(st[:, 1:2], st[:, 0:1], AF.Ln, [st], [st], bias=EPS, scale=1.0 / D)
                ACT(st[:, 2:3], st[:, 1:2], AF.Exp, [st], [st], scale=-0.5)
                STT("dve", yo[:, :], xt[sub][:, :], st[:, 2:3], gfinB[:, :], ALU.mult, ALU.mult, [xt[sub], st, gfinB], [yo])
                row0 = (tt - T_OWN0) * 512 + sub * 128
                DMA("sp", out_d[row0:row0 + 128, :], yo[:, :], "store0", [yo], [])

    def main_loop():
        ck(10)
        n_cont = max(T_OWN0 - 1, 0)
        if n_cont > 0:
            gdn_tiles(range(0, n_cont))
        for tt in range(n_cont, NT):
            gdn_tiles([tt])
            ck(70)
            if tt == T_OWN0 - 1:
                barrier()
                phase2(tt, 384, 512, False)
                barrier()
                ck(80)
            elif tt >= T_OWN0:
                barrier()
                phase2(tt, 0, 512, True)
                barrier()

    try:
        main_loop()
    except _Stop:
        pass

    fw = [(k_, v_) for k_, v_ in P.dsem.items() if k_.startswith("store")]
    if "dbg" in P.dsem:
        fw.append(("dbg", P.dsem["dbg"]))
    for k_ in P.dsem:
        if k_.startswith("xl") or k_.startswith("qw") or k_.startswith("ws") or k_ == "const":
            fw.append((k_, P.dsem[k_]))
    P.final_wait("sp", fw)
    P.final_wait("sp", [(k_, e_.count) for k_, e_ in P.E.items() if e_.count > 0])

    keys = list(P.E.keys()) + list(P.dsem.keys())
    sems = {k: es.enter_context(nc.semaphore("s_" + k)) for k in keys}
    with nc.Block() as block:
        def replay(name):
            def run(eng):
                for waits, fn, inc in P.E[name].ops:
                    for k, v in waits:
                        eng.wait_ge(sems[k], v)
                    if fn is None:
                        continue
                    if isinstance(fn, tuple):
                        ins = fn[0](eng)
                        ins.annotate(fn[1])
                    else:
                        ins = fn(eng)
                    ins.then_inc(sems[inc[0]], inc[1])
            return run
        block.tensor(replay("pe"))
        block.scalar(replay("act"))
        block.vector(replay("dve"))
        block.gpsimd(replay("pool"))
        block.sync(replay("sp"))
    es.close()
    return nc


def consts_np():
    i = np.arange(128)
    ident = np.eye(128, dtype=np.float32)
    uti = (i[:, None] <= i[None, :]).astype(np.float32)
    negsu = np.where(i[:, None] < i[None, :], 0.0, -30000.0).astype(np.float32)
    possl = np.where(i[:, None] > i[None, :], 0.0, 30000.0).astype(np.float32)
    ones = np.ones((128, 128), np.float32)
    return np.ascontiguousarray(np.stack([ident, uti, negsu, possl, ones], axis=1))


def make_in_maps(inputs, W, OWN, core_tokens):
    f = lambda a: np.ascontiguousarray(np.asarray(a, dtype=np.float32))
    x = f(inputs["x"])
    bc = lambda v: np.ascontiguousarray(np.broadcast_to(f(v).reshape(1, -1), (128, f(v).size)))
    cw = lambda w, nt: np.ascontiguousarray(f(w).reshape(w.shape[-2], nt, 128).transpose(2, 1, 0))
    shared = {
        "w_in": f(inputs["w_in"][0]), "w_a": f(inputs["w_a_out"][0]), "w_b": f(inputs["w_b_out"][0]),
        "w_o": f(inputs["w_o"][0]), "w_up": f(inputs["w_up"][0]), "w_dn": f(inputs["w_down"][0]),
        "gmixB": bc(inputs["norm_mix_g"][0]), "gffnB": bc(inputs["norm_ffn_g"][0]), "gfinB": bc(inputs["norm_final_g"]),
        "cwA": cw(np.asarray(inputs["conv_a_w"][0]), 8), "cwG": cw(np.asarray(inputs["gdn_conv_w"][0]), 24),
        "cwF": cw(np.asarray(inputs["ffn_conv_w"][0]), 44),
        "alogB": np.ascontiguousarray(np.tile(bc(inputs["gdn_A_log"][0]), (1, 4))),
        "dtbB": np.ascontiguousarray(np.tile(bc(inputs["gdn_dt_bias"][0]), (1, 4))),
        "ngP": f(inputs["gdn_norm_g"][0]).reshape(128, 1),
        "cst": consts_np(),
    }
    maps = []
    for (b, s) in core_tokens:
        end = s + OWN
        xwin = np.zeros((W, D), np.float32)
        lo = max(0, end - W)
        xwin[W - (end - lo):] = x[b, lo:end]
        m = dict(shared)
        m["xw"] = xwin
        maps.append(m)
    return maps


def kernel(**inputs):
    x = np.asarray(inputs["x"])
    Bsz, S, _ = x.shape
    core_tokens = [(c // 4, (c % 4) * OWN_FULL) for c in range(NCORES)]
    nc = build(SEQ, OWN_FULL)
    maps = make_in_maps(inputs, SEQ, OWN_FULL, core_tokens)
    res = run_bass_kernel_spmd(nc, maps, core_ids=list(range(NCORES)))
    out = np.zeros((Bsz, S, D), np.float32)
    for c, (b, s) in enumerate(core_tokens):
        out[b, s:s + OWN_FULL] = np.asarray(res.results[c]["out"])
    return out
```

```python
import numpy as np
from contextlib import ExitStack
import concourse.bass as bass
import concourse.mybir as mybir
from concourse.bass_utils import run_bass_kernel_spmd

F32 = mybir.dt.float32
BF16 = mybir.dt.bfloat16
AF = mybir.ActivationFunctionType
ALU = mybir.AluOpType

D = 1024
NH = 8
DFF = 2816
EPS = 1e-6
NCORES = 8
SEQ = 8192
OWN_FULL = 2048
SAME_SYNC = True
STAGGER = 11
DEBUG_TAGS = False

C_BG, C_CG, C_XV, C_Q, C_K, C_V, C_Z, C_A, C_B, C_GA, C_GB = 0, 1024, 2048, 3072, 4096, 5120, 6144, 7168, 7176, 7184, 8208


class Buf:
    __slots__ = ("name", "w", "rs", "psum")

    def __init__(self, name, psum=False):
        self.name = name
        self.w = None
        self.rs = []
        self.psum = psum


class Eng:
    def __init__(self, name):
        self.name = name
        self.ops = []
        self.count = 0
        self.known = {}


class Prog:
    def __init__(self):
        self.E = {n: Eng(n) for n in ("pe", "act", "dve", "pool", "sp")}
        self.dsem = {}

    def _deps(self, e, reads, writes):
        deps = {}

        def add(dep):
            if dep is None:
                return
            k, v = dep
            if deps.get(k, 0) < v:
                deps[k] = v

        for b in reads:
            add(b.w)
            if b.psum:
                for r in b.rs:
                    if r[0] != e.name:
                        add(r)
        for b in writes:
            add(b.w)
            for r in b.rs:
                add(r)
        waits = []
        for k, v in deps.items():
            if k == e.name and (k == "pe" or not SAME_SYNC):
                continue
            if e.known.get(k, 0) >= v:
                continue
            e.known[k] = v
            waits.append((k, v))
        return waits

    def op(self, eng, fn, reads=(), writes=()):
        e = self.E[eng]
        waits = self._deps(e, reads, writes)
        e.count += 1
        n = e.count
        if DEBUG_TAGS:
            import sys
            f = sys._getframe(1)
            tag = []
            for _ in range(4):
                if f is None:
                    break
                tag.append(str(f.f_lineno))
                f = f.f_back
            fn = (fn, "L" + "<".join(tag))
        e.ops.append((waits, fn, (eng, 1)))
        for b in writes:
            b.w = (eng, n)
            b.rs = []
        for b in reads:
            b.rs.append((eng, n))

    def dma(self, queue, fn, key, reads=(), writes=()):
        e = self.E[queue]
        waits = self._deps(e, reads, writes)
        self.dsem[key] = self.dsem.get(key, 0) + 16
        n = self.dsem[key]
        e.ops.append((waits, fn, (key, 16)))
        for b in writes:
            b.w = (key, n)
            b.rs = []
        for b in reads:
            b.rs.append((key, n))

    def final_wait(self, eng, deps):
        self.E[eng].ops.append((list(deps), None, None))


class _Stop(Exception):
    pass


def build(W=SEQ, OWN=OWN_FULL, dbg=None, stage=None, flags=()):
    def ck(n):
        if stage is not None and stage == n:
            raise _Stop()

    NT = W // 512
    T_OWN0 = (W - OWN) // 512
    NOWN = OWN // 512
    nc = bass.Bass("TRN2", target_bir_lowering=False)
    P = Prog()
    dbg = dbg or []
    dbg_out = {}

    def din(name, shape, dt=F32):
        return nc.dram_tensor(name, shape, dt, kind="ExternalInput").ap()

    xw = din("xw", [W, D])
    w_in = din("w_in", [D, 9232])
    w_a = din("w_a", [D, D])
    w_b = din("w_b", [D, D])
    w_o = din("w_o", [D, D])
    w_up = din("w_up", [D, 2 * DFF])
    w_dn = din("w_dn", [DFF, D])
    gmixB_d = din("gmixB", [128, D])
    gffnB_d = din("gffnB", [128, D])
    gfinB_d = din("gfinB", [128, D])
    cwA_d = din("cwA", [128, 8, 3])
    cwG_d = din("cwG", [128, 24, 4])
    cwF_d = din("cwF", [128, 44, 3])
    alog_d = din("alogB", [128, 32])
    dtb_d = din("dtbB", [128, 32])
    ng_d = din("ngP", [128, 1])
    cst_d = din("cst", [128, 5, 128])
    out_d = nc.dram_tensor("out", [OWN, D], F32, kind="ExternalOutput").ap()
    NUNIT2 = 37
    wsc = nc.dram_tensor("wsc", [8 + NUNIT2, 128, 8, 512], BF16, kind="Internal").ap()
    for name, shape in dbg:
        dbg_out[name] = nc.dram_tensor("dbg_" + name, shape, F32, kind="ExternalOutput").ap()

    es = ExitStack()
    SB_ACC = [0, []]
    build.sb_acc = SB_ACC

    class T:
        def __init__(self, name, shape, dt=F32):
            self.t = es.enter_context(nc.sbuf_tensor("sb_" + name, shape, dt))
            self.b = Buf(name)
            SB_ACC[0] += int(np.prod(shape[1:])) * (2 if dt == BF16 else 4)
            SB_ACC[1].append((name, int(np.prod(shape[1:])) * (2 if dt == BF16 else 4)))

        def __getitem__(self, k):
            return self.t[k]

    class PS:
        def __init__(self, name, shape, dt=F32):
            self.t = es.enter_context(nc.psum_tensor(name, shape, dt))
            self.b = Buf(name, psum=True)

        def __getitem__(self, k):
            return self.t[k]

    class View:
        def __init__(self, ap, name):
            self.ap = ap
            self.b = Buf(name)

        def __getitem__(self, k):
            return self.ap[k]

    arena = T("arena", [128, 11264], BF16)
    arena2 = T("arena2", [128, 12288], BF16)
    arenas = {1: (arena, 11264, [0]), 2: (arena2, 12288, [0])}

    def AV(name, shape, dt=F32, which=1):
        ar, cap, aoff = arenas[which]
        n = int(np.prod(shape[1:]))
        nb = n * (1 if dt == BF16 else 2)
        ap = ar.t[:, aoff[0]:aoff[0] + nb]
        aoff[0] += nb
        assert aoff[0] <= cap
        if dt != BF16:
            ap = ap.bitcast(F32)
        if len(shape) == 3:
            ap = ap.rearrange("p (a b) -> p a b", a=shape[1])
        return View(ap, name)

    def barrier():
        for en, e in P.E.items():
            waits = [(k_, e2.count) for k_, e2 in P.E.items() if k_ != en and e2.count > 0]
            waits += [(k_, v_) for k_, v_ in P.dsem.items() if not k_.startswith("wc")]
            e.ops.append((waits, None, None))
            for k_, v_ in waits:
                e.known[k_] = max(e.known.get(k_, 0), v_)

    def B(xs):
        return [x.b if hasattr(x, "b") else x for x in xs]

    def ACT(out, in_, func, r, w, bias=0.0, scale=1.0, accum=None):
        if accum is None:
            P.op("act", lambda e: e.activation(out=out, in_=in_, func=func, bias=bias, scale=scale), B(r), B(w))
        else:
            P.op("act", lambda e: e.activation(out=out, in_=in_, func=func, bias=bias, scale=scale, accum_out=accum), B(r), B(w))

    def TS(eng, out, in0, s1, s2, op0, op1, r, w):
        if op1 is None:
            P.op(eng, lambda e: e.tensor_scalar(out=out, in0=in0, scalar1=s1, scalar2=None, op0=op0), B(r), B(w))
        else:
            P.op(eng, lambda e: e.tensor_scalar(out=out, in0=in0, scalar1=s1, scalar2=s2, op0=op0, op1=op1), B(r), B(w))

    def STT(eng, out, in0, scalar, in1, op0, op1, r, w):
        P.op(eng, lambda e: e.scalar_tensor_tensor(out=out, in0=in0, scalar=scalar, in1=in1, op0=op0, op1=op1), B(r), B(w))

    def TT(eng, out, in0, in1, op, r, w):
        P.op(eng, lambda e: e.tensor_tensor(out=out, in0=in0, in1=in1, op=op), B(r), B(w))

    def CP(eng, out, in_, r, w):
        if eng == "act":
            P.op("act", lambda e: e.copy(out=out, in_=in_), B(r), B(w))
        else:
            P.op(eng, lambda e: e.tensor_copy(out=out, in_=in_), B(r), B(w))

    def MM(mms, r, w):
        def fn(e):
            ins = None
            for (o, l, rh, st, sp) in mms:
                ins = e.matmul(o, l, rh, start=st, stop=sp)
            return ins
        P.op("pe", fn, B(r), B(w))

    def TR(trs, r, w):
        def fn(e):
            ins = None
            for (o, i, idn) in trs:
                ins = e.transpose(o, i, idn)
            return ins
        P.op("pe", fn, B(r), B(w))

    def DMA(queue, out, in_, key, r, w):
        q = {"sp": "sp", "pool": "pool", "act": "act"}[queue]
        P.dma(q, lambda e: e.dma_start(out=out, in_=in_), key, B(r), B(w))

    cst = T("cst", [128, 5, 128])
    cstb = T("cstb", [128, 128], BF16)
    cstb2 = T("cstb2", [128, 2, 128], BF16)
    gphls = [T(f"gphl{i}", [128, 2, 32], BF16) for i in range(2)]
    gmixB = T("gmixB", [128, D])
    gffnB = T("gffnB", [128, D])
    gfinB = T("gfinB", [128, D])
    cwA = T("cwA", [128, 8, 3])
    cwG = T("cwG", [128, 24, 4])
    cwF = T("cwF", [128, 44, 3])
    negA = T("negA", [128, 32])
    dtb = T("dtb", [128, 32])
    ngP = T("ngP", [128, 1])
    wab = T("wab", [128, 8, 16], BF16)
    IDF = cst[:, 0, :]
    UTI = cst[:, 1, :]
    NEGSU = cst[:, 2, :]
    POSSL = cst[:, 3, :]
    ONES = cst[:, 4, :]
    IDB = cstb[:, :]

    xt = [T(f"xt{i}", [128, D]) for i in range(4)]
    xn = T("xn", [128, D], BF16)
    junk = T("junk", [128, D], BF16)
    st_small = [T(f"st{i}", [128, 4]) for i in range(2)]
    hT = T("hT", [128, 8, 512], BF16)
    h2T = hT
    wslot = [T(f"wslot{i}", [128, 8, 512], BF16) for i in range(3)]
    qkvw = [T(f"qkvw{i}", [128, 8, 384], BF16) for i in range(2)]
    pre = [T(f"pre{i}", [128, 515]) for i in range(2)]
    cacc = T("cacc", [128, 512])
    post = T("post", [128, 512])
    sqb = T("sqb", [128, 512])
    rr = T("rr", [128, 512])
    NSETS = 5
    qTs = [T(f"qT{i}", [128, 512], BF16) for i in range(NSETS)]
    kTs = [T(f"kT{i}", [128, 512], BF16) for i in range(NSETS)]
    vTs = [T(f"vT{i}", [128, 512], BF16) for i in range(NSETS)]
    oTs = [T(f"oT{i}", [128, 512]) for i in range(2)] + [AV("oT2", [128, 512], F32, 2)]
    oT = oTs[0]
    carG = T("carG", [128, 24, 3])
    carA = T("carA", [128, 8, 2])
    carF = T("carF", [128, 44, 2])
    betaPs = [T(f"betaP{i}", [128, 32]) for i in range(2)]
    gPs = [T(f"gP{i}", [128, 32]) for i in range(2)]
    GPs = [T(f"GP{i}", [128, 32]) for i in range(2)]
    kegP = T("kegP", [128, 32])
    s1Ps = [T(f"s1P{i}", [128, 32]) for i in range(2)]
    tmpP = T("tmpP", [128, 32])
    NPIPE = 3
    def PT(p, name, shape, dt=F32):
        return T(name, shape, dt) if p == 0 else AV(name, shape, dt, p)

    Fs = [[PT(p, f"F{p}_{j}", [128, 512]) for j in range(3)] for p in range(NPIPE)]
    gm0_ = T("gm0", [128, 4, 4, 128], BF16)
    gm = [gm0_ for p in range(NPIPE)]
    Vb = [[PT(p, f"V{p}_{j}", [128, 512], BF16) for j in range(2)] for p in range(NPIPE)]
    VTb = [[PT(p, f"VT{p}_{j}", [128, 512], BF16) for j in range(2)] for p in range(NPIPE)]
    Rb = [[PT(p, f"R{p}_{j}", [128, 512], BF16) for j in range(2)] for p in range(NPIPE)]
    ATb = [PT(p, f"AT{p}", [128, 512], BF16) for p in range(NPIPE)]
    QATb = [PT(p, f"QAT{p}", [128, 512], BF16) for p in range(NPIPE)]
    vbb = [PT(p, f"vb{p}", [128, 4, 128], BF16) for p in range(NPIPE)]
    kbgb = [PT(p, f"kbg{p}", [128, 4, 128], BF16) for p in range(NPIPE)]
    kdecb = [PT(p, f"kdec{p}", [128, 4, 128], BF16) for p in range(NPIPE)]
    ub = [PT(p, f"u{p}", [128, 512], BF16) for p in range(NPIPE)]
    wb = [PT(p, f"w{p}", [128, 512], BF16) for p in range(NPIPE)]
    nKWTb = [PT(p, f"nKWT{p}", [128, 512], BF16) for p in range(NPIPE)]
    kbTs = [PT(p, f"kbT{p}", [128, 512], BF16) for p in range(NPIPE)]
    sm4 = [T(f"sm4_{p}", [128, 16]) for p in range(NPIPE)]
    bphls = [T(f"bphl{i}", [128, 2, 32], BF16) for i in range(2)]
    Sf = [T(f"Sf{h}", [128, 128]) for h in range(NH)]
    Sb = [[T(f"Sb{h}_{j}", [128, 128], BF16) for j in range(2)] for h in range(NH)]
    sb_par = [0] * NH
    ob = T("ob", [128, 8, 512], BF16)
    yap = View(arena2.t[:, 0:4096].rearrange("p (f n) -> p f n", f=8), "yap")
    sz = View(arena2.t[:, 4096:8192].rearrange("p (f n) -> p f n", f=8), "sz")
    mixT = View(arena2.t[:, 8192:12288].rearrange("p (f n) -> p f n", f=8), "mixT")
    sga = rr
    sgb = oT
    cgs = cacc
    uw = pre
    actT = View(arena.t[:, :].rearrange("p (f n) -> p f n", f=22), "actT")
    cvg = post
    cvv = sqb
    yout = [T(f"yout{i}", [128, D]) for i in range(1)]

    class Slot:
        def __init__(self, ap, b):
            self.ap = ap
            self.b = b

        def __getitem__(self, k):
            return self.ap[k]

    ps_proj = [PS(f"ps_proj{i}", [128, 512]) for i in range(2)]
    ps_small = ps_proj[0]
    ps_gb = [Slot(ps_small[:, 128:384], ps_small.b) for i in range(2)]
    ps_mm_t = [PS(f"ps_mm{i}", [128, 512]) for i in range(6)]


    mm_slots = [Slot(ps_mm_t[i][:, :], ps_mm_t[i].b) for i in range(6)]
    bank_free = list(mm_slots)
    mm_i = [0]

    def mslot():
        s = mm_slots[mm_i[0] % 3]
        mm_i[0] += 1
        return s

    proj_i = [0]

    def pslot():
        s = ps_proj[proj_i[0] % 2]
        proj_i[0] += 1
        return s

    cl = [(cst, cst_d), (gmixB, gmixB_d), (gffnB, gffnB_d), (gfinB, gfinB_d), (cwA, cwA_d), (cwG, cwG_d),
          (cwF, cwF_d), (negA, alog_d), (dtb, dtb_d), (ngP, ng_d)]
    for t, d_ in cl:
        DMA("sp", t[:], d_, "const", [], [t])
    for t, d_ in cl:
        t.b.w = ("const", P.dsem["const"])
    CP("dve", cstb[:, :], cst[:, 0, :], [cst], [cstb])
    CP("dve", cstb2[:, 0, :], cst[:, 1, :], [cst], [cstb2])
    CP("dve", cstb2[:, 1, :], cst[:, 4, :], [cst], [cstb2])
    ACT(negA[:, :], negA[:, :], AF.Exp, [negA], [negA])
    TS("dve", negA[:, :], negA[:, :], -1.0, None, ALU.mult, None, [negA], [negA])
    for t in (carG, carA, carF):
        P.op("pool", lambda e, t=t: e.memset(t[:], 0.0), [], [t.b])
    for h in range(NH):
        P.op("pool", lambda e, h=h: e.memset(Sf[h][:, :], 0.0), [], [Sf[h].b])
        P.op("pool", lambda e, h=h: e.memset(Sb[h][0][:, :], 0.0), [], [Sb[h][0].b])

    wsc_b = [Buf(f"wsc{u}") for u in range(8 + NUNIT2)]
    stopped = [False]

    uext = {}

    def cast_piece(u, col0, src, r0, nk, c0, width, key="wcast"):
        uext[u] = (nk, max(uext.get(u, (0, 0))[1], col0 + width))
        s = src[r0:r0 + nk * 128, c0:c0 + width].rearrange("(k p) n -> p k n", p=128)
        cast_dma(wsc[u, :, 0:nk, col0:col0 + width], s)

    cast_n = [0]
    dummy = T("dummy", [128, 4])

    def cast_dma(out, in_):
        if "nocast" in flags:
            return
        key = f"wc{cast_n[0] % 6}"
        cast_n[0] += 1
        if P.dsem.get(key, 0) > 0:
            P.E["pool"].ops.append(([(key, P.dsem[key])], None, None))
        DMA("pool", out, in_, key, [], [])

    def cast_barrier(bufs):
        fake = []
        for j in range(6):
            key = f"wc{j}"
            if P.dsem.get(key, 0) > 0:
                fb = Buf("fk")
                fb.w = (key, P.dsem[key])
                fake.append(fb)
        P.op("pool", lambda e: e.memset(dummy[:, :], 0.0), fake, [dummy.b])
        for b_ in bufs:
            b_.w = dummy.b.w

    cast_dma(wab[:, :, :], w_in[:, C_A:C_A + 16].rearrange("(k p) n -> p k n", p=128))
    for h in range(NH):
        for j, cb in enumerate((C_Q, C_K, C_V)):
            cast_piece(h, j * 128, w_in, 0, 8, cb + h * 128, 128, key="wcast0")
    cast_barrier([wab.b] + [wsc_b[h] for h in range(NH)])
    units = []
    u = 8
    U_A = []
    for ct in range(8):
        for j, cb in enumerate((C_BG, C_CG, C_XV)):
            cast_piece(u, j * 128, w_in, 0, 8, cb + ct * 128, 128)
        U_A.append(u); u += 1
    U_Z = []
    for i in range(2):
        cast_piece(u, 0, w_in, 0, 8, C_Z + i * 512, 512)
        U_Z.append(u); u += 1
    U_G = []
    for nt in range(8):
        cast_piece(u, 0, w_in, 0, 8, C_GA + nt * 128, 128)
        cast_piece(u, 128, w_in, 0, 8, C_GB + nt * 128, 128)
        cast_piece(u, 256, w_a, 0, 8, nt * 128, 128)
        cast_piece(u, 384, w_b, 0, 8, nt * 128, 128)
        U_G.append(u); u += 1
    U_O = []
    for i in range(2):
        cast_piece(u, 0, w_o, 0, 8, i * 512, 512)
        U_O.append(u); u += 1
    U_UP = []
    for fp in range(11):
        for j in range(2):
            ft = fp * 2 + j
            cast_piece(u, j * 256, w_up, 0, 8, ft * 128, 128)
            cast_piece(u, j * 256 + 128, w_up, 0, 8, DFF + ft * 128, 128)
        U_UP.append(u); u += 1
    U_D = []
    for nh in range(2):
        row = []
        for kg, (k0, nk) in enumerate(((0, 8), (8, 8), (16, 6))):
            cast_piece(u, 0, w_dn, k0 * 128, nk, nh * 512, 512)
            row.append((u, nk)); u += 1
        U_D.append(row)
    assert u == 8 + NUNIT2
    cast_barrier([wsc_b[uu] for uu in range(8, 8 + NUNIT2)])

    ws_state = {"q": [], "n": 0}

    def ws_issue(unit):
        i = ws_state["n"] % 3
        ws_state["n"] += 1
        nk, ncol = uext[unit]
        DMA("sp", wslot[i][:, 0:nk, 0:ncol], wsc[unit, :, 0:nk, 0:ncol], f"ws{i}", [wsc_b[unit]], [wslot[i]])
        return wslot[i]

    class WStream:
        def __init__(self, seq, ahead=2):
            self.seq = list(seq)
            self.loaded = []
            self.pos = 0
            self.ahead = ahead

        def get(self, ahead=None):
            ahead = self.ahead if ahead is None else ahead
            while len(self.loaded) < min(len(self.seq), self.pos + 1 + ahead):
                self.loaded.append(ws_issue(self.seq[len(self.loaded)]))
            s = self.loaded[self.pos]
            self.pos += 1
            return s

    sidx = [0]

    def norm_to_T(src_tile, gB, dstT, sub, bank=None):
        bank = bank if bank is not None else bank_free[0]
        ps_tr = Slot(bank[:, 0:512].bitcast(BF16), bank.b)
        st = st_small[sidx[0] % 2]
        sidx[0] += 1
        ACT(junk[:, :], src_tile[:, :], AF.Square, [src_tile], [junk, st], accum=st[:, 0:1])
        ACT(st[:, 1:2], st[:, 0:1], AF.Ln, [st], [st], bias=EPS, scale=1.0 / D)
        ACT(st[:, 2:3], st[:, 1:2], AF.Exp, [st], [st], scale=-0.5)
        STT("dve", xn[:, :], src_tile[:, :], st[:, 2:3], gB[:, :], ALU.mult, ALU.mult, [src_tile, st, gB], [xn])
        TR([(ps_tr[:, k * 128:(k + 1) * 128], xn[:, k * 128:(k + 1) * 128], IDB) for k in range(8)], [xn, cstb], [ps_tr])
        CP("act", dstT[:, :, sub * 128:(sub + 1) * 128], ps_tr[:, :].rearrange("p (k n) -> p k n", k=8), [ps_tr], [dstT])
        return st

    def dump(name, ap, r):
        if name in dbg_out:
            DMA("sp", dbg_out[name], ap, "dbg", r, [])

    def dwconv(work, carry_ap, K, wts, n, outp, r_w, w_out, car_t):
        CP("pool", work[:, 0:K - 1], carry_ap, [car_t], [work])
        P.op("act", lambda e: e.mul(out=outp, in_=work[:, 0:n], mul=wts(0)), B([work] + r_w), B(w_out))
        for j in range(1, K):
            STT("dve", outp, work[:, j:j + n], wts(j), outp, ALU.mult, ALU.add, [work] + r_w + w_out, w_out)
        CP("pool", carry_ap, work[:, n:n + K - 1], [work], [car_t])

    cb_i = [0]
    bc_i = lambda ap4: ap4.unsqueeze(2).broadcast_to([128, 4, 128])
    bc_c = lambda ap: ap.unsqueeze(1).broadcast_to([128, 4, 128])
    v4 = lambda ap: ap.rearrange("p (c i) -> p c i", c=4)
    UTIB = cstb2[:, 0, :]
    ONESB = cstb2[:, 1, :]

    def acquire(n):
        while len(bank_free) < n:
            yield
        return [bank_free.pop(0) for _ in range(n)]

    def release(*bs):
        for b_ in bs:
            bank_free.append(b_)

    def stageA(tt, h, wq, qs):
        qT, kT, vT = qTs[qs], kTs[qs], vTs[qs]
        cc = [cacc, post, sqb]
        for j in range(3):
            ps = pslot()
            MM([(ps[:, :], wq[:, k, j * 128:(j + 1) * 128], hT[:, k, :], k == 0, k == 7) for k in range(8)], [wq, hT], [ps])
            pw = pre[(cb_i[0]) % 2]
            cb_i[0] += 1
            CP("act", pw[:, 3:515], ps[:, :], [ps], [pw])
            ch = j * 8 + h
            dwconv(pw, carG[:, ch, :], 4, lambda jj, ch=ch: cwG[:, ch, jj:jj + 1], 512, cc[j][:, :], [cwG], [cc[j]], carG)
            yield
        ACT(cc[0][:, :], cc[0][:, :], AF.Silu, [cc[0]], [cc[0]])
        ACT(cc[1][:, :], cc[1][:, :], AF.Silu, [cc[1]], [cc[1]])
        ACT(vT[:, :], cc[2][:, :], AF.Silu, [cc[2]], [vT])
        yield
        for src, dst, scale in ((cc[0], qT, 128 ** -0.5), (cc[1], kT, 1.0)):
            ACT(sqb[:, :], src[:, :], AF.Square, [src], [sqb])
            ps2 = pslot()
            CP("pool", junk[:, 0:512], sqb[:, :], [sqb], [junk])
            MM([(ps2[:, :], ONESB, junk[:, 0:512], True, True)], [junk, cstb2], [ps2])
            ACT(rr[:, :], ps2[:, :], AF.Ln, [ps2], [rr], bias=EPS)
            ACT(rr[:, :], rr[:, :], AF.Exp, [rr], [rr], scale=-0.5)
            STT("dve", dst[:, :], src[:, :], scale, rr[:, :], ALU.mult, ALU.mult, [src, rr], [dst])
            yield

    def stageC(tt, h, qs, p=0):
        qT, kT, vT = qTs[qs], kTs[qs], vTs[qs]
        sl = slice(h * 4, h * 4 + 4)
        F0, F1, F2 = Fs[p]
        G, V, VT, R = gm[p], Vb[p], VTb[p], Rb[p]
        AT, QAT, vb, kbg, kdec, u_, w_, nK, sm = ATb[p], QATb[p], vbb[p], kbgb[p], kdecb[p], ub[p], wb[p], nKWTb[p], sm4[p]
        kbT, oT = kbTs[p], oTs[p]
        tp = tt % 2
        betaP, GP, s1P, gphl, bphl = betaPs[tp], GPs[tp], s1Ps[tp], gphls[tp], bphls[tp]
        f2 = lambda t3: t3.rearrange("p c i -> p (c i)")
        cs = lambda c: slice(c * 128, (c + 1) * 128)
        bG, bB = yield from acquire(2)
        TT("pool", G[:, 0], bc_c(UTIB), bc_i(gphl[:, 0, sl]), ALU.mult, [cstb2, gphl], [G])
        TT("pool", G[:, 1], bc_c(UTIB), bc_i(gphl[:, 1, sl]), ALU.mult, [cstb2, gphl], [G])
        TT("pool", G[:, 2], bc_c(IDB), bc_i(bphl[:, 0, sl]), ALU.mult, [cstb, bphl], [G])
        TT("pool", G[:, 3], bc_c(IDB), bc_i(bphl[:, 1, sl]), ALU.mult, [cstb, bphl], [G])
        MM([(bG[:, 0:512], ONESB, f2(G[:, 0]), True, False), (bG[:, 0:512], ONESB, f2(G[:, 1]), False, True)], [G, cstb2], [bG])
        MM([(bB[:, 0:512], ONESB, f2(G[:, 2]), True, False), (bB[:, 0:512], ONESB, f2(G[:, 3]), False, True)], [G, cstb2], [bB])
        yield
        TT("dve", v4(F0[:, :]), v4(bG[:, 0:512]), bc_i(GP[:, sl]), ALU.subtract, [bG, GP], [F0])
        ACT(F2[:, :], bG[:, 0:512], AF.Exp, [bG], [F2])
        CP("act", sm[:, 0:4], v4(bG[:, 0:512])[:, :, 127], [bG], [sm])
        TT("dve", kbT[:, :], kT[:, :], bB[:, 0:512], ALU.mult, [kT, bB], [kbT])
        release(bG, bB)
        yield
        TT("dve", v4(F1[:, :]), v4(F0[:, :]), bc_c(POSSL), ALU.add, [F0, cst], [F1])
        TT("pool", v4(F0[:, :]), v4(F0[:, :]), bc_c(NEGSU), ALU.add, [F0, cst], [F0])
        TT("dve", sm[:, 4:8], sm[:, 0:4], GP[:, sl], ALU.subtract, [sm, GP], [sm])
        yield
        ACT(F1[:, :], F1[:, :], AF.Exp, [F1], [F1], scale=-1.0)
        ACT(F0[:, :], F0[:, :], AF.Exp, [F0], [F0])
        ACT(sm[:, 8:12], sm[:, 4:8], AF.Exp, [sm], [sm])
        ACT(sm[:, 12:16], sm[:, 0:4], AF.Exp, [sm], [sm])
        TT("pool", F2[:, :], qT[:, :], F2[:, :], ALU.mult, [qT, F2], [F2])
        bL, bU, bA = yield from acquire(3)
        MM([(bL[:, cs(c)], kbT[:, cs(c)], kT[:, cs(c)], True, True) for c in range(4)], [kbT, kT], [bL])
        MM([(bU[:, cs(c)], kT[:, cs(c)], kbT[:, cs(c)], True, True) for c in range(4)], [kbT, kT], [bU])
        MM([(bA[:, cs(c)], kT[:, cs(c)], qT[:, cs(c)], True, True) for c in range(4)], [qT, kT], [bA])
        yield
        STT("dve", VT[0][:, :], bL[:, 0:512], -1.0, F1[:, :], ALU.mult, ALU.mult, [bL, F1], [VT[0]])
        STT("dve", V[0][:, :], bU[:, 0:512], -1.0, F0[:, :], ALU.mult, ALU.mult, [bU, F0], [V[0]])
        TT("pool", v4(F0[:, :]), v4(F0[:, :]), bc_c(IDF), ALU.add, [F0, cst], [F0])
        TT("dve", AT[:, :], bA[:, 0:512], F0[:, :], ALU.mult, [bA, F0], [AT])
        TT("pool", v4(R[0][:, :]), v4(V[0][:, :]), bc_c(IDB), ALU.add, [V[0], cstb], [R[0]])
        release(bL, bU, bA)
        (bX,) = yield from acquire(1)
        bXb = bX[:, 0:512].bitcast(BF16)
        TR([(bXb[:, cs(c)], kT[:, cs(c)], IDB) for c in range(4)] +
           [(bXb[:, 512 + c * 128:512 + (c + 1) * 128], vT[:, cs(c)], IDB) for c in range(4)], [kT, vT, cstb], [bX])
        bT, bV = yield from acquire(2)
        MM([(bT[:, cs(c)], V[0][:, cs(c)], VT[0][:, cs(c)], True, True) for c in range(4)], [V[0], VT[0]], [bT])
        MM([(bV[:, cs(c)], VT[0][:, cs(c)], V[0][:, cs(c)], True, True) for c in range(4)], [V[0], VT[0]], [bV])
        yield
        TT("dve", kbg[:, :, :], v4(bXb[:, 0:512]), bc_i(s1P[:, sl]), ALU.mult, [bX, s1P], [kbg])
        TT("dve", kdec[:, :, :], v4(bXb[:, 0:512]), bc_i(sm[:, 8:12]), ALU.mult, [bX, sm], [kdec])
        TT("dve", vb[:, :, :], v4(bXb[:, 512:1024]), bc_i(betaP[:, sl]), ALU.mult, [bX, betaP], [vb])
        release(bX)
        cur = 0
        for m in range(1, 7):
            nxt = 1 - cur
            CP("act", VT[nxt][:, :], bT[:, 0:512], [bT], [VT[nxt]])
            release(bT)
            if m < 6:
                CP("act", V[nxt][:, :], bV[:, 0:512], [bV], [V[nxt]])
                release(bV)
            if m > 1:
                CP("act", R[cur][:, :], bR[:, 0:512], [bR], [R[cur]])
                release(bR)
            yield
            (bR,) = yield from acquire(1)
            mmr = []
            for c in range(4):
                mmr.append((bR[:, cs(c)], IDB, R[cur][:, cs(c)], True, False))
                mmr.append((bR[:, cs(c)], VT[nxt][:, cs(c)], R[cur][:, cs(c)], False, True))
            MM(mmr, [cstb, R[cur], VT[nxt]], [bR])
            if m < 5:
                bT, bV = yield from acquire(2)
            elif m == 5:
                (bT,) = yield from acquire(1)
            if m < 6:
                MM([(bT[:, cs(c)], V[nxt][:, cs(c)], VT[nxt][:, cs(c)], True, True) for c in range(4)], [V[nxt], VT[nxt]], [bT])
            if m < 5:
                MM([(bV[:, cs(c)], VT[nxt][:, cs(c)], V[nxt][:, cs(c)], True, True) for c in range(4)], [V[nxt], VT[nxt]], [bV])
            cur = nxt
            yield
        CP("act", R[cur][:, :], bR[:, 0:512], [bR], [R[cur]])
        release(bR)
        Rf = R[cur]
        yield
        b0, b1 = yield from acquire(2)
        MM([(b0[:, cs(c)], Rf[:, cs(c)], vb[:, c, :], True, True) for c in range(4)], [Rf, vb], [b0])
        MM([(b1[:, cs(c)], Rf[:, cs(c)], kbg[:, c, :], True, True) for c in range(4)], [Rf, kbg], [b1])
        yield
        CP("act", u_[:, :], b0[:, 0:512], [b0], [u_])
        CP("act", w_[:, :], b1[:, 0:512], [b1], [w_])
        release(b0, b1)
        yield
        bK, bQ = yield from acquire(2)
        MM([(bK[:, cs(c)], w_[:, cs(c)], kdec[:, c, :], True, True) for c in range(4)], [w_, kdec], [bK])
        MM([(bQ[:, cs(c)], w_[:, cs(c)], AT[:, cs(c)], True, True) for c in range(4)], [w_, AT], [bQ])
        yield
        P.op("act", lambda e: e.mul(out=nK[:, :], in_=bK[:, 0:512], mul=-1.0), B([bK]), B([nK]))
        TT("dve", QAT[:, :], F2[:, :], bQ[:, 0:512], ALU.subtract, [F2, bQ], [QAT])
        release(bK, bQ)
        yield
        for c in range(4):
            so = sb_par[h]
            S_old = Sb[h][so]
            S_new = Sb[h][1 - so]
            sb_par[h] = 1 - so
            (bS,) = yield from acquire(1)
            MM([(bS[:, 128:256], kdec[:, c, :], u_[:, cs(c)], True, False),
                (bS[:, 128:256], nK[:, cs(c)], S_old[:, :], False, True),
                (bS[:, 0:128], S_old[:, :], QAT[:, cs(c)], True, False),
                (bS[:, 0:128], u_[:, cs(c)], AT[:, cs(c)], False, True)],
               [S_old, QAT, u_, AT, kdec, nK], [bS])
            yield
            gam = sm[:, 12 + c:13 + c]
            STT("dve", S_new[:, :], Sf[h][:, :], gam, bS[:, 128:256], ALU.mult, ALU.add, [Sf[h], sm, bS], [S_new])
            STT("dve", Sf[h][:, :], Sf[h][:, :], gam, bS[:, 128:256], ALU.mult, ALU.add, [Sf[h], sm, bS], [Sf[h]])
            CP("act", oT[:, cs(c)], bS[:, 0:128], [bS], [oT])
            release(bS)
            yield
        ACT(F0[:, :], oT[:, :], AF.Square, [oT], [F0])
        CP("pool", V[0][:, :], F0[:, :], [F0], [V[0]])
        yield
        (b2,) = yield from acquire(1)
        MM([(b2[:, 0:512], ONESB, V[0][:, :], True, True)], [V[0], cstb2], [b2])
        yield
        ACT(F1[:, :], b2[:, 0:512], AF.Ln, [b2], [F1], bias=EPS, scale=1.0 / 128)
        release(b2)
        ACT(F1[:, :], F1[:, :], AF.Exp, [F1], [F1], scale=-0.5)
        STT("dve", ob[:, h, :], oT[:, :], ngP[:, 0:1], F1[:, :], ALU.mult, ALU.mult, [oT, F1, ngP], [ob])
        yield

    def run_il(gens):
        gens = [g for g in gens if g is not None]
        while gens:
            for g in list(gens):
                try:
                    next(g)
                except StopIteration:
                    gens.remove(g)

    def load_wq(h):
        wq = qkvw[h % 2]
        DMA("sp", wq[:, :, :], wsc[h, :, :, 0:384], f"qw{h % 2}", [wsc_b[h]], [wq])
        return wq

    x_loaded = set()

    def load_x(tt):
        if tt in x_loaded or tt >= NT:
            return
        x_loaded.add(tt)
        for sub in range(4):
            r0 = tt * 512 + sub * 128
            DMA("sp", xt[sub][:, :], xw[r0:r0 + 128, :], f"xl{sub}", [], [xt[sub]])

    def tile_start(tt):
        tp = tt % 2
        betaP, gP, GP, s1P, gphl, bphl = betaPs[tp], gPs[tp], GPs[tp], s1Ps[tp], gphls[tp], bphls[tp]
        load_x(tt)
        for sub in range(4):
            (bk,) = yield from acquire(1)
            norm_to_T(xt[sub], gmixB, hT, sub, bk)
            release(bk)
            yield
        if tt < T_OWN0 - 1:
            load_x(tt + 1)
        MM([(ps_small[:, c * 16:(c + 1) * 16], hT[:, k, c * 128:(c + 1) * 128], wab[:, k, :], k == 0, k == 7)
            for c in range(4) for k in range(8)], [hT, wab], [ps_small])
        pa = ps_small[:, 0:64].rearrange("p (c x) -> p c x", c=4)
        vch = lambda t: t[:, :].rearrange("p (c h) -> p c h", c=4)
        vhc = lambda t: t[:, :].rearrange("p (h c) -> p c h", c=4)
        ACT(vhc(betaP), pa[:, :, 8:16], AF.Exp, [ps_small], [betaP], scale=-1.0)
        ACT(betaP[:, :], betaP[:, :], AF.Ln, [betaP], [betaP], bias=1.0)
        ACT(betaP[:, :], betaP[:, :], AF.Exp, [betaP], [betaP], scale=-1.0)
        TT("dve", vch(tmpP), pa[:, :, 0:8], vch(dtb), ALU.add, [ps_small, dtb], [tmpP])
        ACT(tmpP[:, :], tmpP[:, :], AF.Exp, [tmpP], [tmpP])
        ACT(tmpP[:, :], tmpP[:, :], AF.Ln, [tmpP], [tmpP], bias=1.0)
        TT("dve", vhc(gP), vch(tmpP), vch(negA), ALU.mult, [tmpP, negA], [gP])
        CP("pool", gphl[:, 0, :], gP[:, :], [gP], [gphl])
        TT("pool", tmpP[:, :], gP[:, :], gphl[:, 0, :], ALU.subtract, [gP, gphl], [tmpP])
        CP("pool", gphl[:, 1, :], tmpP[:, :], [tmpP], [gphl])
        CP("pool", bphl[:, 0, :], betaP[:, :], [betaP], [bphl])
        TT("pool", tmpP[:, :], betaP[:, :], bphl[:, 0, :], ALU.subtract, [betaP, bphl], [tmpP])
        CP("pool", bphl[:, 1, :], tmpP[:, :], [tmpP], [bphl])
        MM([(ps_small[:, 64:96], cstb2[:, 0, :], gphl[:, 0, :], True, False), (ps_small[:, 64:96], cstb2[:, 0, :], gphl[:, 1, :], False, True)], [cstb2, gphl], [ps_small])
        CP("act", GP[:, :], ps_small[:, 64:96], [ps_small], [GP])
        ACT(kegP[:, :], ps_small[:, 64:96], AF.Exp, [ps_small], [kegP])
        TT("dve", s1P[:, :], kegP[:, :], betaP[:, :], ALU.mult, [kegP, betaP], [s1P])
        yield

    def gdn_tiles(tts):
        tts = list(tts)
        gl = [(tt, h) for tt in tts for h in range(NH)]
        doneA = set()
        doneC = set()

        def chainA():
            for gi, (tt, h) in enumerate(gl):
                while gi >= NSETS and gl[gi - NSETS] not in doneC:
                    yield
                if h == 0:
                    while any((tt - 2, hh) in gl and (tt - 2, hh) not in doneC for hh in range(NH)):
                        yield
                    yield from tile_start(tt)
                yield from stageA(tt, h, load_wq(h), gi % NSETS)
                doneA.add((tt, h))

        def chainC(p, delay):
            for _ in range(delay):
                yield
            for gi, (tt, h) in enumerate(gl):
                if gi % NPIPE != p:
                    continue
                while (tt, h) not in doneA:
                    yield
                while gi >= NH and gl[gi - NH] not in doneC:
                    yield
                yield from stageC(tt, h, gi % NSETS, p)
                doneC.add((tt, h))

        run_il([chainA()] + [chainC(p, p * STAGGER) for p in range(NPIPE)])

    out_sem_total = [0]

    def phase2(tt, c0, c1, store):
        n = c1 - c0
        subs = list(range(c0 // 128, c1 // 128))
        seq = U_A + U_Z + U_G + U_O + U_UP + [uu for row in U_D for (uu, _) in row]
        wsm = WStream(seq)
        hs = hT[:, :, c0:c1]
        for ct in range(8):
            wu = wsm.get()
            psc = pslot()
            MM([(psc[:, 0:n], wu[:, k, 128:256], hT[:, k, c0:c1], k == 0, k == 7) for k in range(8)], [wu, hT], [psc])
            CP("act", cgs[:, 0:n], psc[:, 0:n], [psc], [cgs])
            psx = pslot()
            MM([(psx[:, 0:n], wu[:, k, 256:384], hT[:, k, c0:c1], k == 0, k == 7) for k in range(8)], [wu, hT], [psx])
            w_ = uw[ct % 2]
            TT("dve", w_[:, 2:2 + n], psx[:, 0:n], cgs[:, 0:n], ALU.mult, [psx, cgs], [w_])
            dwconv(w_, carA[:, ct, :], 3, lambda jj, ct=ct: cwA[:, ct, jj:jj + 1], n, cvg[:, 0:n], [cwA], [cvg], carA)
            psb = pslot()
            MM([(psb[:, 0:n], wu[:, k, 0:128], hT[:, k, c0:c1], k == 0, k == 7) for k in range(8)], [wu, hT], [psb])
            TT("dve", yap[:, ct, 0:n], psb[:, 0:n], cvg[:, 0:n], ALU.mult, [psb, cvg], [yap])
        for i2 in range(2):
            wu = wsm.get()
            for j in range(4):
                ct = i2 * 4 + j
                ps = pslot()
                MM([(ps[:, 0:n], wu[:, k, j * 128:(j + 1) * 128], hT[:, k, c0:c1], k == 0, k == 7) for k in range(8)], [wu, hT], [ps])
                ACT(sz[:, ct, 0:n], ps[:, 0:n], AF.Silu, [ps], [sz])
                TT("pool", sz[:, ct, 0:n], sz[:, ct, 0:n], ob[:, ct, c0:c1], ALU.mult, [sz, ob], [sz])
        for nt in range(8):
            wu = wsm.get()
            ps = pslot()
            MM([(ps[:, 0:n], wu[:, k, 0:128], hT[:, k, c0:c1], k == 0, k == 7) for k in range(8)], [wu, hT], [ps])
            ACT(sga[:, 0:n], ps[:, 0:n], AF.Sigmoid, [ps], [sga])
            ps = pslot()
            MM([(ps[:, 0:n], wu[:, k, 128:256], hT[:, k, c0:c1], k == 0, k == 7) for k in range(8)], [wu, hT], [ps])
            ACT(sgb[:, 0:n], ps[:, 0:n], AF.Sigmoid, [ps], [sgb])
            ps = pslot()
            MM([(ps[:, 0:n], wu[:, k, 256:384], yap[:, k, 0:n], k == 0, k == 7) for k in range(8)], [wu, yap], [ps])
            TT("dve", sga[:, 0:n], sga[:, 0:n], ps[:, 0:n], ALU.mult, [sga, ps], [sga])
            ps = pslot()
            MM([(ps[:, 0:n], wu[:, k, 384:512], sz[:, k, 0:n], k == 0, k == 7) for k in range(8)], [wu, sz], [ps])
            TT("dve", sgb[:, 0:n], sgb[:, 0:n], ps[:, 0:n], ALU.mult, [sgb, ps], [sgb])
            TT("pool", mixT[:, nt, 0:n], sga[:, 0:n], sgb[:, 0:n], ALU.add, [sga, sgb], [mixT])
        for nh in range(2):
            wu = wsm.get()
            for si, sub in enumerate(subs):
                ps = pslot()
                MM([(ps[:, :], mixT[:, k, si * 128:(si + 1) * 128], wu[:, k, :], k == 0, k == 7) for k in range(8)], [wu, mixT], [ps])
                TT("dve", xt[sub][:, nh * 512:(nh + 1) * 512], xt[sub][:, nh * 512:(nh + 1) * 512], ps[:, :], ALU.add, [xt[sub], ps], [xt[sub]])
        for si, sub in enumerate(subs):
            norm_to_T(xt[sub], gffnB, h2T, si)
        for fp in range(11):
            wu = wsm.get()
            for j in range(2):
                ft = fp * 2 + j
                psg = pslot()
                MM([(psg[:, 0:n], wu[:, k, j * 256:j * 256 + 128], h2T[:, k, 0:n], k == 0, k == 7) for k in range(8)], [wu, h2T], [psg])
                w0 = uw[0]
                CP("act", w0[:, 2:2 + n], psg[:, 0:n], [psg], [w0])
                dwconv(w0, carF[:, ft, :], 3, lambda jj, ft=ft: cwF[:, ft, jj:jj + 1], n, cvg[:, 0:n], [cwF], [cvg], carF)
                psv = pslot()
                MM([(psv[:, 0:n], wu[:, k, j * 256 + 128:j * 256 + 256], h2T[:, k, 0:n], k == 0, k == 7) for k in range(8)], [wu, h2T], [psv])
                w1 = uw[1]
                CP("act", w1[:, 2:2 + n], psv[:, 0:n], [psv], [w1])
                dwconv(w1, carF[:, 22 + ft, :], 3, lambda jj, ft=ft: cwF[:, 22 + ft, jj:jj + 1], n, cvv[:, 0:n], [cwF], [cvv], carF)
                ACT(cvg[:, 0:n], cvg[:, 0:n], AF.Silu, [cvg], [cvg])
                TT("dve", actT[:, ft, 0:n], cvg[:, 0:n], cvv[:, 0:n], ALU.mult, [cvg, cvv], [actT])
        for nh in range(2):
            wus = [(wsm.get(ahead=0), nk) for (_, nk) in U_D[nh]]
            if not store:
                continue
            for si, sub in enumerate(subs):
                ps = pslot()
                mms = []
                for kg, (wu, nk) in enumerate(wus):
                    for k in range(nk):
                        kk = kg * 8 + k
                        mms.append((ps[:, :], actT[:, kk, si * 128:(si + 1) * 128], wu[:, k, :], kk == 0, kk == 21))
                MM(mms, [actT] + [w for w, _ in wus], [ps])
                TT("dve", xt[sub][:, nh * 512:(nh + 1) * 512], xt[sub][:, nh * 512:(nh + 1) * 512], ps[:, :], ALU.add, [xt[sub], ps], [xt[sub]])
        if store:
            for si, sub in enumerate(subs):
                st = st_small[sidx[0] % 2]
                sidx[0] += 1
                yo = yout[0]
                ACT(junk[:, :], xt[sub][:, :], AF.Square, [xt[sub]], [junk, st], accum=st[:, 0:1])
                ACT(st[:, 1:2], st[:, 0:1], AF.Ln, [st], [st], bias=EPS, scale=1.0 / D)
                ACT(st[:, 2:3], st[:, 1:2], AF.Exp, [st], [st], scale=-0.5)
                STT("dve", yo[:, :], xt[sub][:, :], st[:, 2:3], gfinB[:, :], ALU.mult, ALU.mult, [xt[sub], st, gfinB], [yo])
                row0 = (tt - T_OWN0) * 512 + sub * 128
                DMA("sp", out_d[row0:row0 + 128, :], yo[:, :], "store0", [yo], [])

    def main_loop():
        ck(10)
        n_cont = max(T_OWN0 - 1, 0)
        if n_cont > 0:
            gdn_tiles(range(0, n_cont))
        for tt in range(n_cont, NT):
            gdn_tiles([tt])
            ck(70)
            if tt == T_OWN0 - 1:
                barrier()
                phase2(tt, 384, 512, False)
                barrier()
                ck(80)
            elif tt >= T_OWN0:
                barrier()
                phase2(tt, 0, 512, True)
                barrier()

    try:
        main_loop()
    except _Stop:
        pass

    fw = [(k_, v_) for k_, v_ in P.dsem.items() if k_.startswith("store")]
    if "dbg" in P.dsem:
        fw.append(("dbg", P.dsem["dbg"]))
    for k_ in P.dsem:
        if k_.startswith("xl") or k_.startswith("qw") or k_.startswith("ws") or k_ == "const":
            fw.append((k_, P.dsem[k_]))
    P.final_wait("sp", fw)
    P.final_wait("sp", [(k_, e_.count) for k_, e_ in P.E.items() if e_.count > 0])

    keys = list(P.E.keys()) + list(P.dsem.keys())
    sems = {k: es.enter_context(nc.semaphore("s_" + k)) for k in keys}
    with nc.Block() as block:
        def replay(name):
            def run(eng):
                for waits, fn, inc in P.E[name].ops:
                    for k, v in waits:
                        eng.wait_ge(sems[k], v)
                    if fn is None:
                        continue
                    if isinstance(fn, tuple):
                        ins = fn[0](eng)
                        ins.annotate(fn[1])
                    else:
                        ins = fn(eng)
                    ins.then_inc(sems[inc[0]], inc[1])
            return run
        block.tensor(replay("pe"))
        block.scalar(replay("act"))
        block.vector(replay("dve"))
        block.gpsimd(replay("pool"))
        block.sync(replay("sp"))
    es.close()
    return nc


def consts_np():
    i = np.arange(128)
    ident = np.eye(128, dtype=np.float32)
    uti = (i[:, None] <= i[None, :]).astype(np.float32)
    negsu = np.where(i[:, None] < i[None, :], 0.0, -30000.0).astype(np.float32)
    possl = np.where(i[:, None] > i[None, :], 0.0, 30000.0).astype(np.float32)
    ones = np.ones((128, 128), np.float32)
    return np.ascontiguousarray(np.stack([ident, uti, negsu, possl, ones], axis=1))


def make_in_maps(inputs, W, OWN, core_tokens):
    f = lambda a: np.ascontiguousarray(np.asarray(a, dtype=np.float32))
    x = f(inputs["x"])
    bc = lambda v: np.ascontiguousarray(np.broadcast_to(f(v).reshape(1, -1), (128, f(v).size)))
    cw = lambda w, nt: np.ascontiguousarray(f(w).reshape(w.shape[-2], nt, 128).transpose(2, 1, 0))
    shared = {
        "w_in": f(inputs["w_in"][0]), "w_a": f(inputs["w_a_out"][0]), "w_b": f(inputs["w_b_out"][0]),
        "w_o": f(inputs["w_o"][0]), "w_up": f(inputs["w_up"][0]), "w_dn": f(inputs["w_down"][0]),
        "gmixB": bc(inputs["norm_mix_g"][0]), "gffnB": bc(inputs["norm_ffn_g"][0]), "gfinB": bc(inputs["norm_final_g"]),
        "cwA": cw(np.asarray(inputs["conv_a_w"][0]), 8), "cwG": cw(np.asarray(inputs["gdn_conv_w"][0]), 24),
        "cwF": cw(np.asarray(inputs["ffn_conv_w"][0]), 44),
        "alogB": np.ascontiguousarray(np.tile(bc(inputs["gdn_A_log"][0]), (1, 4))),
        "dtbB": np.ascontiguousarray(np.tile(bc(inputs["gdn_dt_bias"][0]), (1, 4))),
        "ngP": f(inputs["gdn_norm_g"][0]).reshape(128, 1),
        "cst": consts_np(),
    }
    maps = []
    for (b, s) in core_tokens:
        end = s + OWN
        xwin = np.zeros((W, D), np.float32)
        lo = max(0, end - W)
        xwin[W - (end - lo):] = x[b, lo:end]
        m = dict(shared)
        m["xw"] = xwin
        maps.append(m)
    return maps


def kernel(**inputs):
    x = np.asarray(inputs["x"])
    Bsz, S, _ = x.shape
    core_tokens = [(c // 4, (c % 4) * OWN_FULL) for c in range(NCORES)]
    nc = build(SEQ, OWN_FULL)
    maps = make_in_maps(inputs, SEQ, OWN_FULL, core_tokens)
    res = run_bass_kernel_spmd(nc, maps, core_ids=list(range(NCORES)))
    out = np.zeros((Bsz, S, D), np.float32)
    for c, (b, s) in enumerate(core_tokens):
        out[b, s:s + OWN_FULL] = np.asarray(res.results[c]["out"])
    return out
```

```python
import numpy as np
from contextlib import ExitStack
import concourse.bass as bass
import concourse.mybir as mybir
from concourse.bass_utils import run_bass_kernel_spmd

F32 = mybir.dt.float32
BF16 = mybir.dt.bfloat16
AF = mybir.ActivationFunctionType
ALU = mybir.AluOpType

D = 1024
NH = 8
DFF = 2816
EPS = 1e-6
NCORES = 8
SEQ = 8192
OWN_FULL = 2048
SAME_SYNC = True
STAGGER = 11
A_PRIO = 3
DEBUG_TAGS = False

C_BG, C_CG, C_XV, C_Q, C_K, C_V, C_Z, C_A, C_B, C_GA, C_GB = 0, 1024, 2048, 3072, 4096, 5120, 6144, 7168, 7176, 7184, 8208


class Buf:
    __slots__ = ("name", "w", "rs", "psum")

    def __init__(self, name, psum=False):
        self.name = name
        self.w = None
        self.rs = []
        self.psum = psum


class Eng:
    def __init__(self, name):
        self.name = name
        self.ops = []
        self.count = 0
        self.known = {}


class Prog:
    def __init__(self):
        self.E = {n: Eng(n) for n in ("pe", "act", "dve", "pool", "sp")}
        self.dsem = {}

    def _deps(self, e, reads, writes):
        deps = {}

        def add(dep):
            if dep is None:
                return
            k, v = dep
            if deps.get(k, 0) < v:
                deps[k] = v

        for b in reads:
            add(b.w)
            if b.psum:
                for r in b.rs:
                    if r[0] != e.name:
                        add(r)
        for b in writes:
            add(b.w)
            for r in b.rs:
                add(r)
        waits = []
        for k, v in deps.items():
            if k == e.name and (k == "pe" or not SAME_SYNC):
                continue
            if e.known.get(k, 0) >= v:
                continue
            e.known[k] = v
            waits.append((k, v))
        return waits

    def op(self, eng, fn, reads=(), writes=()):
        e = self.E[eng]
        waits = self._deps(e, reads, writes)
        e.count += 1
        n = e.count
        if DEBUG_TAGS:
            import sys
            f = sys._getframe(1)
            tag = []
            for _ in range(4):
                if f is None:
                    break
                tag.append(str(f.f_lineno))
                f = f.f_back
            fn = (fn, "L" + "<".join(tag))
        e.ops.append((waits, fn, (eng, 1)))
        for b in writes:
            b.w = (eng, n)
            b.rs = []
        for b in reads:
            b.rs.append((eng, n))

    def dma(self, queue, fn, key, reads=(), writes=()):
        e = self.E[queue]
        waits = self._deps(e, reads, writes)
        self.dsem[key] = self.dsem.get(key, 0) + 16
        n = self.dsem[key]
        e.ops.append((waits, fn, (key, 16)))
        for b in writes:
            b.w = (key, n)
            b.rs = []
        for b in reads:
            b.rs.append((key, n))

    def final_wait(self, eng, deps):
        self.E[eng].ops.append((list(deps), None, None))


class _Stop(Exception):
    pass


def build(W=SEQ, OWN=OWN_FULL, dbg=None, stage=None, flags=()):
    def ck(n):
        if stage is not None and stage == n:
            raise _Stop()

    NT = W // 512
    T_OWN0 = (W - OWN) // 512
    NOWN = OWN // 512
    nc = bass.Bass("TRN2", target_bir_lowering=False)
    P = Prog()
    dbg = dbg or []
    dbg_out = {}

    def din(name, shape, dt=F32):
        return nc.dram_tensor(name, shape, dt, kind="ExternalInput").ap()

    xw = din("xw", [W, D])
    w_in = din("w_in", [D, 9232])
    w_a = din("w_a", [D, D])
    w_b = din("w_b", [D, D])
    w_o = din("w_o", [D, D])
    w_up = din("w_up", [D, 2 * DFF])
    w_dn = din("w_dn", [DFF, D])
    gmixB_d = din("gmixB", [128, D])
    gffnB_d = din("gffnB", [128, D])
    gfinB_d = din("gfinB", [128, D])
    cwA_d = din("cwA", [128, 8, 3])
    cwG_d = din("cwG", [128, 24, 4])
    cwF_d = din("cwF", [128, 44, 3])
    alog_d = din("alogB", [128, 32])
    dtb_d = din("dtbB", [128, 32])
    ng_d = din("ngP", [128, 1])
    cst_d = din("cst", [128, 5, 128])
    out_d = nc.dram_tensor("out", [OWN, D], F32, kind="ExternalOutput").ap()
    NUNIT2 = 37
    wsc = nc.dram_tensor("wsc", [8 + NUNIT2, 128, 8, 512], BF16, kind="Internal").ap()
    for name, shape in dbg:
        dbg_out[name] = nc.dram_tensor("dbg_" + name, shape, F32, kind="ExternalOutput").ap()

    es = ExitStack()
    SB_ACC = [0, []]
    build.sb_acc = SB_ACC

    class T:
        def __init__(self, name, shape, dt=F32):
            self.t = es.enter_context(nc.sbuf_tensor("sb_" + name, shape, dt))
            self.b = Buf(name)
            SB_ACC[0] += int(np.prod(shape[1:])) * (2 if dt == BF16 else 4)
            SB_ACC[1].append((name, int(np.prod(shape[1:])) * (2 if dt == BF16 else 4)))

        def __getitem__(self, k):
            return self.t[k]

    class PS:
        def __init__(self, name, shape, dt=F32):
            self.t = es.enter_context(nc.psum_tensor(name, shape, dt))
            self.b = Buf(name, psum=True)

        def __getitem__(self, k):
            return self.t[k]

    class View:
        def __init__(self, ap, name):
            self.ap = ap
            self.b = Buf(name)

        def __getitem__(self, k):
            return self.ap[k]

    arena = T("arena", [128, 11264], BF16)
    arena2 = T("arena2", [128, 12288], BF16)
    arenas = {1: (arena, 11264, [0]), 2: (arena2, 12288, [0])}

    def AV(name, shape, dt=F32, which=1):
        ar, cap, aoff = arenas[which]
        n = int(np.prod(shape[1:]))
        nb = n * (1 if dt == BF16 else 2)
        ap = ar.t[:, aoff[0]:aoff[0] + nb]
        aoff[0] += nb
        assert aoff[0] <= cap
        if dt != BF16:
            ap = ap.bitcast(F32)
        if len(shape) == 3:
            ap = ap.rearrange("p (a b) -> p a b", a=shape[1])
        return View(ap, name)

    def barrier():
        for en, e in P.E.items():
            waits = [(k_, e2.count) for k_, e2 in P.E.items() if k_ != en and e2.count > 0]
            waits += [(k_, v_) for k_, v_ in P.dsem.items() if not k_.startswith("wc")]
            e.ops.append((waits, None, None))
            for k_, v_ in waits:
                e.known[k_] = max(e.known.get(k_, 0), v_)

    def B(xs):
        return [x.b if hasattr(x, "b") else x for x in xs]

    def ACT(out, in_, func, r, w, bias=0.0, scale=1.0, accum=None):
        if accum is None:
            P.op("act", lambda e: e.activation(out=out, in_=in_, func=func, bias=bias, scale=scale), B(r), B(w))
        else:
            P.op("act", lambda e: e.activation(out=out, in_=in_, func=func, bias=bias, scale=scale, accum_out=accum), B(r), B(w))

    def TS(eng, out, in0, s1, s2, op0, op1, r, w):
        if op1 is None:
            P.op(eng, lambda e: e.tensor_scalar(out=out, in0=in0, scalar1=s1, scalar2=None, op0=op0), B(r), B(w))
        else:
            P.op(eng, lambda e: e.tensor_scalar(out=out, in0=in0, scalar1=s1, scalar2=s2, op0=op0, op1=op1), B(r), B(w))

    def STT(eng, out, in0, scalar, in1, op0, op1, r, w):
        P.op(eng, lambda e: e.scalar_tensor_tensor(out=out, in0=in0, scalar=scalar, in1=in1, op0=op0, op1=op1), B(r), B(w))

    def TT(eng, out, in0, in1, op, r, w):
        P.op(eng, lambda e: e.tensor_tensor(out=out, in0=in0, in1=in1, op=op), B(r), B(w))

    def CP(eng, out, in_, r, w):
        if eng == "act":
            P.op("act", lambda e: e.copy(out=out, in_=in_), B(r), B(w))
        else:
            P.op(eng, lambda e: e.tensor_copy(out=out, in_=in_), B(r), B(w))

    def MM(mms, r, w):
        def fn(e):
            ins = None
            for (o, l, rh, st, sp) in mms:
                ins = e.matmul(o, l, rh, start=st, stop=sp)
            return ins
        P.op("pe", fn, B(r), B(w))

    def TR(trs, r, w):
        def fn(e):
            ins = None
            for (o, i, idn) in trs:
                ins = e.transpose(o, i, idn)
            return ins
        P.op("pe", fn, B(r), B(w))

    def DMA(queue, out, in_, key, r, w):
        q = {"sp": "sp", "pool": "pool", "act": "act"}[queue]
        P.dma(q, lambda e: e.dma_start(out=out, in_=in_), key, B(r), B(w))

    cst = T("cst", [128, 5, 128])
    cstb = T("cstb", [128, 128], BF16)
    cstb2 = T("cstb2", [128, 2, 128], BF16)
    gphls = [T(f"gphl{i}", [128, 2, 32], BF16) for i in range(2)]
    gmixB = T("gmixB", [128, D])
    gffnB = T("gffnB", [128, D])
    gfinB = T("gfinB", [128, D])
    cwA = T("cwA", [128, 8, 3])
    cwG = T("cwG", [128, 24, 4])
    cwF = T("cwF", [128, 44, 3])
    negA = T("negA", [128, 32])
    dtb = T("dtb", [128, 32])
    ngP = T("ngP", [128, 1])
    wab = T("wab", [128, 8, 16], BF16)
    IDF = cst[:, 0, :]
    UTI = cst[:, 1, :]
    NEGSU = cst[:, 2, :]
    POSSL = cst[:, 3, :]
    ONES = cst[:, 4, :]
    IDB = cstb[:, :]

    xt = [T(f"xt{i}", [128, D]) for i in range(4)]
    xn = T("xn", [128, D], BF16)
    junk = T("junk", [128, D], BF16)
    st_small = [T(f"st{i}", [128, 4]) for i in range(2)]
    hT = T("hT", [128, 8, 512], BF16)
    h2T = hT
    wslot = [T(f"wslot{i}", [128, 8, 512], BF16) for i in range(3)]
    qkvw = [T(f"qkvw{i}", [128, 8, 384], BF16) for i in range(2)]
    pre = [T(f"pre{i}", [128, 515]) for i in range(2)]
    cacc = T("cacc", [128, 512])
    post = T("post", [128, 512])
    sqb = T("sqb", [128, 512])
    rr = T("rr", [128, 512])
    NSETS = 5
    qTs = [T(f"qT{i}", [128, 512], BF16) for i in range(NSETS)]
    kTs = [T(f"kT{i}", [128, 512], BF16) for i in range(NSETS)]
    vTs = [T(f"vT{i}", [128, 512], BF16) for i in range(NSETS)]
    oTs = [T(f"oT{i}", [128, 512]) for i in range(2)] + [AV("oT2", [128, 512], F32, 2)]
    oT = oTs[0]
    carG = T("carG", [128, 24, 3])
    carA = T("carA", [128, 8, 2])
    carF = T("carF", [128, 44, 2])
    betaPs = [T(f"betaP{i}", [128, 32]) for i in range(2)]
    gPs = [T(f"gP{i}", [128, 32]) for i in range(2)]
    GPs = [T(f"GP{i}", [128, 32]) for i in range(2)]
    kegP = T("kegP", [128, 32])
    s1Ps = [T(f"s1P{i}", [128, 32]) for i in range(2)]
    tmpP = T("tmpP", [128, 32])
    NPIPE = 3
    def PT(p, name, shape, dt=F32):
        return T(name, shape, dt) if p == 0 else AV(name, shape, dt, p)

    Fs = [[PT(p, f"F{p}_{j}", [128, 512]) for j in range(3)] for p in range(NPIPE)]
    gm0_ = T("gm0", [128, 4, 4, 128], BF16)
    gm = [gm0_ for p in range(NPIPE)]
    Vb = [[PT(p, f"V{p}_{j}", [128, 512], BF16) for j in range(2)] for p in range(NPIPE)]
    VTb = [[PT(p, f"VT{p}_{j}", [128, 512], BF16) for j in range(2)] for p in range(NPIPE)]
    Rb = [[PT(p, f"R{p}_{j}", [128, 512], BF16) for j in range(2)] for p in range(NPIPE)]
    ATb = [PT(p, f"AT{p}", [128, 512], BF16) for p in range(NPIPE)]
    QATb = [PT(p, f"QAT{p}", [128, 512], BF16) for p in range(NPIPE)]
    vbb = [PT(p, f"vb{p}", [128, 4, 128], BF16) for p in range(NPIPE)]
    kbgb = [PT(p, f"kbg{p}", [128, 4, 128], BF16) for p in range(NPIPE)]
    kdecb = [PT(p, f"kdec{p}", [128, 4, 128], BF16) for p in range(NPIPE)]
    ub = [PT(p, f"u{p}", [128, 512], BF16) for p in range(NPIPE)]
    wb = [PT(p, f"w{p}", [128, 512], BF16) for p in range(NPIPE)]
    nKWTb = [PT(p, f"nKWT{p}", [128, 512], BF16) for p in range(NPIPE)]
    kbTs = [PT(p, f"kbT{p}", [128, 512], BF16) for p in range(NPIPE)]
    sm4 = [T(f"sm4_{p}", [128, 16]) for p in range(NPIPE)]
    bphls = [T(f"bphl{i}", [128, 2, 32], BF16) for i in range(2)]
    Sf = [T(f"Sf{h}", [128, 128]) for h in range(NH)]
    Sb = [[T(f"Sb{h}_{j}", [128, 128], BF16) for j in range(2)] for h in range(NH)]
    sb_par = [0] * NH
    ob = T("ob", [128, 8, 512], BF16)
    yap = View(arena2.t[:, 0:4096].rearrange("p (f n) -> p f n", f=8), "yap")
    sz = View(arena2.t[:, 4096:8192].rearrange("p (f n) -> p f n", f=8), "sz")
    mixT = View(arena2.t[:, 8192:12288].rearrange("p (f n) -> p f n", f=8), "mixT")
    sga = rr
    sgb = oT
    cgs = cacc
    uw = pre
    actT = View(arena.t[:, :].rearrange("p (f n) -> p f n", f=22), "actT")
    cvg = post
    cvv = sqb
    yout = [T(f"yout{i}", [128, D]) for i in range(1)]

    class Slot:
        def __init__(self, ap, b):
            self.ap = ap
            self.b = b

        def __getitem__(self, k):
            return self.ap[k]

    ps_proj = [PS(f"ps_proj{i}", [128, 512]) for i in range(2)]
    ps_small = ps_proj[0]
    ps_gb = [Slot(ps_small[:, 128:384], ps_small.b) for i in range(2)]
    ps_mm_t = [PS(f"ps_mm{i}", [128, 512]) for i in range(6)]


    mm_slots = [Slot(ps_mm_t[i][:, :], ps_mm_t[i].b) for i in range(6)]
    bank_free = list(mm_slots)
    mm_i = [0]

    def mslot():
        s = mm_slots[mm_i[0] % 3]
        mm_i[0] += 1
        return s

    proj_i = [0]

    def pslot():
        s = ps_proj[proj_i[0] % 2]
        proj_i[0] += 1
        return s

    cl = [(cst, cst_d), (gmixB, gmixB_d), (gffnB, gffnB_d), (gfinB, gfinB_d), (cwA, cwA_d), (cwG, cwG_d),
          (cwF, cwF_d), (negA, alog_d), (dtb, dtb_d), (ngP, ng_d)]
    for t, d_ in cl:
        DMA("sp", t[:], d_, "const", [], [t])
    for t, d_ in cl:
        t.b.w = ("const", P.dsem["const"])
    CP("dve", cstb[:, :], cst[:, 0, :], [cst], [cstb])
    CP("dve", cstb2[:, 0, :], cst[:, 1, :], [cst], [cstb2])
    CP("dve", cstb2[:, 1, :], cst[:, 4, :], [cst], [cstb2])
    ACT(negA[:, :], negA[:, :], AF.Exp, [negA], [negA])
    TS("dve", negA[:, :], negA[:, :], -1.0, None, ALU.mult, None, [negA], [negA])
    for t in (carG, carA, carF):
        P.op("pool", lambda e, t=t: e.memset(t[:], 0.0), [], [t.b])
    for h in range(NH):
        P.op("pool", lambda e, h=h: e.memset(Sf[h][:, :], 0.0), [], [Sf[h].b])
        P.op("pool", lambda e, h=h: e.memset(Sb[h][0][:, :], 0.0), [], [Sb[h][0].b])

    wsc_b = [Buf(f"wsc{u}") for u in range(8 + NUNIT2)]
    stopped = [False]

    uext = {}

    def cast_piece(u, col0, src, r0, nk, c0, width, key="wcast"):
        uext[u] = (nk, max(uext.get(u, (0, 0))[1], col0 + width))
        s = src[r0:r0 + nk * 128, c0:c0 + width].rearrange("(k p) n -> p k n", p=128)
        cast_dma(wsc[u, :, 0:nk, col0:col0 + width], s)

    cast_n = [0]
    dummy = T("dummy", [128, 4])

    def cast_dma(out, in_):
        if "nocast" in flags:
            return
        key = f"wc{cast_n[0] % 6}"
        cast_n[0] += 1
        if P.dsem.get(key, 0) > 0:
            P.E["pool"].ops.append(([(key, P.dsem[key])], None, None))
        DMA("pool", out, in_, key, [], [])

    def cast_barrier(bufs):
        fake = []
        for j in range(6):
            key = f"wc{j}"
            if P.dsem.get(key, 0) > 0:
                fb = Buf("fk")
                fb.w = (key, P.dsem[key])
                fake.append(fb)
        P.op("pool", lambda e: e.memset(dummy[:, :], 0.0), fake, [dummy.b])
        for b_ in bufs:
            b_.w = dummy.b.w

    cast_dma(wab[:, :, :], w_in[:, C_A:C_A + 16].rearrange("(k p) n -> p k n", p=128))
    for h in range(NH):
        for j, cb in enumerate((C_Q, C_K, C_V)):
            cast_piece(h, j * 128, w_in, 0, 8, cb + h * 128, 128, key="wcast0")
    cast_barrier([wab.b] + [wsc_b[h] for h in range(NH)])
    units = []
    u = 8
    U_A = []
    for ct in range(8):
        for j, cb in enumerate((C_BG, C_CG, C_XV)):
            cast_piece(u, j * 128, w_in, 0, 8, cb + ct * 128, 128)
        U_A.append(u); u += 1
    U_Z = []
    for i in range(2):
        cast_piece(u, 0, w_in, 0, 8, C_Z + i * 512, 512)
        U_Z.append(u); u += 1
    U_G = []
    for nt in range(8):
        cast_piece(u, 0, w_in, 0, 8, C_GA + nt * 128, 128)
        cast_piece(u, 128, w_in, 0, 8, C_GB + nt * 128, 128)
        cast_piece(u, 256, w_a, 0, 8, nt * 128, 128)
        cast_piece(u, 384, w_b, 0, 8, nt * 128, 128)
        U_G.append(u); u += 1
    U_O = []
    for i in range(2):
        cast_piece(u, 0, w_o, 0, 8, i * 512, 512)
        U_O.append(u); u += 1
    U_UP = []
    for fp in range(11):
        for j in range(2):
            ft = fp * 2 + j
            cast_piece(u, j * 256, w_up, 0, 8, ft * 128, 128)
            cast_piece(u, j * 256 + 128, w_up, 0, 8, DFF + ft * 128, 128)
        U_UP.append(u); u += 1
    U_D = []
    for nh in range(2):
        row = []
        for kg, (k0, nk) in enumerate(((0, 8), (8, 8), (16, 6))):
            cast_piece(u, 0, w_dn, k0 * 128, nk, nh * 512, 512)
            row.append((u, nk)); u += 1
        U_D.append(row)
    assert u == 8 + NUNIT2
    cast_barrier([wsc_b[uu] for uu in range(8, 8 + NUNIT2)])

    ws_state = {"q": [], "n": 0}

    def ws_issue(unit):
        i = ws_state["n"] % 3
        ws_state["n"] += 1
        nk, ncol = uext[unit]
        DMA("sp", wslot[i][:, 0:nk, 0:ncol], wsc[unit, :, 0:nk, 0:ncol], f"ws{i}", [wsc_b[unit]], [wslot[i]])
        return wslot[i]

    class WStream:
        def __init__(self, seq, ahead=2):
            self.seq = list(seq)
            self.loaded = []
            self.pos = 0
            self.ahead = ahead

        def get(self, ahead=None):
            ahead = self.ahead if ahead is None else ahead
            while len(self.loaded) < min(len(self.seq), self.pos + 1 + ahead):
                self.loaded.append(ws_issue(self.seq[len(self.loaded)]))
            s = self.loaded[self.pos]
            self.pos += 1
            return s

    sidx = [0]

    def norm_to_T(src_tile, gB, dstT, sub, bank=None):
        bank = bank if bank is not None else bank_free[0]
        ps_tr = Slot(bank[:, 0:512].bitcast(BF16), bank.b)
        st = st_small[sidx[0] % 2]
        sidx[0] += 1
        ACT(junk[:, :], src_tile[:, :], AF.Square, [src_tile], [junk, st], accum=st[:, 0:1])
        ACT(st[:, 1:2], st[:, 0:1], AF.Ln, [st], [st], bias=EPS, scale=1.0 / D)
        ACT(st[:, 2:3], st[:, 1:2], AF.Exp, [st], [st], scale=-0.5)
        STT("dve", xn[:, :], src_tile[:, :], st[:, 2:3], gB[:, :], ALU.mult, ALU.mult, [src_tile, st, gB], [xn])
        TR([(ps_tr[:, k * 128:(k + 1) * 128], xn[:, k * 128:(k + 1) * 128], IDB) for k in range(8)], [xn, cstb], [ps_tr])
        CP("act", dstT[:, :, sub * 128:(sub + 1) * 128], ps_tr[:, :].rearrange("p (k n) -> p k n", k=8), [ps_tr], [dstT])
        return st

    def dump(name, ap, r):
        if name in dbg_out:
            DMA("sp", dbg_out[name], ap, "dbg", r, [])

    def dwconv(work, carry_ap, K, wts, n, outp, r_w, w_out, car_t):
        CP("pool", work[:, 0:K - 1], carry_ap, [car_t], [work])
        P.op("act", lambda e: e.mul(out=outp, in_=work[:, 0:n], mul=wts(0)), B([work] + r_w), B(w_out))
        for j in range(1, K):
            STT("dve", outp, work[:, j:j + n], wts(j), outp, ALU.mult, ALU.add, [work] + r_w + w_out, w_out)
        CP("pool", carry_ap, work[:, n:n + K - 1], [work], [car_t])

    cb_i = [0]
    bc_i = lambda ap4: ap4.unsqueeze(2).broadcast_to([128, 4, 128])
    bc_c = lambda ap: ap.unsqueeze(1).broadcast_to([128, 4, 128])
    v4 = lambda ap: ap.rearrange("p (c i) -> p c i", c=4)
    UTIB = cstb2[:, 0, :]
    ONESB = cstb2[:, 1, :]

    def acquire(n):
        while len(bank_free) < n:
            yield
        return [bank_free.pop(0) for _ in range(n)]

    def release(*bs):
        for b_ in bs:
            bank_free.append(b_)

    def stageA(tt, h, wq, qs):
        qT, kT, vT = qTs[qs], kTs[qs], vTs[qs]
        cc = [cacc, post, sqb]
        for j in range(3):
            ps = pslot()
            MM([(ps[:, :], wq[:, k, j * 128:(j + 1) * 128], hT[:, k, :], k == 0, k == 7) for k in range(8)], [wq, hT], [ps])
            pw = pre[(cb_i[0]) % 2]
            cb_i[0] += 1
            CP("act", pw[:, 3:515], ps[:, :], [ps], [pw])
            ch = j * 8 + h
            dwconv(pw, carG[:, ch, :], 4, lambda jj, ch=ch: cwG[:, ch, jj:jj + 1], 512, cc[j][:, :], [cwG], [cc[j]], carG)
            yield
        ACT(cc[0][:, :], cc[0][:, :], AF.Silu, [cc[0]], [cc[0]])
        ACT(cc[1][:, :], cc[1][:, :], AF.Silu, [cc[1]], [cc[1]])
        ACT(vT[:, :], cc[2][:, :], AF.Silu, [cc[2]], [vT])
        yield
        for src, dst, scale in ((cc[0], qT, 128 ** -0.5), (cc[1], kT, 1.0)):
            ACT(junk[:, 0:512], src[:, :], AF.Square, [src], [junk])
            ps2 = pslot()
            MM([(ps2[:, :], ONESB, junk[:, 0:512], True, True)], [junk, cstb2], [ps2])
            ACT(rr[:, :], ps2[:, :], AF.Ln, [ps2], [rr], bias=EPS)
            ACT(rr[:, :], rr[:, :], AF.Exp, [rr], [rr], scale=-0.5)
            STT("dve", dst[:, :], src[:, :], scale, rr[:, :], ALU.mult, ALU.mult, [src, rr], [dst])
            yield

    def stageC(tt, h, qs, p=0):
        qT, kT, vT = qTs[qs], kTs[qs], vTs[qs]
        sl = slice(h * 4, h * 4 + 4)
        F0, F1, F2 = Fs[p]
        G, V, VT, R = gm[p], Vb[p], VTb[p], Rb[p]
        AT, QAT, vb, kbg, kdec, u_, w_, nK, sm = ATb[p], QATb[p], vbb[p], kbgb[p], kdecb[p], ub[p], wb[p], nKWTb[p], sm4[p]
        kbT, oT = kbTs[p], oTs[p]
        tp = tt % 2
        betaP, GP, s1P, gphl, bphl = betaPs[tp], GPs[tp], s1Ps[tp], gphls[tp], bphls[tp]
        f2 = lambda t3: t3.rearrange("p c i -> p (c i)")
        cs = lambda c: slice(c * 128, (c + 1) * 128)
        bG, bB = yield from acquire(2)
        TT("pool", G[:, 0], bc_c(UTIB), bc_i(gphl[:, 0, sl]), ALU.mult, [cstb2, gphl], [G])
        TT("pool", G[:, 1], bc_c(UTIB), bc_i(gphl[:, 1, sl]), ALU.mult, [cstb2, gphl], [G])
        TT("pool", G[:, 2], bc_c(IDB), bc_i(bphl[:, 0, sl]), ALU.mult, [cstb, bphl], [G])
        TT("pool", G[:, 3], bc_c(IDB), bc_i(bphl[:, 1, sl]), ALU.mult, [cstb, bphl], [G])
        MM([(bG[:, 0:512], ONESB, f2(G[:, 0]), True, False), (bG[:, 0:512], ONESB, f2(G[:, 1]), False, True)], [G, cstb2], [bG])
        MM([(bB[:, 0:512], ONESB, f2(G[:, 2]), True, False), (bB[:, 0:512], ONESB, f2(G[:, 3]), False, True)], [G, cstb2], [bB])
        yield
        TT("dve", v4(F0[:, :]), v4(bG[:, 0:512]), bc_i(GP[:, sl]), ALU.subtract, [bG, GP], [F0])
        ACT(F2[:, :], bG[:, 0:512], AF.Exp, [bG], [F2])
        CP("act", sm[:, 0:4], v4(bG[:, 0:512])[:, :, 127], [bG], [sm])
        TT("dve", kbT[:, :], kT[:, :], bB[:, 0:512], ALU.mult, [kT, bB], [kbT])
        release(bG, bB)
        yield
        TT("dve", v4(F1[:, :]), v4(F0[:, :]), bc_c(POSSL), ALU.add, [F0, cst], [F1])
        TT("pool", v4(F0[:, :]), v4(F0[:, :]), bc_c(NEGSU), ALU.add, [F0, cst], [F0])
        TT("dve", sm[:, 4:8], sm[:, 0:4], GP[:, sl], ALU.subtract, [sm, GP], [sm])
        yield
        ACT(F1[:, :], F1[:, :], AF.Exp, [F1], [F1], scale=-1.0)
        ACT(F0[:, :], F0[:, :], AF.Exp, [F0], [F0])
        ACT(sm[:, 8:12], sm[:, 4:8], AF.Exp, [sm], [sm])
        ACT(sm[:, 12:16], sm[:, 0:4], AF.Exp, [sm], [sm])
        TT("pool", F2[:, :], qT[:, :], F2[:, :], ALU.mult, [qT, F2], [F2])
        bL, bU, bA = yield from acquire(3)
        MM([(bL[:, cs(c)], kbT[:, cs(c)], kT[:, cs(c)], True, True) for c in range(4)], [kbT, kT], [bL])
        MM([(bU[:, cs(c)], kT[:, cs(c)], kbT[:, cs(c)], True, True) for c in range(4)], [kbT, kT], [bU])
        MM([(bA[:, cs(c)], kT[:, cs(c)], qT[:, cs(c)], True, True) for c in range(4)], [qT, kT], [bA])
        yield
        STT("dve", VT[0][:, :], bL[:, 0:512], -1.0, F1[:, :], ALU.mult, ALU.mult, [bL, F1], [VT[0]])
        STT("dve", V[0][:, :], bU[:, 0:512], -1.0, F0[:, :], ALU.mult, ALU.mult, [bU, F0], [V[0]])
        TT("pool", v4(F0[:, :]), v4(F0[:, :]), bc_c(IDF), ALU.add, [F0, cst], [F0])
        TT("dve", AT[:, :], bA[:, 0:512], F0[:, :], ALU.mult, [bA, F0], [AT])
        TT("pool", v4(R[0][:, :]), v4(V[0][:, :]), bc_c(IDB), ALU.add, [V[0], cstb], [R[0]])
        release(bL, bU, bA)
        (bX,) = yield from acquire(1)
        bXb = bX[:, 0:512].bitcast(BF16)
        TR([(bXb[:, cs(c)], kT[:, cs(c)], IDB) for c in range(4)] +
           [(bXb[:, 512 + c * 128:512 + (c + 1) * 128], vT[:, cs(c)], IDB) for c in range(4)], [kT, vT, cstb], [bX])
        bT, bV = yield from acquire(2)
        MM([(bT[:, cs(c)], V[0][:, cs(c)], VT[0][:, cs(c)], True, True) for c in range(4)], [V[0], VT[0]], [bT])
        MM([(bV[:, cs(c)], VT[0][:, cs(c)], V[0][:, cs(c)], True, True) for c in range(4)], [V[0], VT[0]], [bV])
        yield
        TT("dve", kbg[:, :, :], v4(bXb[:, 0:512]), bc_i(s1P[:, sl]), ALU.mult, [bX, s1P], [kbg])
        TT("dve", kdec[:, :, :], v4(bXb[:, 0:512]), bc_i(sm[:, 8:12]), ALU.mult, [bX, sm], [kdec])
        TT("dve", vb[:, :, :], v4(bXb[:, 512:1024]), bc_i(betaP[:, sl]), ALU.mult, [bX, betaP], [vb])
        release(bX)
        cur = 0
        for m in range(1, 7):
            nxt = 1 - cur
            CP("act", VT[nxt][:, :], bT[:, 0:512], [bT], [VT[nxt]])
            release(bT)
            if m < 6:
                CP("act", V[nxt][:, :], bV[:, 0:512], [bV], [V[nxt]])
                release(bV)
            if m > 1:
                CP("act", R[cur][:, :], bR[:, 0:512], [bR], [R[cur]])
                release(bR)
            yield
            (bR,) = yield from acquire(1)
            mmr = []
            for c in range(4):
                mmr.append((bR[:, cs(c)], IDB, R[cur][:, cs(c)], True, False))
                mmr.append((bR[:, cs(c)], VT[nxt][:, cs(c)], R[cur][:, cs(c)], False, True))
            MM(mmr, [cstb, R[cur], VT[nxt]], [bR])
            if m < 5:
                bT, bV = yield from acquire(2)
            elif m == 5:
                (bT,) = yield from acquire(1)
            if m < 6:
                MM([(bT[:, cs(c)], V[nxt][:, cs(c)], VT[nxt][:, cs(c)], True, True) for c in range(4)], [V[nxt], VT[nxt]], [bT])
            if m < 5:
                MM([(bV[:, cs(c)], VT[nxt][:, cs(c)], V[nxt][:, cs(c)], True, True) for c in range(4)], [V[nxt], VT[nxt]], [bV])
            cur = nxt
            yield
        CP("act", R[cur][:, :], bR[:, 0:512], [bR], [R[cur]])
        release(bR)
        Rf = R[cur]
        yield
        b0, b1 = yield from acquire(2)
        MM([(b0[:, cs(c)], Rf[:, cs(c)], vb[:, c, :], True, True) for c in range(4)], [Rf, vb], [b0])
        MM([(b1[:, cs(c)], Rf[:, cs(c)], kbg[:, c, :], True, True) for c in range(4)], [Rf, kbg], [b1])
        yield
        CP("act", u_[:, :], b0[:, 0:512], [b0], [u_])
        CP("act", w_[:, :], b1[:, 0:512], [b1], [w_])
        release(b0, b1)
        yield
        bK, bQ = yield from acquire(2)
        MM([(bK[:, cs(c)], w_[:, cs(c)], kdec[:, c, :], True, True) for c in range(4)], [w_, kdec], [bK])
        MM([(bQ[:, cs(c)], w_[:, cs(c)], AT[:, cs(c)], True, True) for c in range(4)], [w_, AT], [bQ])
        yield
        P.op("act", lambda e: e.mul(out=nK[:, :], in_=bK[:, 0:512], mul=-1.0), B([bK]), B([nK]))
        TT("dve", QAT[:, :], F2[:, :], bQ[:, 0:512], ALU.subtract, [F2, bQ], [QAT])
        release(bK, bQ)
        yield
        for c in range(4):
            so = sb_par[h]
            S_old = Sb[h][so]
            S_new = Sb[h][1 - so]
            sb_par[h] = 1 - so
            (bS,) = yield from acquire(1)
            MM([(bS[:, 128:256], kdec[:, c, :], u_[:, cs(c)], True, False),
                (bS[:, 128:256], nK[:, cs(c)], S_old[:, :], False, True),
                (bS[:, 0:128], S_old[:, :], QAT[:, cs(c)], True, False),
                (bS[:, 0:128], u_[:, cs(c)], AT[:, cs(c)], False, True)],
               [S_old, QAT, u_, AT, kdec, nK], [bS])
            yield
            gam = sm[:, 12 + c:13 + c]
            STT("dve", S_new[:, :], Sf[h][:, :], gam, bS[:, 128:256], ALU.mult, ALU.add, [Sf[h], sm, bS], [S_new])
            STT("dve", Sf[h][:, :], Sf[h][:, :], gam, bS[:, 128:256], ALU.mult, ALU.add, [Sf[h], sm, bS], [Sf[h]])
            CP("act", oT[:, cs(c)], bS[:, 0:128], [bS], [oT])
            release(bS)
            yield
        ACT(V[0][:, :], oT[:, :], AF.Square, [oT], [V[0]])
        yield
        (b2,) = yield from acquire(1)
        MM([(b2[:, 0:512], ONESB, V[0][:, :], True, True)], [V[0], cstb2], [b2])
        yield
        ACT(F1[:, :], b2[:, 0:512], AF.Ln, [b2], [F1], bias=EPS, scale=1.0 / 128)
        release(b2)
        ACT(F1[:, :], F1[:, :], AF.Exp, [F1], [F1], scale=-0.5)
        STT("dve", ob[:, h, :], oT[:, :], ngP[:, 0:1], F1[:, :], ALU.mult, ALU.mult, [oT, F1, ngP], [ob])
        yield

    def run_il(gens, prio=1):
        gens = [g for g in gens if g is not None]
        first = gens[0] if gens else None
        while gens:
            for g in list(gens):
                for _ in range(prio if g is first else 1):
                    try:
                        next(g)
                    except StopIteration:
                        gens.remove(g)
                        break

    def load_wq(h):
        wq = qkvw[h % 2]
        DMA("sp", wq[:, :, :], wsc[h, :, :, 0:384], f"qw{h % 2}", [wsc_b[h]], [wq])
        return wq

    x_loaded = set()

    def load_x(tt):
        if tt in x_loaded or tt >= NT:
            return
        x_loaded.add(tt)
        for sub in range(4):
            r0 = tt * 512 + sub * 128
            DMA("sp", xt[sub][:, :], xw[r0:r0 + 128, :], f"xl{sub}", [], [xt[sub]])

    def tile_start(tt):
        tp = tt % 2
        betaP, gP, GP, s1P, gphl, bphl = betaPs[tp], gPs[tp], GPs[tp], s1Ps[tp], gphls[tp], bphls[tp]
        load_x(tt)
        for sub in range(4):
            (bk,) = yield from acquire(1)
            norm_to_T(xt[sub], gmixB, hT, sub, bk)
            release(bk)
            yield
        if tt < T_OWN0 - 1:
            load_x(tt + 1)
        MM([(ps_small[:, c * 16:(c + 1) * 16], hT[:, k, c * 128:(c + 1) * 128], wab[:, k, :], k == 0, k == 7)
            for c in range(4) for k in range(8)], [hT, wab], [ps_small])
        pa = ps_small[:, 0:64].rearrange("p (c x) -> p c x", c=4)
        vch = lambda t: t[:, :].rearrange("p (c h) -> p c h", c=4)
        vhc = lambda t: t[:, :].rearrange("p (h c) -> p c h", c=4)
        ACT(vhc(betaP), pa[:, :, 8:16], AF.Exp, [ps_small], [betaP], scale=-1.0)
        ACT(betaP[:, :], betaP[:, :], AF.Ln, [betaP], [betaP], bias=1.0)
        ACT(betaP[:, :], betaP[:, :], AF.Exp, [betaP], [betaP], scale=-1.0)
        TT("dve", vch(tmpP), pa[:, :, 0:8], vch(dtb), ALU.add, [ps_small, dtb], [tmpP])
        ACT(tmpP[:, :], tmpP[:, :], AF.Exp, [tmpP], [tmpP])
        ACT(tmpP[:, :], tmpP[:, :], AF.Ln, [tmpP], [tmpP], bias=1.0)
        TT("dve", vhc(gP), vch(tmpP), vch(negA), ALU.mult, [tmpP, negA], [gP])
        CP("pool", gphl[:, 0, :], gP[:, :], [gP], [gphl])
        TT("pool", tmpP[:, :], gP[:, :], gphl[:, 0, :], ALU.subtract, [gP, gphl], [tmpP])
        CP("pool", gphl[:, 1, :], tmpP[:, :], [tmpP], [gphl])
        CP("pool", bphl[:, 0, :], betaP[:, :], [betaP], [bphl])
        TT("pool", tmpP[:, :], betaP[:, :], bphl[:, 0, :], ALU.subtract, [betaP, bphl], [tmpP])
        CP("pool", bphl[:, 1, :], tmpP[:, :], [tmpP], [bphl])
        MM([(ps_small[:, 64:96], cstb2[:, 0, :], gphl[:, 0, :], True, False), (ps_small[:, 64:96], cstb2[:, 0, :], gphl[:, 1, :], False, True)], [cstb2, gphl], [ps_small])
        CP("act", GP[:, :], ps_small[:, 64:96], [ps_small], [GP])
        ACT(kegP[:, :], ps_small[:, 64:96], AF.Exp, [ps_small], [kegP])
        TT("dve", s1P[:, :], kegP[:, :], betaP[:, :], ALU.mult, [kegP, betaP], [s1P])
        yield

    def gdn_tiles(tts):
        tts = list(tts)
        gl = [(tt, h) for tt in tts for h in range(NH)]
        doneA = set()
        doneC = set()

        def chainA():
            for gi, (tt, h) in enumerate(gl):
                while gi >= NSETS and gl[gi - NSETS] not in doneC:
                    yield
                if h == 0:
                    while any((tt - 2, hh) in gl and (tt - 2, hh) not in doneC for hh in range(NH)):
                        yield
                    yield from tile_start(tt)
                yield from stageA(tt, h, load_wq(h), gi % NSETS)
                doneA.add((tt, h))

        def chainC(p, delay):
            for _ in range(delay):
                yield
            for gi, (tt, h) in enumerate(gl):
                if gi % NPIPE != p:
                    continue
                while (tt, h) not in doneA:
                    yield
                while gi >= NH and gl[gi - NH] not in doneC:
                    yield
                yield from stageC(tt, h, gi % NSETS, p)
                doneC.add((tt, h))

        run_il([chainA()] + [chainC(p, p * STAGGER) for p in range(NPIPE)], prio=A_PRIO)

    out_sem_total = [0]

    def phase2(tt, c0, c1, store):
        n = c1 - c0
        subs = list(range(c0 // 128, c1 // 128))
        seq = U_A + U_Z + U_G + U_O + U_UP + [uu for row in U_D for (uu, _) in row]
        wsm = WStream(seq)
        hs = hT[:, :, c0:c1]
        for ct in range(8):
            wu = wsm.get()
            psc = pslot()
            MM([(psc[:, 0:n], wu[:, k, 128:256], hT[:, k, c0:c1], k == 0, k == 7) for k in range(8)], [wu, hT], [psc])
            CP("act", cgs[:, 0:n], psc[:, 0:n], [psc], [cgs])
            psx = pslot()
            MM([(psx[:, 0:n], wu[:, k, 256:384], hT[:, k, c0:c1], k == 0, k == 7) for k in range(8)], [wu, hT], [psx])
            w_ = uw[ct % 2]
            TT("dve", w_[:, 2:2 + n], psx[:, 0:n], cgs[:, 0:n], ALU.mult, [psx, cgs], [w_])
            dwconv(w_, carA[:, ct, :], 3, lambda jj, ct=ct: cwA[:, ct, jj:jj + 1], n, cvg[:, 0:n], [cwA], [cvg], carA)
            psb = pslot()
            MM([(psb[:, 0:n], wu[:, k, 0:128], hT[:, k, c0:c1], k == 0, k == 7) for k in range(8)], [wu, hT], [psb])
            TT("dve", yap[:, ct, 0:n], psb[:, 0:n], cvg[:, 0:n], ALU.mult, [psb, cvg], [yap])
        for i2 in range(2):
            wu = wsm.get()
            for j in range(4):
                ct = i2 * 4 + j
                ps = pslot()
                MM([(ps[:, 0:n], wu[:, k, j * 128:(j + 1) * 128], hT[:, k, c0:c1], k == 0, k == 7) for k in range(8)], [wu, hT], [ps])
                ACT(sz[:, ct, 0:n], ps[:, 0:n], AF.Silu, [ps], [sz])
                TT("pool", sz[:, ct, 0:n], sz[:, ct, 0:n], ob[:, ct, c0:c1], ALU.mult, [sz, ob], [sz])
        for nt in range(8):
            wu = wsm.get()
            ps = pslot()
            MM([(ps[:, 0:n], wu[:, k, 0:128], hT[:, k, c0:c1], k == 0, k == 7) for k in range(8)], [wu, hT], [ps])
            ACT(sga[:, 0:n], ps[:, 0:n], AF.Sigmoid, [ps], [sga])
            ps = pslot()
            MM([(ps[:, 0:n], wu[:, k, 128:256], hT[:, k, c0:c1], k == 0, k == 7) for k in range(8)], [wu, hT], [ps])
            ACT(sgb[:, 0:n], ps[:, 0:n], AF.Sigmoid, [ps], [sgb])
            ps = pslot()
            MM([(ps[:, 0:n], wu[:, k, 256:384], yap[:, k, 0:n], k == 0, k == 7) for k in range(8)], [wu, yap], [ps])
            TT("dve", sga[:, 0:n], sga[:, 0:n], ps[:, 0:n], ALU.mult, [sga, ps], [sga])
            ps = pslot()
            MM([(ps[:, 0:n], wu[:, k, 384:512], sz[:, k, 0:n], k == 0, k == 7) for k in range(8)], [wu, sz], [ps])
            TT("dve", sgb[:, 0:n], sgb[:, 0:n], ps[:, 0:n], ALU.mult, [sgb, ps], [sgb])
            TT("pool", mixT[:, nt, 0:n], sga[:, 0:n], sgb[:, 0:n], ALU.add, [sga, sgb], [mixT])
        for nh in range(2):
            wu = wsm.get()
            for si, sub in enumerate(subs):
                ps = pslot()
                MM([(ps[:, :], mixT[:, k, si * 128:(si + 1) * 128], wu[:, k, :], k == 0, k == 7) for k in range(8)], [wu, mixT], [ps])
                TT("dve", xt[sub][:, nh * 512:(nh + 1) * 512], xt[sub][:, nh * 512:(nh + 1) * 512], ps[:, :], ALU.add, [xt[sub], ps], [xt[sub]])
        for si, sub in enumerate(subs):
            norm_to_T(xt[sub], gffnB, h2T, si)
        for fp in range(11):
            wu = wsm.get()
            for j in range(2):
                ft = fp * 2 + j
                psg = pslot()
                MM([(psg[:, 0:n], wu[:, k, j * 256:j * 256 + 128], h2T[:, k, 0:n], k == 0, k == 7) for k in range(8)], [wu, h2T], [psg])
                w0 = uw[0]
                CP("act", w0[:, 2:2 + n], psg[:, 0:n], [psg], [w0])
                dwconv(w0, carF[:, ft, :], 3, lambda jj, ft=ft: cwF[:, ft, jj:jj + 1], n, cvg[:, 0:n], [cwF], [cvg], carF)
                psv = pslot()
                MM([(psv[:, 0:n], wu[:, k, j * 256 + 128:j * 256 + 256], h2T[:, k, 0:n], k == 0, k == 7) for k in range(8)], [wu, h2T], [psv])
                w1 = uw[1]
                CP("act", w1[:, 2:2 + n], psv[:, 0:n], [psv], [w1])
                dwconv(w1, carF[:, 22 + ft, :], 3, lambda jj, ft=ft: cwF[:, 22 + ft, jj:jj + 1], n, cvv[:, 0:n], [cwF], [cvv], carF)
                ACT(cvg[:, 0:n], cvg[:, 0:n], AF.Silu, [cvg], [cvg])
                TT("dve", actT[:, ft, 0:n], cvg[:, 0:n], cvv[:, 0:n], ALU.mult, [cvg, cvv], [actT])
        for nh in range(2):
            wus = [(wsm.get(ahead=0), nk) for (_, nk) in U_D[nh]]
            if not store:
                continue
            for si, sub in enumerate(subs):
                ps = pslot()
                mms = []
                for kg, (wu, nk) in enumerate(wus):
                    for k in range(nk):
                        kk = kg * 8 + k
                        mms.append((ps[:, :], actT[:, kk, si * 128:(si + 1) * 128], wu[:, k, :], kk == 0, kk == 21))
                MM(mms, [actT] + [w for w, _ in wus], [ps])
                TT("dve", xt[sub][:, nh * 512:(nh + 1) * 512], xt[sub][:, nh * 512:(nh + 1) * 512], ps[:, :], ALU.add, [xt[sub], ps], [xt[sub]])
        if store:
            for si, sub in enumerate(subs):
                st = st_small[sidx[0] % 2]
                sidx[0] += 1
                yo = yout[0]
                ACT(junk[:, :], xt[sub][:, :], AF.Square, [xt[sub]], [junk, st], accum=st[:, 0:1])
                ACT(st[:, 1:2], st[:, 0:1], AF.Ln, [st], [st], bias=EPS, scale=1.0 / D)
                ACT(st[:, 2:3], st[:, 1:2], AF.Exp, [st], [st], scale=-0.5)
                STT("dve", yo[:, :], xt[sub][:, :], st[:, 2:3], gfinB[:, :], ALU.mult, ALU.mult, [xt[sub], st, gfinB], [yo])
                row0 = (tt - T_OWN0) * 512 + sub * 128
                DMA("sp", out_d[row0:row0 + 128, :], yo[:, :], "store0", [yo], [])

    def main_loop():
        ck(10)
        n_cont = max(T_OWN0 - 1, 0)
        if n_cont > 0:
            gdn_tiles(range(0, n_cont))
        for tt in range(n_cont, NT):
            gdn_tiles([tt])
            ck(70)
            if tt == T_OWN0 - 1:
                barrier()
                phase2(tt, 384, 512, False)
                barrier()
                ck(80)
            elif tt >= T_OWN0:
                barrier()
                phase2(tt, 0, 512, True)
                barrier()

    try:
        main_loop()
    except _Stop:
        pass

    fw = [(k_, v_) for k_, v_ in P.dsem.items() if k_.startswith("store")]
    if "dbg" in P.dsem:
        fw.append(("dbg", P.dsem["dbg"]))
    for k_ in P.dsem:
        if k_.startswith("xl") or k_.startswith("qw") or k_.startswith("ws") or k_ == "const":
            fw.append((k_, P.dsem[k_]))
    P.final_wait("sp", fw)
    P.final_wait("sp", [(k_, e_.count) for k_, e_ in P.E.items() if e_.count > 0])

    keys = list(P.E.keys()) + list(P.dsem.keys())
    sems = {k: es.enter_context(nc.semaphore("s_" + k)) for k in keys}
    with nc.Block() as block:
        def replay(name):
            def run(eng):
                for waits, fn, inc in P.E[name].ops:
                    for k, v in waits:
                        eng.wait_ge(sems[k], v)
                    if fn is None:
                        continue
                    if isinstance(fn, tuple):
                        ins = fn[0](eng)
                        ins.annotate(fn[1])
                    else:
                        ins = fn(eng)
                    ins.then_inc(sems[inc[0]], inc[1])
            return run
        block.tensor(replay("pe"))
        block.scalar(replay("act"))
        block.vector(replay("dve"))
        block.gpsimd(replay("pool"))
        block.sync(replay("sp"))
    es.close()
    return nc


def consts_np():
    i = np.arange(128)
    ident = np.eye(128, dtype=np.float32)
    uti = (i[:, None] <= i[None, :]).astype(np.float32)
    negsu = np.where(i[:, None] < i[None, :], 0.0, -30000.0).astype(np.float32)
    possl = np.where(i[:, None] > i[None, :], 0.0, 30000.0).astype(np.float32)
    ones = np.ones((128, 128), np.float32)
    return np.ascontiguousarray(np.stack([ident, uti, negsu, possl, ones], axis=1))


def make_in_maps(inputs, W, OWN, core_tokens):
    f = lambda a: np.ascontiguousarray(np.asarray(a, dtype=np.float32))
    x = f(inputs["x"])
    bc = lambda v: np.ascontiguousarray(np.broadcast_to(f(v).reshape(1, -1), (128, f(v).size)))
    cw = lambda w, nt: np.ascontiguousarray(f(w).reshape(w.shape[-2], nt, 128).transpose(2, 1, 0))
    shared = {
        "w_in": f(inputs["w_in"][0]), "w_a": f(inputs["w_a_out"][0]), "w_b": f(inputs["w_b_out"][0]),
        "w_o": f(inputs["w_o"][0]), "w_up": f(inputs["w_up"][0]), "w_dn": f(inputs["w_down"][0]),
        "gmixB": bc(inputs["norm_mix_g"][0]), "gffnB": bc(inputs["norm_ffn_g"][0]), "gfinB": bc(inputs["norm_final_g"]),
        "cwA": cw(np.asarray(inputs["conv_a_w"][0]), 8), "cwG": cw(np.asarray(inputs["gdn_conv_w"][0]), 24),
        "cwF": cw(np.asarray(inputs["ffn_conv_w"][0]), 44),
        "alogB": np.ascontiguousarray(np.tile(bc(inputs["gdn_A_log"][0]), (1, 4))),
        "dtbB": np.ascontiguousarray(np.tile(bc(inputs["gdn_dt_bias"][0]), (1, 4))),
        "ngP": f(inputs["gdn_norm_g"][0]).reshape(128, 1),
        "cst": consts_np(),
    }
    maps = []
    for (b, s) in core_tokens:
        end = s + OWN
        xwin = np.zeros((W, D), np.float32)
        lo = max(0, end - W)
        xwin[W - (end - lo):] = x[b, lo:end]
        m = dict(shared)
        m["xw"] = xwin
        maps.append(m)
    return maps


def kernel(**inputs):
    x = np.asarray(inputs["x"])
    Bsz, S, _ = x.shape
    core_tokens = [(c // 4, (c % 4) * OWN_FULL) for c in range(NCORES)]
    nc = build(SEQ, OWN_FULL)
    maps = make_in_maps(inputs, SEQ, OWN_FULL, core_tokens)
    res = run_bass_kernel_spmd(nc, maps, core_ids=list(range(NCORES)))
    out = np.zeros((Bsz, S, D), np.float32)
    for c, (b, s) in enumerate(core_tokens):
        out[b, s:s + OWN_FULL] = np.asarray(res.results[c]["out"])
    return out
```

```python
import numpy as np
from contextlib import ExitStack
import concourse.bass as bass
import concourse.mybir as mybir
from concourse.bass_utils import run_bass_kernel_spmd

F32 = mybir.dt.float32
BF16 = mybir.dt.bfloat16
AF = mybir.ActivationFunctionType
ALU = mybir.AluOpType

D = 1024
NH = 8
DFF = 2816
EPS = 1e-6
NCORES = 8
SEQ = 8192
OWN_FULL = 2048
SAME_SYNC = True
STAGGER = 11
A_PRIO = 6
DEBUG_TAGS = False

C_BG, C_CG, C_XV, C_Q, C_K, C_V, C_Z, C_A, C_B, C_GA, C_GB = 0, 1024, 2048, 3072, 4096, 5120, 6144, 7168, 7176, 7184, 8208


class Buf:
    __slots__ = ("name", "w", "rs", "psum")

    def __init__(self, name, psum=False):
        self.name = name
        self.w = None
        self.rs = []
        self.psum = psum


class Eng:
    def __init__(self, name):
        self.name = name
        self.ops = []
        self.count = 0
        self.known = {}


class Prog:
    def __init__(self):
        self.E = {n: Eng(n) for n in ("pe", "act", "dve", "pool", "sp")}
        self.dsem = {}

    def _deps(self, e, reads, writes):
        deps = {}

        def add(dep):
            if dep is None:
                return
            k, v = dep
            if deps.get(k, 0) < v:
                deps[k] = v

        for b in reads:
            add(b.w)
            if b.psum:
                for r in b.rs:
                    if r[0] != e.name:
                        add(r)
        for b in writes:
            add(b.w)
            for r in b.rs:
                add(r)
        waits = []
        for k, v in deps.items():
            if k == e.name and (k == "pe" or not SAME_SYNC):
                continue
            if e.known.get(k, 0) >= v:
                continue
            e.known[k] = v
            waits.append((k, v))
        return waits

    def op(self, eng, fn, reads=(), writes=()):
        e = self.E[eng]
        waits = self._deps(e, reads, writes)
        e.count += 1
        n = e.count
        if DEBUG_TAGS:
            import sys
            f = sys._getframe(1)
            tag = []
            for _ in range(4):
                if f is None:
                    break
                tag.append(str(f.f_lineno))
                f = f.f_back
            fn = (fn, "L" + "<".join(tag))
        e.ops.append((waits, fn, (eng, 1)))
        for b in writes:
            b.w = (eng, n)
            b.rs = []
        for b in reads:
            b.rs.append((eng, n))

    def dma(self, queue, fn, key, reads=(), writes=()):
        e = self.E[queue]
        waits = self._deps(e, reads, writes)
        self.dsem[key] = self.dsem.get(key, 0) + 16
        n = self.dsem[key]
        e.ops.append((waits, fn, (key, 16)))
        for b in writes:
            b.w = (key, n)
            b.rs = []
        for b in reads:
            b.rs.append((key, n))

    def final_wait(self, eng, deps):
        self.E[eng].ops.append((list(deps), None, None))


class _Stop(Exception):
    pass


def build(W=SEQ, OWN=OWN_FULL, dbg=None, stage=None, flags=()):
    def ck(n):
        if stage is not None and stage == n:
            raise _Stop()

    NT = W // 512
    T_OWN0 = (W - OWN) // 512
    NOWN = OWN // 512
    nc = bass.Bass("TRN2", target_bir_lowering=False)
    P = Prog()
    dbg = dbg or []
    dbg_out = {}

    def din(name, shape, dt=F32):
        return nc.dram_tensor(name, shape, dt, kind="ExternalInput").ap()

    xw = din("xw", [W, D])
    w_in = din("w_in", [D, 9232])
    w_a = din("w_a", [D, D])
    w_b = din("w_b", [D, D])
    w_o = din("w_o", [D, D])
    w_up = din("w_up", [D, 2 * DFF])
    w_dn = din("w_dn", [DFF, D])
    gmixB_d = din("gmixB", [128, D])
    gffnB_d = din("gffnB", [128, D])
    gfinB_d = din("gfinB", [128, D])
    cwA_d = din("cwA", [128, 8, 3])
    cwG_d = din("cwG", [128, 24, 4])
    cwF_d = din("cwF", [128, 44, 3])
    alog_d = din("alogB", [128, 32])
    dtb_d = din("dtbB", [128, 32])
    ng_d = din("ngP", [128, 1])
    cst_d = din("cst", [128, 5, 128])
    out_d = nc.dram_tensor("out", [OWN, D], F32, kind="ExternalOutput").ap()
    NUNIT2 = 37
    wsc = nc.dram_tensor("wsc", [8 + NUNIT2, 128, 8, 512], BF16, kind="Internal").ap()
    for name, shape in dbg:
        dbg_out[name] = nc.dram_tensor("dbg_" + name, shape, F32, kind="ExternalOutput").ap()

    es = ExitStack()
    SB_ACC = [0, []]
    build.sb_acc = SB_ACC

    class T:
        def __init__(self, name, shape, dt=F32):
            self.t = es.enter_context(nc.sbuf_tensor("sb_" + name, shape, dt))
            self.b = Buf(name)
            SB_ACC[0] += int(np.prod(shape[1:])) * (2 if dt == BF16 else 4)
            SB_ACC[1].append((name, int(np.prod(shape[1:])) * (2 if dt == BF16 else 4)))

        def __getitem__(self, k):
            return self.t[k]

    class PS:
        def __init__(self, name, shape, dt=F32):
            self.t = es.enter_context(nc.psum_tensor(name, shape, dt))
            self.b = Buf(name, psum=True)

        def __getitem__(self, k):
            return self.t[k]

    class View:
        def __init__(self, ap, name):
            self.ap = ap
            self.b = Buf(name)

        def __getitem__(self, k):
            return self.ap[k]

    arena = T("arena", [128, 11264], BF16)
    arena2 = T("arena2", [128, 12288], BF16)
    arenas = {1: (arena, 11264, [0]), 2: (arena2, 12288, [0])}

    def AV(name, shape, dt=F32, which=1):
        ar, cap, aoff = arenas[which]
        n = int(np.prod(shape[1:]))
        nb = n * (1 if dt == BF16 else 2)
        ap = ar.t[:, aoff[0]:aoff[0] + nb]
        aoff[0] += nb
        assert aoff[0] <= cap
        if dt != BF16:
            ap = ap.bitcast(F32)
        if len(shape) == 3:
            ap = ap.rearrange("p (a b) -> p a b", a=shape[1])
        return View(ap, name)

    def barrier():
        for en, e in P.E.items():
            waits = [(k_, e2.count) for k_, e2 in P.E.items() if k_ != en and e2.count > 0]
            waits += [(k_, v_) for k_, v_ in P.dsem.items() if not k_.startswith("wc")]
            e.ops.append((waits, None, None))
            for k_, v_ in waits:
                e.known[k_] = max(e.known.get(k_, 0), v_)

    def B(xs):
        return [x.b if hasattr(x, "b") else x for x in xs]

    def ACT(out, in_, func, r, w, bias=0.0, scale=1.0, accum=None):
        if accum is None:
            P.op("act", lambda e: e.activation(out=out, in_=in_, func=func, bias=bias, scale=scale), B(r), B(w))
        else:
            P.op("act", lambda e: e.activation(out=out, in_=in_, func=func, bias=bias, scale=scale, accum_out=accum), B(r), B(w))

    def TS(eng, out, in0, s1, s2, op0, op1, r, w):
        if op1 is None:
            P.op(eng, lambda e: e.tensor_scalar(out=out, in0=in0, scalar1=s1, scalar2=None, op0=op0), B(r), B(w))
        else:
            P.op(eng, lambda e: e.tensor_scalar(out=out, in0=in0, scalar1=s1, scalar2=s2, op0=op0, op1=op1), B(r), B(w))

    def STT(eng, out, in0, scalar, in1, op0, op1, r, w):
        P.op(eng, lambda e: e.scalar_tensor_tensor(out=out, in0=in0, scalar=scalar, in1=in1, op0=op0, op1=op1), B(r), B(w))

    def TT(eng, out, in0, in1, op, r, w):
        P.op(eng, lambda e: e.tensor_tensor(out=out, in0=in0, in1=in1, op=op), B(r), B(w))

    def CP(eng, out, in_, r, w):
        if eng == "act":
            P.op("act", lambda e: e.copy(out=out, in_=in_), B(r), B(w))
        else:
            P.op(eng, lambda e: e.tensor_copy(out=out, in_=in_), B(r), B(w))

    def MM(mms, r, w):
        def fn(e):
            ins = None
            for (o, l, rh, st, sp) in mms:
                ins = e.matmul(o, l, rh, start=st, stop=sp)
            return ins
        P.op("pe", fn, B(r), B(w))

    def TR(trs, r, w):
        def fn(e):
            ins = None
            for (o, i, idn) in trs:
                ins = e.transpose(o, i, idn)
            return ins
        P.op("pe", fn, B(r), B(w))

    def DMA(queue, out, in_, key, r, w):
        q = {"sp": "sp", "pool": "pool", "act": "act"}[queue]
        P.dma(q, lambda e: e.dma_start(out=out, in_=in_), key, B(r), B(w))

    cst = T("cst", [128, 5, 128])
    cstb = T("cstb", [128, 128], BF16)
    cstb2 = T("cstb2", [128, 2, 128], BF16)
    gphls = [T(f"gphl{i}", [128, 2, 32], BF16) for i in range(2)]
    gmixB = T("gmixB", [128, D])
    gffnB = T("gffnB", [128, D])
    gfinB = T("gfinB", [128, D])
    cwA = T("cwA", [128, 8, 3])
    cwG = T("cwG", [128, 24, 4])
    cwF = T("cwF", [128, 44, 3])
    negA = T("negA", [128, 32])
    dtb = T("dtb", [128, 32])
    ngP = T("ngP", [128, 1])
    wab = T("wab", [128, 8, 16], BF16)
    IDF = cst[:, 0, :]
    UTI = cst[:, 1, :]
    NEGSU = cst[:, 2, :]
    POSSL = cst[:, 3, :]
    ONES = cst[:, 4, :]
    IDB = cstb[:, :]

    xt = [T(f"xt{i}", [128, D]) for i in range(4)]
    xn = T("xn", [128, D], BF16)
    junk = T("junk", [128, D], BF16)
    st_small = [T(f"st{i}", [128, 4]) for i in range(2)]
    hT = T("hT", [128, 8, 512], BF16)
    h2T = hT
    wslot = [T(f"wslot{i}", [128, 8, 512], BF16) for i in range(3)]
    qkvw = [T(f"qkvw{i}", [128, 8, 384], BF16) for i in range(2)]
    pre = [T(f"pre{i}", [128, 515]) for i in range(2)]
    cacc = T("cacc", [128, 512])
    post = T("post", [128, 512])
    sqb = T("sqb", [128, 512])
    rr = T("rr", [128, 512])
    NSETS = 5
    qTs = [T(f"qT{i}", [128, 512], BF16) for i in range(NSETS)]
    kTs = [T(f"kT{i}", [128, 512], BF16) for i in range(NSETS)]
    vTs = [T(f"vT{i}", [128, 512], BF16) for i in range(NSETS)]
    oTs = [T(f"oT{i}", [128, 512]) for i in range(2)] + [AV("oT2", [128, 512], F32, 2)]
    oT = oTs[0]
    carG = T("carG", [128, 24, 3])
    carA = T("carA", [128, 8, 2])
    carF = T("carF", [128, 44, 2])
    betaPs = [T(f"betaP{i}", [128, 32]) for i in range(2)]
    gPs = [T(f"gP{i}", [128, 32]) for i in range(2)]
    GPs = [T(f"GP{i}", [128, 32]) for i in range(2)]
    kegP = T("kegP", [128, 32])
    s1Ps = [T(f"s1P{i}", [128, 32]) for i in range(2)]
    tmpP = T("tmpP", [128, 32])
    NPIPE = 3
    def PT(p, name, shape, dt=F32):
        return T(name, shape, dt) if p == 0 else AV(name, shape, dt, p)

    Fs = [[PT(p, f"F{p}_{j}", [128, 512]) for j in range(3)] for p in range(NPIPE)]
    gm0_ = T("gm0", [128, 4, 4, 128], BF16)
    gm = [gm0_ for p in range(NPIPE)]
    Vb = [[PT(p, f"V{p}_{j}", [128, 512], BF16) for j in range(2)] for p in range(NPIPE)]
    VTb = [[PT(p, f"VT{p}_{j}", [128, 512], BF16) for j in range(2)] for p in range(NPIPE)]
    Rb = [[PT(p, f"R{p}_{j}", [128, 512], BF16) for j in range(2)] for p in range(NPIPE)]
    ATb = [PT(p, f"AT{p}", [128, 512], BF16) for p in range(NPIPE)]
    QATb = [PT(p, f"QAT{p}", [128, 512], BF16) for p in range(NPIPE)]
    vbb = [PT(p, f"vb{p}", [128, 4, 128], BF16) for p in range(NPIPE)]
    kbgb = [PT(p, f"kbg{p}", [128, 4, 128], BF16) for p in range(NPIPE)]
    kdecb = [PT(p, f"kdec{p}", [128, 4, 128], BF16) for p in range(NPIPE)]
    ub = [PT(p, f"u{p}", [128, 512], BF16) for p in range(NPIPE)]
    wb = [PT(p, f"w{p}", [128, 512], BF16) for p in range(NPIPE)]
    nKWTb = [PT(p, f"nKWT{p}", [128, 512], BF16) for p in range(NPIPE)]
    kbTs = [PT(p, f"kbT{p}", [128, 512], BF16) for p in range(NPIPE)]
    sm4 = [T(f"sm4_{p}", [128, 16]) for p in range(NPIPE)]
    bphls = [T(f"bphl{i}", [128, 2, 32], BF16) for i in range(2)]
    Sf = [T(f"Sf{h}", [128, 128]) for h in range(NH)]
    Sb = [[T(f"Sb{h}_{j}", [128, 128], BF16) for j in range(2)] for h in range(NH)]
    sb_par = [0] * NH
    ob = T("ob", [128, 8, 512], BF16)
    yap = View(arena2.t[:, 0:4096].rearrange("p (f n) -> p f n", f=8), "yap")
    sz = View(arena2.t[:, 4096:8192].rearrange("p (f n) -> p f n", f=8), "sz")
    mixT = View(arena2.t[:, 8192:12288].rearrange("p (f n) -> p f n", f=8), "mixT")
    sga = rr
    sgb = oT
    cgs = cacc
    uw = pre
    actT = View(arena.t[:, :].rearrange("p (f n) -> p f n", f=22), "actT")
    cvg = post
    cvv = sqb
    yout = [T(f"yout{i}", [128, D]) for i in range(1)]

    class Slot:
        def __init__(self, ap, b):
            self.ap = ap
            self.b = b

        def __getitem__(self, k):
            return self.ap[k]

    ps_proj = [PS(f"ps_proj{i}", [128, 512]) for i in range(2)]
    ps_small = ps_proj[0]
    ps_gb = [Slot(ps_small[:, 128:384], ps_small.b) for i in range(2)]
    ps_mm_t = [PS(f"ps_mm{i}", [128, 512]) for i in range(6)]


    mm_slots = [Slot(ps_mm_t[i][:, :], ps_mm_t[i].b) for i in range(6)]
    bank_free = list(mm_slots)
    mm_i = [0]

    def mslot():
        s = mm_slots[mm_i[0] % 3]
        mm_i[0] += 1
        return s

    proj_i = [0]

    in_phase2 = [False]

    def pslot():
        banks = (ps_proj + mm_slots) if in_phase2[0] else ps_proj
        s = banks[proj_i[0] % len(banks)]
        proj_i[0] += 1
        return s

    cl = [(cst, cst_d), (gmixB, gmixB_d), (gffnB, gffnB_d), (gfinB, gfinB_d), (cwA, cwA_d), (cwG, cwG_d),
          (cwF, cwF_d), (negA, alog_d), (dtb, dtb_d), (ngP, ng_d)]
    for t, d_ in cl:
        DMA("sp", t[:], d_, "const", [], [t])
    for t, d_ in cl:
        t.b.w = ("const", P.dsem["const"])
    CP("dve", cstb[:, :], cst[:, 0, :], [cst], [cstb])
    CP("dve", cstb2[:, 0, :], cst[:, 1, :], [cst], [cstb2])
    CP("dve", cstb2[:, 1, :], cst[:, 4, :], [cst], [cstb2])
    ACT(negA[:, :], negA[:, :], AF.Exp, [negA], [negA])
    TS("dve", negA[:, :], negA[:, :], -1.0, None, ALU.mult, None, [negA], [negA])
    for t in (carG, carA, carF):
        P.op("pool", lambda e, t=t: e.memset(t[:], 0.0), [], [t.b])
    for h in range(NH):
        P.op("pool", lambda e, h=h: e.memset(Sf[h][:, :], 0.0), [], [Sf[h].b])
        P.op("pool", lambda e, h=h: e.memset(Sb[h][0][:, :], 0.0), [], [Sb[h][0].b])

    wsc_b = [Buf(f"wsc{u}") for u in range(8 + NUNIT2)]
    stopped = [False]

    uext = {}

    def cast_piece(u, col0, src, r0, nk, c0, width, key="wcast"):
        uext[u] = (nk, max(uext.get(u, (0, 0))[1], col0 + width))
        s = src[r0:r0 + nk * 128, c0:c0 + width].rearrange("(k p) n -> p k n", p=128)
        cast_dma(wsc[u, :, 0:nk, col0:col0 + width], s)

    cast_n = [0]
    dummy = T("dummy", [128, 4])

    def cast_dma(out, in_):
        if "nocast" in flags:
            return
        key = f"wc{cast_n[0] % 6}"
        cast_n[0] += 1
        if P.dsem.get(key, 0) > 0:
            P.E["pool"].ops.append(([(key, P.dsem[key])], None, None))
        DMA("pool", out, in_, key, [], [])

    def cast_barrier(bufs):
        fake = []
        for j in range(6):
            key = f"wc{j}"
            if P.dsem.get(key, 0) > 0:
                fb = Buf("fk")
                fb.w = (key, P.dsem[key])
                fake.append(fb)
        P.op("pool", lambda e: e.memset(dummy[:, :], 0.0), fake, [dummy.b])
        for b_ in bufs:
            b_.w = dummy.b.w

    cast_dma(wab[:, :, :], w_in[:, C_A:C_A + 16].rearrange("(k p) n -> p k n", p=128))
    for h in range(NH):
        for j, cb in enumerate((C_Q, C_K, C_V)):
            cast_piece(h, j * 128, w_in, 0, 8, cb + h * 128, 128, key="wcast0")
    cast_barrier([wab.b] + [wsc_b[h] for h in range(NH)])
    units = []
    u = 8
    U_A = []
    for ct in range(8):
        for j, cb in enumerate((C_BG, C_CG, C_XV)):
            cast_piece(u, j * 128, w_in, 0, 8, cb + ct * 128, 128)
        U_A.append(u); u += 1
    U_Z = []
    for i in range(2):
        cast_piece(u, 0, w_in, 0, 8, C_Z + i * 512, 512)
        U_Z.append(u); u += 1
    U_G = []
    for nt in range(8):
        cast_piece(u, 0, w_in, 0, 8, C_GA + nt * 128, 128)
        cast_piece(u, 128, w_in, 0, 8, C_GB + nt * 128, 128)
        cast_piece(u, 256, w_a, 0, 8, nt * 128, 128)
        cast_piece(u, 384, w_b, 0, 8, nt * 128, 128)
        U_G.append(u); u += 1
    U_O = []
    for i in range(2):
        cast_piece(u, 0, w_o, 0, 8, i * 512, 512)
        U_O.append(u); u += 1
    U_UP = []
    for fp in range(11):
        for j in range(2):
            ft = fp * 2 + j
            cast_piece(u, j * 256, w_up, 0, 8, ft * 128, 128)
            cast_piece(u, j * 256 + 128, w_up, 0, 8, DFF + ft * 128, 128)
        U_UP.append(u); u += 1
    U_D = []
    for nh in range(2):
        row = []
        for kg, (k0, nk) in enumerate(((0, 8), (8, 8), (16, 6))):
            cast_piece(u, 0, w_dn, k0 * 128, nk, nh * 512, 512)
            row.append((u, nk)); u += 1
        U_D.append(row)
    assert u == 8 + NUNIT2
    cast_barrier([wsc_b[uu] for uu in range(8, 8 + NUNIT2)])

    ws_state = {"q": [], "n": 0}

    def ws_issue(unit):
        i = ws_state["n"] % 3
        ws_state["n"] += 1
        nk, ncol = uext[unit]
        DMA("sp", wslot[i][:, 0:nk, 0:ncol], wsc[unit, :, 0:nk, 0:ncol], f"ws{i}", [wsc_b[unit]], [wslot[i]])
        return wslot[i]

    class WStream:
        def __init__(self, seq, ahead=2):
            self.seq = list(seq)
            self.loaded = []
            self.pos = 0
            self.ahead = ahead

        def get(self, ahead=None):
            ahead = self.ahead if ahead is None else ahead
            while len(self.loaded) < min(len(self.seq), self.pos + 1 + ahead):
                self.loaded.append(ws_issue(self.seq[len(self.loaded)]))
            s = self.loaded[self.pos]
            self.pos += 1
            return s

    sidx = [0]

    def norm_to_T(src_tile, gB, dstT, sub, bank=None):
        bank = bank if bank is not None else bank_free[0]
        ps_tr = Slot(bank[:, 0:512].bitcast(BF16), bank.b)
        st = st_small[sidx[0] % 2]
        sidx[0] += 1
        ACT(junk[:, :], src_tile[:, :], AF.Square, [src_tile], [junk, st], accum=st[:, 0:1])
        ACT(st[:, 1:2], st[:, 0:1], AF.Ln, [st], [st], bias=EPS, scale=1.0 / D)
        ACT(st[:, 2:3], st[:, 1:2], AF.Exp, [st], [st], scale=-0.5)
        STT("dve", xn[:, :], src_tile[:, :], st[:, 2:3], gB[:, :], ALU.mult, ALU.mult, [src_tile, st, gB], [xn])
        TR([(ps_tr[:, k * 128:(k + 1) * 128], xn[:, k * 128:(k + 1) * 128], IDB) for k in range(8)], [xn, cstb], [ps_tr])
        CP("act", dstT[:, :, sub * 128:(sub + 1) * 128], ps_tr[:, :].rearrange("p (k n) -> p k n", k=8), [ps_tr], [dstT])
        return st

    def dump(name, ap, r):
        if name in dbg_out:
            DMA("sp", dbg_out[name], ap, "dbg", r, [])

    def dwconv(work, carry_ap, K, wts, n, outp, r_w, w_out, car_t):
        CP("pool", work[:, 0:K - 1], carry_ap, [car_t], [work])
        P.op("act", lambda e: e.mul(out=outp, in_=work[:, 0:n], mul=wts(0)), B([work] + r_w), B(w_out))
        for j in range(1, K):
            STT("dve", outp, work[:, j:j + n], wts(j), outp, ALU.mult, ALU.add, [work] + r_w + w_out, w_out)
        CP("pool", carry_ap, work[:, n:n + K - 1], [work], [car_t])

    cb_i = [0]
    bc_i = lambda ap4: ap4.unsqueeze(2).broadcast_to([128, 4, 128])
    bc_c = lambda ap: ap.unsqueeze(1).broadcast_to([128, 4, 128])
    v4 = lambda ap: ap.rearrange("p (c i) -> p c i", c=4)
    UTIB = cstb2[:, 0, :]
    ONESB = cstb2[:, 1, :]

    def acquire(n):
        while len(bank_free) < n:
            yield
        return [bank_free.pop(0) for _ in range(n)]

    def release(*bs):
        for b_ in bs:
            bank_free.append(b_)

    def stageA(tt, h, wq, qs):
        qT, kT, vT = qTs[qs], kTs[qs], vTs[qs]
        cc = [cacc, post, sqb]
        for j in range(3):
            ps = pslot()
            MM([(ps[:, :], wq[:, k, j * 128:(j + 1) * 128], hT[:, k, :], k == 0, k == 7) for k in range(8)], [wq, hT], [ps])
            pw = pre[(cb_i[0]) % 2]
            cb_i[0] += 1
            CP("act", pw[:, 3:515], ps[:, :], [ps], [pw])
            ch = j * 8 + h
            dwconv(pw, carG[:, ch, :], 4, lambda jj, ch=ch: cwG[:, ch, jj:jj + 1], 512, cc[j][:, :], [cwG], [cc[j]], carG)
            yield
        ACT(cc[0][:, :], cc[0][:, :], AF.Silu, [cc[0]], [cc[0]])
        ACT(cc[1][:, :], cc[1][:, :], AF.Silu, [cc[1]], [cc[1]])
        ACT(vT[:, :], cc[2][:, :], AF.Silu, [cc[2]], [vT])
        yield
        for src, dst, scale in ((cc[0], qT, 128 ** -0.5), (cc[1], kT, 1.0)):
            ACT(junk[:, 0:512], src[:, :], AF.Square, [src], [junk])
            ps2 = pslot()
            MM([(ps2[:, :], ONESB, junk[:, 0:512], True, True)], [junk, cstb2], [ps2])
            ACT(rr[:, :], ps2[:, :], AF.Ln, [ps2], [rr], bias=EPS)
            ACT(rr[:, :], rr[:, :], AF.Exp, [rr], [rr], scale=-0.5)
            STT("dve", dst[:, :], src[:, :], scale, rr[:, :], ALU.mult, ALU.mult, [src, rr], [dst])
            yield

    def stageC(tt, h, qs, p=0):
        qT, kT, vT = qTs[qs], kTs[qs], vTs[qs]
        sl = slice(h * 4, h * 4 + 4)
        F0, F1, F2 = Fs[p]
        G, V, VT, R = gm[p], Vb[p], VTb[p], Rb[p]
        AT, QAT, vb, kbg, kdec, u_, w_, nK, sm = ATb[p], QATb[p], vbb[p], kbgb[p], kdecb[p], ub[p], wb[p], nKWTb[p], sm4[p]
        kbT, oT = kbTs[p], oTs[p]
        tp = tt % 2
        betaP, GP, s1P, gphl, bphl = betaPs[tp], GPs[tp], s1Ps[tp], gphls[tp], bphls[tp]
        f2 = lambda t3: t3.rearrange("p c i -> p (c i)")
        cs = lambda c: slice(c * 128, (c + 1) * 128)
        bG, bB = yield from acquire(2)
        TT("pool", G[:, 0], bc_c(UTIB), bc_i(gphl[:, 0, sl]), ALU.mult, [cstb2, gphl], [G])
        TT("pool", G[:, 1], bc_c(UTIB), bc_i(gphl[:, 1, sl]), ALU.mult, [cstb2, gphl], [G])
        TT("pool", G[:, 2], bc_c(IDB), bc_i(bphl[:, 0, sl]), ALU.mult, [cstb, bphl], [G])
        TT("pool", G[:, 3], bc_c(IDB), bc_i(bphl[:, 1, sl]), ALU.mult, [cstb, bphl], [G])
        MM([(bG[:, 0:512], ONESB, f2(G[:, 0]), True, False), (bG[:, 0:512], ONESB, f2(G[:, 1]), False, True)], [G, cstb2], [bG])
        MM([(bB[:, 0:512], ONESB, f2(G[:, 2]), True, False), (bB[:, 0:512], ONESB, f2(G[:, 3]), False, True)], [G, cstb2], [bB])
        yield
        TT("dve", v4(F0[:, :]), v4(bG[:, 0:512]), bc_i(GP[:, sl]), ALU.subtract, [bG, GP], [F0])
        ACT(F2[:, :], bG[:, 0:512], AF.Exp, [bG], [F2])
        CP("act", sm[:, 0:4], v4(bG[:, 0:512])[:, :, 127], [bG], [sm])
        TT("dve", kbT[:, :], kT[:, :], bB[:, 0:512], ALU.mult, [kT, bB], [kbT])
        release(bG, bB)
        yield
        TT("dve", v4(F1[:, :]), v4(F0[:, :]), bc_c(POSSL), ALU.add, [F0, cst], [F1])
        TT("pool", v4(F0[:, :]), v4(F0[:, :]), bc_c(NEGSU), ALU.add, [F0, cst], [F0])
        TT("dve", sm[:, 4:8], sm[:, 0:4], GP[:, sl], ALU.subtract, [sm, GP], [sm])
        yield
        ACT(F1[:, :], F1[:, :], AF.Exp, [F1], [F1], scale=-1.0)
        ACT(F0[:, :], F0[:, :], AF.Exp, [F0], [F0])
        ACT(sm[:, 8:12], sm[:, 4:8], AF.Exp, [sm], [sm])
        ACT(sm[:, 12:16], sm[:, 0:4], AF.Exp, [sm], [sm])
        TT("pool", F2[:, :], qT[:, :], F2[:, :], ALU.mult, [qT, F2], [F2])
        bL, bU, bA = yield from acquire(3)
        MM([(bL[:, cs(c)], kbT[:, cs(c)], kT[:, cs(c)], True, True) for c in range(4)], [kbT, kT], [bL])
        MM([(bU[:, cs(c)], kT[:, cs(c)], kbT[:, cs(c)], True, True) for c in range(4)], [kbT, kT], [bU])
        MM([(bA[:, cs(c)], kT[:, cs(c)], qT[:, cs(c)], True, True) for c in range(4)], [qT, kT], [bA])
        yield
        STT("dve", VT[0][:, :], bL[:, 0:512], -1.0, F1[:, :], ALU.mult, ALU.mult, [bL, F1], [VT[0]])
        STT("dve", V[0][:, :], bU[:, 0:512], -1.0, F0[:, :], ALU.mult, ALU.mult, [bU, F0], [V[0]])
        TT("pool", v4(F0[:, :]), v4(F0[:, :]), bc_c(IDF), ALU.add, [F0, cst], [F0])
        TT("dve", AT[:, :], bA[:, 0:512], F0[:, :], ALU.mult, [bA, F0], [AT])
        TT("pool", v4(R[0][:, :]), v4(V[0][:, :]), bc_c(IDB), ALU.add, [V[0], cstb], [R[0]])
        release(bL, bU, bA)
        (bX,) = yield from acquire(1)
        bXb = bX[:, 0:512].bitcast(BF16)
        TR([(bXb[:, cs(c)], kT[:, cs(c)], IDB) for c in range(4)] +
           [(bXb[:, 512 + c * 128:512 + (c + 1) * 128], vT[:, cs(c)], IDB) for c in range(4)], [kT, vT, cstb], [bX])
        bT, bV = yield from acquire(2)
        MM([(bT[:, cs(c)], V[0][:, cs(c)], VT[0][:, cs(c)], True, True) for c in range(4)], [V[0], VT[0]], [bT])
        MM([(bV[:, cs(c)], VT[0][:, cs(c)], V[0][:, cs(c)], True, True) for c in range(4)], [V[0], VT[0]], [bV])
        yield
        TT("dve", kbg[:, :, :], v4(bXb[:, 0:512]), bc_i(s1P[:, sl]), ALU.mult, [bX, s1P], [kbg])
        TT("dve", kdec[:, :, :], v4(bXb[:, 0:512]), bc_i(sm[:, 8:12]), ALU.mult, [bX, sm], [kdec])
        TT("dve", vb[:, :, :], v4(bXb[:, 512:1024]), bc_i(betaP[:, sl]), ALU.mult, [bX, betaP], [vb])
        release(bX)
        cur = 0
        for m in range(1, 7):
            nxt = 1 - cur
            CP("act", VT[nxt][:, :], bT[:, 0:512], [bT], [VT[nxt]])
            release(bT)
            if m < 6:
                CP("act", V[nxt][:, :], bV[:, 0:512], [bV], [V[nxt]])
                release(bV)
            if m > 1:
                CP("act", R[cur][:, :], bR[:, 0:512], [bR], [R[cur]])
                release(bR)
            yield
            (bR,) = yield from acquire(1)
            mmr = []
            for c in range(4):
                mmr.append((bR[:, cs(c)], IDB, R[cur][:, cs(c)], True, False))
                mmr.append((bR[:, cs(c)], VT[nxt][:, cs(c)], R[cur][:, cs(c)], False, True))
            MM(mmr, [cstb, R[cur], VT[nxt]], [bR])
            if m < 5:
                bT, bV = yield from acquire(2)
            elif m == 5:
                (bT,) = yield from acquire(1)
            if m < 6:
                MM([(bT[:, cs(c)], V[nxt][:, cs(c)], VT[nxt][:, cs(c)], True, True) for c in range(4)], [V[nxt], VT[nxt]], [bT])
            if m < 5:
                MM([(bV[:, cs(c)], VT[nxt][:, cs(c)], V[nxt][:, cs(c)], True, True) for c in range(4)], [V[nxt], VT[nxt]], [bV])
            cur = nxt
            yield
        CP("act", R[cur][:, :], bR[:, 0:512], [bR], [R[cur]])
        release(bR)
        Rf = R[cur]
        yield
        b0, b1 = yield from acquire(2)
        MM([(b0[:, cs(c)], Rf[:, cs(c)], vb[:, c, :], True, True) for c in range(4)], [Rf, vb], [b0])
        MM([(b1[:, cs(c)], Rf[:, cs(c)], kbg[:, c, :], True, True) for c in range(4)], [Rf, kbg], [b1])
        yield
        CP("act", u_[:, :], b0[:, 0:512], [b0], [u_])
        CP("act", w_[:, :], b1[:, 0:512], [b1], [w_])
        release(b0, b1)
        yield
        bK, bQ = yield from acquire(2)
        MM([(bK[:, cs(c)], w_[:, cs(c)], kdec[:, c, :], True, True) for c in range(4)], [w_, kdec], [bK])
        MM([(bQ[:, cs(c)], w_[:, cs(c)], AT[:, cs(c)], True, True) for c in range(4)], [w_, AT], [bQ])
        yield
        P.op("act", lambda e: e.mul(out=nK[:, :], in_=bK[:, 0:512], mul=-1.0), B([bK]), B([nK]))
        TT("dve", QAT[:, :], F2[:, :], bQ[:, 0:512], ALU.subtract, [F2, bQ], [QAT])
        release(bK, bQ)
        yield
        for c in range(4):
            so = sb_par[h]
            S_old = Sb[h][so]
            S_new = Sb[h][1 - so]
            sb_par[h] = 1 - so
            (bS,) = yield from acquire(1)
            MM([(bS[:, 128:256], kdec[:, c, :], u_[:, cs(c)], True, False),
                (bS[:, 128:256], nK[:, cs(c)], S_old[:, :], False, True),
                (bS[:, 0:128], S_old[:, :], QAT[:, cs(c)], True, False),
                (bS[:, 0:128], u_[:, cs(c)], AT[:, cs(c)], False, True)],
               [S_old, QAT, u_, AT, kdec, nK], [bS])
            yield
            gam = sm[:, 12 + c:13 + c]
            STT("dve", S_new[:, :], Sf[h][:, :], gam, bS[:, 128:256], ALU.mult, ALU.add, [Sf[h], sm, bS], [S_new])
            STT("dve", Sf[h][:, :], Sf[h][:, :], gam, bS[:, 128:256], ALU.mult, ALU.add, [Sf[h], sm, bS], [Sf[h]])
            CP("act", oT[:, cs(c)], bS[:, 0:128], [bS], [oT])
            release(bS)
            yield
        ACT(V[0][:, :], oT[:, :], AF.Square, [oT], [V[0]])
        yield
        (b2,) = yield from acquire(1)
        MM([(b2[:, 0:512], ONESB, V[0][:, :], True, True)], [V[0], cstb2], [b2])
        yield
        ACT(F1[:, :], b2[:, 0:512], AF.Ln, [b2], [F1], bias=EPS, scale=1.0 / 128)
        release(b2)
        ACT(F1[:, :], F1[:, :], AF.Exp, [F1], [F1], scale=-0.5)
        STT("dve", ob[:, h, :], oT[:, :], ngP[:, 0:1], F1[:, :], ALU.mult, ALU.mult, [oT, F1, ngP], [ob])
        yield

    def run_il(gens, prio=1):
        gens = [g for g in gens if g is not None]
        first = gens[0] if gens else None
        while gens:
            for g in list(gens):
                for _ in range(prio if g is first else 1):
                    try:
                        next(g)
                    except StopIteration:
                        gens.remove(g)
                        break

    def load_wq(h):
        wq = qkvw[h % 2]
        DMA("sp", wq[:, :, :], wsc[h, :, :, 0:384], f"qw{h % 2}", [wsc_b[h]], [wq])
        return wq

    x_loaded = set()

    def load_x(tt):
        if tt in x_loaded or tt >= NT:
            return
        x_loaded.add(tt)
        for sub in range(4):
            r0 = tt * 512 + sub * 128
            DMA("sp", xt[sub][:, :], xw[r0:r0 + 128, :], f"xl{sub}", [], [xt[sub]])

    def tile_start(tt):
        tp = tt % 2
        betaP, gP, GP, s1P, gphl, bphl = betaPs[tp], gPs[tp], GPs[tp], s1Ps[tp], gphls[tp], bphls[tp]
        load_x(tt)
        for sub in range(4):
            (bk,) = yield from acquire(1)
            norm_to_T(xt[sub], gmixB, hT, sub, bk)
            release(bk)
            yield
        if tt < T_OWN0 - 1:
            load_x(tt + 1)
        MM([(ps_small[:, c * 16:(c + 1) * 16], hT[:, k, c * 128:(c + 1) * 128], wab[:, k, :], k == 0, k == 7)
            for c in range(4) for k in range(8)], [hT, wab], [ps_small])
        pa = ps_small[:, 0:64].rearrange("p (c x) -> p c x", c=4)
        vch = lambda t: t[:, :].rearrange("p (c h) -> p c h", c=4)
        vhc = lambda t: t[:, :].rearrange("p (h c) -> p c h", c=4)
        ACT(vhc(betaP), pa[:, :, 8:16], AF.Exp, [ps_small], [betaP], scale=-1.0)
        ACT(betaP[:, :], betaP[:, :], AF.Ln, [betaP], [betaP], bias=1.0)
        ACT(betaP[:, :], betaP[:, :], AF.Exp, [betaP], [betaP], scale=-1.0)
        TT("dve", vch(tmpP), pa[:, :, 0:8], vch(dtb), ALU.add, [ps_small, dtb], [tmpP])
        ACT(tmpP[:, :], tmpP[:, :], AF.Exp, [tmpP], [tmpP])
        ACT(tmpP[:, :], tmpP[:, :], AF.Ln, [tmpP], [tmpP], bias=1.0)
        TT("dve", vhc(gP), vch(tmpP), vch(negA), ALU.mult, [tmpP, negA], [gP])
        CP("pool", gphl[:, 0, :], gP[:, :], [gP], [gphl])
        TT("pool", tmpP[:, :], gP[:, :], gphl[:, 0, :], ALU.subtract, [gP, gphl], [tmpP])
        CP("pool", gphl[:, 1, :], tmpP[:, :], [tmpP], [gphl])
        CP("pool", bphl[:, 0, :], betaP[:, :], [betaP], [bphl])
        TT("pool", tmpP[:, :], betaP[:, :], bphl[:, 0, :], ALU.subtract, [betaP, bphl], [tmpP])
        CP("pool", bphl[:, 1, :], tmpP[:, :], [tmpP], [bphl])
        MM([(ps_small[:, 64:96], cstb2[:, 0, :], gphl[:, 0, :], True, False), (ps_small[:, 64:96], cstb2[:, 0, :], gphl[:, 1, :], False, True)], [cstb2, gphl], [ps_small])
        CP("act", GP[:, :], ps_small[:, 64:96], [ps_small], [GP])
        ACT(kegP[:, :], ps_small[:, 64:96], AF.Exp, [ps_small], [kegP])
        TT("dve", s1P[:, :], kegP[:, :], betaP[:, :], ALU.mult, [kegP, betaP], [s1P])
        yield

    def gdn_tiles(tts):
        tts = list(tts)
        gl = [(tt, h) for tt in tts for h in range(NH)]
        doneA = set()
        doneC = set()

        def chainA():
            for gi, (tt, h) in enumerate(gl):
                while gi >= NSETS and gl[gi - NSETS] not in doneC:
                    yield
                if h == 0:
                    while any((tt - 2, hh) in gl and (tt - 2, hh) not in doneC for hh in range(NH)):
                        yield
                    yield from tile_start(tt)
                yield from stageA(tt, h, load_wq(h), gi % NSETS)
                doneA.add((tt, h))

        def chainC(p, delay):
            for _ in range(delay):
                yield
            for gi, (tt, h) in enumerate(gl):
                if gi % NPIPE != p:
                    continue
                while (tt, h) not in doneA:
                    yield
                while gi >= NH and gl[gi - NH] not in doneC:
                    yield
                yield from stageC(tt, h, gi % NSETS, p)
                doneC.add((tt, h))

        run_il([chainA()] + [chainC(p, p * STAGGER) for p in range(NPIPE)], prio=A_PRIO)

    out_sem_total = [0]

    def phase2(tt, c0, c1, store):
        in_phase2[0] = True
        try:
            _phase2(tt, c0, c1, store)
        finally:
            in_phase2[0] = False

    def _phase2(tt, c0, c1, store):
        n = c1 - c0
        subs = list(range(c0 // 128, c1 // 128))
        seq = U_A + U_Z + U_G + U_O + U_UP + [uu for row in U_D for (uu, _) in row]
        wsm = WStream(seq)
        hs = hT[:, :, c0:c1]
        for ct in range(8):
            wu = wsm.get()
            psc = pslot()
            MM([(psc[:, 0:n], wu[:, k, 128:256], hT[:, k, c0:c1], k == 0, k == 7) for k in range(8)], [wu, hT], [psc])
            CP("act", cgs[:, 0:n], psc[:, 0:n], [psc], [cgs])
            psx = pslot()
            MM([(psx[:, 0:n], wu[:, k, 256:384], hT[:, k, c0:c1], k == 0, k == 7) for k in range(8)], [wu, hT], [psx])
            w_ = uw[ct % 2]
            TT("dve", w_[:, 2:2 + n], psx[:, 0:n], cgs[:, 0:n], ALU.mult, [psx, cgs], [w_])
            dwconv(w_, carA[:, ct, :], 3, lambda jj, ct=ct: cwA[:, ct, jj:jj + 1], n, cvg[:, 0:n], [cwA], [cvg], carA)
            psb = pslot()
            MM([(psb[:, 0:n], wu[:, k, 0:128], hT[:, k, c0:c1], k == 0, k == 7) for k in range(8)], [wu, hT], [psb])
            TT("dve", yap[:, ct, 0:n], psb[:, 0:n], cvg[:, 0:n], ALU.mult, [psb, cvg], [yap])
        for i2 in range(2):
            wu = wsm.get()
            for j in range(4):
                ct = i2 * 4 + j
                ps = pslot()
                MM([(ps[:, 0:n], wu[:, k, j * 128:(j + 1) * 128], hT[:, k, c0:c1], k == 0, k == 7) for k in range(8)], [wu, hT], [ps])
                ACT(sz[:, ct, 0:n], ps[:, 0:n], AF.Silu, [ps], [sz])
                TT("pool", sz[:, ct, 0:n], sz[:, ct, 0:n], ob[:, ct, c0:c1], ALU.mult, [sz, ob], [sz])
        for nt in range(8):
            wu = wsm.get()
            ps = pslot()
            MM([(ps[:, 0:n], wu[:, k, 0:128], hT[:, k, c0:c1], k == 0, k == 7) for k in range(8)], [wu, hT], [ps])
            ACT(sga[:, 0:n], ps[:, 0:n], AF.Sigmoid, [ps], [sga])
            ps = pslot()
            MM([(ps[:, 0:n], wu[:, k, 128:256], hT[:, k, c0:c1], k == 0, k == 7) for k in range(8)], [wu, hT], [ps])
            ACT(sgb[:, 0:n], ps[:, 0:n], AF.Sigmoid, [ps], [sgb])
            ps = pslot()
            MM([(ps[:, 0:n], wu[:, k, 256:384], yap[:, k, 0:n], k == 0, k == 7) for k in range(8)], [wu, yap], [ps])
            TT("dve", sga[:, 0:n], sga[:, 0:n], ps[:, 0:n], ALU.mult, [sga, ps], [sga])
            ps = pslot()
            MM([(ps[:, 0:n], wu[:, k, 384:512], sz[:, k, 0:n], k == 0, k == 7) for k in range(8)], [wu, sz], [ps])
            TT("dve", sgb[:, 0:n], sgb[:, 0:n], ps[:, 0:n], ALU.mult, [sgb, ps], [sgb])
            TT("pool", mixT[:, nt, 0:n], sga[:, 0:n], sgb[:, 0:n], ALU.add, [sga, sgb], [mixT])
        for nh in range(2):
            wu = wsm.get()
            for si, sub in enumerate(subs):
                ps = pslot()
                MM([(ps[:, :], mixT[:, k, si * 128:(si + 1) * 128], wu[:, k, :], k == 0, k == 7) for k in range(8)], [wu, mixT], [ps])
                TT("dve", xt[sub][:, nh * 512:(nh + 1) * 512], xt[sub][:, nh * 512:(nh + 1) * 512], ps[:, :], ALU.add, [xt[sub], ps], [xt[sub]])
        for si, sub in enumerate(subs):
            norm_to_T(xt[sub], gffnB, h2T, si)
        for fp in range(11):
            wu = wsm.get()
            for j in range(2):
                ft = fp * 2 + j
                psg = pslot()
                MM([(psg[:, 0:n], wu[:, k, j * 256:j * 256 + 128], h2T[:, k, 0:n], k == 0, k == 7) for k in range(8)], [wu, h2T], [psg])
                w0 = uw[0]
                CP("act", w0[:, 2:2 + n], psg[:, 0:n], [psg], [w0])
                dwconv(w0, carF[:, ft, :], 3, lambda jj, ft=ft: cwF[:, ft, jj:jj + 1], n, cvg[:, 0:n], [cwF], [cvg], carF)
                psv = pslot()
                MM([(psv[:, 0:n], wu[:, k, j * 256 + 128:j * 256 + 256], h2T[:, k, 0:n], k == 0, k == 7) for k in range(8)], [wu, h2T], [psv])
                w1 = uw[1]
                CP("act", w1[:, 2:2 + n], psv[:, 0:n], [psv], [w1])
                dwconv(w1, carF[:, 22 + ft, :], 3, lambda jj, ft=ft: cwF[:, 22 + ft, jj:jj + 1], n, cvv[:, 0:n], [cwF], [cvv], carF)
                ACT(cvg[:, 0:n], cvg[:, 0:n], AF.Silu, [cvg], [cvg])
                TT("dve", actT[:, ft, 0:n], cvg[:, 0:n], cvv[:, 0:n], ALU.mult, [cvg, cvv], [actT])
        for nh in range(2):
            wus = [(wsm.get(ahead=0), nk) for (_, nk) in U_D[nh]]
            if not store:
                continue
            for si, sub in enumerate(subs):
                ps = pslot()
                mms = []
                for kg, (wu, nk) in enumerate(wus):
                    for k in range(nk):
                        kk = kg * 8 + k
                        mms.append((ps[:, :], actT[:, kk, si * 128:(si + 1) * 128], wu[:, k, :], kk == 0, kk == 21))
                MM(mms, [actT] + [w for w, _ in wus], [ps])
                TT("dve", xt[sub][:, nh * 512:(nh + 1) * 512], xt[sub][:, nh * 512:(nh + 1) * 512], ps[:, :], ALU.add, [xt[sub], ps], [xt[sub]])
        if store:
            for si, sub in enumerate(subs):
                st = st_small[sidx[0] % 2]
                sidx[0] += 1
                yo = yout[0]
                ACT(junk[:, :], xt[sub][:, :], AF.Square, [xt[sub]], [junk, st], accum=st[:, 0:1])
                ACT(st[:, 1:2], st[:, 0:1], AF.Ln, [st], [st], bias=EPS, scale=1.0 / D)
                ACT(st[:, 2:3], st[:, 1:2], AF.Exp, [st], [st], scale=-0.5)
                STT("dve", yo[:, :], xt[sub][:, :], st[:, 2:3], gfinB[:, :], ALU.mult, ALU.mult, [xt[sub], st, gfinB], [yo])
                row0 = (tt - T_OWN0) * 512 + sub * 128
                DMA("sp", out_d[row0:row0 + 128, :], yo[:, :], "store0", [yo], [])

    def main_loop():
        ck(10)
        n_cont = max(T_OWN0 - 1, 0)
        if n_cont > 0:
            gdn_tiles(range(0, n_cont))
        for tt in range(n_cont, NT):
            gdn_tiles([tt])
            ck(70)
            if tt == T_OWN0 - 1:
                barrier()
                phase2(tt, 384, 512, False)
                barrier()
                ck(80)
            elif tt >= T_OWN0:
                barrier()
                phase2(tt, 0, 512, True)
                barrier()

    try:
        main_loop()
    except _Stop:
        pass

    fw = [(k_, v_) for k_, v_ in P.dsem.items() if k_.startswith("store")]
    if "dbg" in P.dsem:
        fw.append(("dbg", P.dsem["dbg"]))
    for k_ in P.dsem:
        if k_.startswith("xl") or k_.startswith("qw") or k_.startswith("ws") or k_ == "const":
            fw.append((k_, P.dsem[k_]))
    P.final_wait("sp", fw)
    P.final_wait("sp", [(k_, e_.count) for k_, e_ in P.E.items() if e_.count > 0])

    keys = list(P.E.keys()) + list(P.dsem.keys())
    sems = {k: es.enter_context(nc.semaphore("s_" + k)) for k in keys}
    with nc.Block() as block:
        def replay(name):
            def run(eng):
                for waits, fn, inc in P.E[name].ops:
                    for k, v in waits:
                        eng.wait_ge(sems[k], v)
                    if fn is None:
                        continue
                    if isinstance(fn, tuple):
                        ins = fn[0](eng)
                        ins.annotate(fn[1])
                    else:
                        ins = fn(eng)
                    ins.then_inc(sems[inc[0]], inc[1])
            return run
        block.tensor(replay("pe"))
        block.scalar(replay("act"))
        block.vector(replay("dve"))
        block.gpsimd(replay("pool"))
        block.sync(replay("sp"))
    es.close()
    return nc


def consts_np():
    i = np.arange(128)
    ident = np.eye(128, dtype=np.float32)
    uti = (i[:, None] <= i[None, :]).astype(np.float32)
    negsu = np.where(i[:, None] < i[None, :], 0.0, -30000.0).astype(np.float32)
    possl = np.where(i[:, None] > i[None, :], 0.0, 30000.0).astype(np.float32)
    ones = np.ones((128, 128), np.float32)
    return np.ascontiguousarray(np.stack([ident, uti, negsu, possl, ones], axis=1))


def make_in_maps(inputs, W, OWN, core_tokens):
    f = lambda a: np.ascontiguousarray(np.asarray(a, dtype=np.float32))
    x = f(inputs["x"])
    bc = lambda v: np.ascontiguousarray(np.broadcast_to(f(v).reshape(1, -1), (128, f(v).size)))
    cw = lambda w, nt: np.ascontiguousarray(f(w).reshape(w.shape[-2], nt, 128).transpose(2, 1, 0))
    shared = {
        "w_in": f(inputs["w_in"][0]), "w_a": f(inputs["w_a_out"][0]), "w_b": f(inputs["w_b_out"][0]),
        "w_o": f(inputs["w_o"][0]), "w_up": f(inputs["w_up"][0]), "w_dn": f(inputs["w_down"][0]),
        "gmixB": bc(inputs["norm_mix_g"][0]), "gffnB": bc(inputs["norm_ffn_g"][0]), "gfinB": bc(inputs["norm_final_g"]),
        "cwA": cw(np.asarray(inputs["conv_a_w"][0]), 8), "cwG": cw(np.asarray(inputs["gdn_conv_w"][0]), 24),
        "cwF": cw(np.asarray(inputs["ffn_conv_w"][0]), 44),
        "alogB": np.ascontiguousarray(np.tile(bc(inputs["gdn_A_log"][0]), (1, 4))),
        "dtbB": np.ascontiguousarray(np.tile(bc(inputs["gdn_dt_bias"][0]), (1, 4))),
        "ngP": f(inputs["gdn_norm_g"][0]).reshape(128, 1),
        "cst": consts_np(),
    }
    maps = []
    for (b, s) in core_tokens:
        end = s + OWN
        xwin = np.zeros((W, D), np.float32)
        lo = max(0, end - W)
        xwin[W - (end - lo):] = x[b, lo:end]
        m = dict(shared)
        m["xw"] = xwin
        maps.append(m)
    return maps


def kernel(**inputs):
    x = np.asarray(inputs["x"])
    Bsz, S, _ = x.shape
    core_tokens = [(c // 4, (c % 4) * OWN_FULL) for c in range(NCORES)]
    nc = build(SEQ, OWN_FULL)
    maps = make_in_maps(inputs, SEQ, OWN_FULL, core_tokens)
    res = run_bass_kernel_spmd(nc, maps, core_ids=list(range(NCORES)))
    out = np.zeros((Bsz, S, D), np.float32)
    for c, (b, s) in enumerate(core_tokens):
        out[b, s:s + OWN_FULL] = np.asarray(res.results[c]["out"])
    return out
```

```python
import numpy as np
from contextlib import ExitStack
import concourse.bass as bass
import concourse.mybir as mybir
from concourse.bass_utils import run_bass_kernel_spmd

F32 = mybir.dt.float32
BF16 = mybir.dt.bfloat16
AF = mybir.ActivationFunctionType
ALU = mybir.AluOpType

D = 1024
NH = 8
DFF = 2816
EPS = 1e-6
NCORES = 8
SEQ = 8192
OWN_FULL = 2048
SAME_SYNC = True
STAGGER = 8
A_PRIO = 12
DEBUG_TAGS = False

C_BG, C_CG, C_XV, C_Q, C_K, C_V, C_Z, C_A, C_B, C_GA, C_GB = 0, 1024, 2048, 3072, 4096, 5120, 6144, 7168, 7176, 7184, 8208


class Buf:
    __slots__ = ("name", "w", "rs", "psum")

    def __init__(self, name, psum=False):
        self.name = name
        self.w = None
        self.rs = []
        self.psum = psum


class Eng:
    def __init__(self, name):
        self.name = name
        self.ops = []
        self.count = 0
        self.known = {}


class Prog:
    def __init__(self):
        self.E = {n: Eng(n) for n in ("pe", "act", "dve", "pool", "sp")}
        self.dsem = {}

    def _deps(self, e, reads, writes):
        deps = {}

        def add(dep):
            if dep is None:
                return
            k, v = dep
            if deps.get(k, 0) < v:
                deps[k] = v

        for b in reads:
            add(b.w)
            if b.psum:
                for r in b.rs:
                    if r[0] != e.name:
                        add(r)
        for b in writes:
            add(b.w)
            for r in b.rs:
                add(r)
        waits = []
        for k, v in deps.items():
            if k == e.name and (k == "pe" or not SAME_SYNC):
                continue
            if e.known.get(k, 0) >= v:
                continue
            e.known[k] = v
            waits.append((k, v))
        return waits

    def op(self, eng, fn, reads=(), writes=()):
        e = self.E[eng]
        waits = self._deps(e, reads, writes)
        e.count += 1
        n = e.count
        if DEBUG_TAGS:
            import sys
            f = sys._getframe(1)
            tag = []
            for _ in range(4):
                if f is None:
                    break
                tag.append(str(f.f_lineno))
                f = f.f_back
            fn = (fn, "L" + "<".join(tag))
        e.ops.append((waits, fn, (eng, 1)))
        for b in writes:
            b.w = (eng, n)
            b.rs = []
        for b in reads:
            b.rs.append((eng, n))

    def dma(self, queue, fn, key, reads=(), writes=()):
        e = self.E[queue]
        waits = self._deps(e, reads, writes)
        self.dsem[key] = self.dsem.get(key, 0) + 16
        n = self.dsem[key]
        e.ops.append((waits, fn, (key, 16)))
        for b in writes:
            b.w = (key, n)
            b.rs = []
        for b in reads:
            b.rs.append((key, n))

    def final_wait(self, eng, deps):
        self.E[eng].ops.append((list(deps), None, None))


class _Stop(Exception):
    pass


def build(W=SEQ, OWN=OWN_FULL, dbg=None, stage=None, flags=()):
    def ck(n):
        if stage is not None and stage == n:
            raise _Stop()

    NT = W // 512
    T_OWN0 = (W - OWN) // 512
    NOWN = OWN // 512
    nc = bass.Bass("TRN2", target_bir_lowering=False)
    P = Prog()
    dbg = dbg or []
    dbg_out = {}

    def din(name, shape, dt=F32):
        return nc.dram_tensor(name, shape, dt, kind="ExternalInput").ap()

    xw = din("xw", [W, D])
    w_in = din("w_in", [D, 9232])
    w_a = din("w_a", [D, D])
    w_b = din("w_b", [D, D])
    w_o = din("w_o", [D, D])
    w_up = din("w_up", [D, 2 * DFF])
    w_dn = din("w_dn", [DFF, D])
    gmixB_d = din("gmixB", [128, D])
    gffnB_d = din("gffnB", [128, D])
    gfinB_d = din("gfinB", [128, D])
    cwA_d = din("cwA", [128, 8, 3])
    cwG_d = din("cwG", [128, 24, 4])
    cwF_d = din("cwF", [128, 44, 3])
    alog_d = din("alogB", [128, 32])
    dtb_d = din("dtbB", [128, 32])
    ng_d = din("ngP", [128, 1])
    cst_d = din("cst", [128, 5, 128])
    out_d = nc.dram_tensor("out", [OWN, D], F32, kind="ExternalOutput").ap()
    NUNIT2 = 37
    wsc = nc.dram_tensor("wsc", [8 + NUNIT2, 128, 8, 512], BF16, kind="Internal").ap()
    for name, shape in dbg:
        dbg_out[name] = nc.dram_tensor("dbg_" + name, shape, F32, kind="ExternalOutput").ap()

    es = ExitStack()
    SB_ACC = [0, []]
    build.sb_acc = SB_ACC

    class T:
        def __init__(self, name, shape, dt=F32):
            self.t = es.enter_context(nc.sbuf_tensor("sb_" + name, shape, dt))
            self.b = Buf(name)
            SB_ACC[0] += int(np.prod(shape[1:])) * (2 if dt == BF16 else 4)
            SB_ACC[1].append((name, int(np.prod(shape[1:])) * (2 if dt == BF16 else 4)))

        def __getitem__(self, k):
            return self.t[k]

    class PS:
        def __init__(self, name, shape, dt=F32):
            self.t = es.enter_context(nc.psum_tensor(name, shape, dt))
            self.b = Buf(name, psum=True)

        def __getitem__(self, k):
            return self.t[k]

    class View:
        def __init__(self, ap, name):
            self.ap = ap
            self.b = Buf(name)

        def __getitem__(self, k):
            return self.ap[k]

    arena = T("arena", [128, 11264], BF16)
    arena2 = T("arena2", [128, 12288], BF16)
    arenas = {1: (arena, 11264, [0]), 2: (arena2, 12288, [0])}

    def AV(name, shape, dt=F32, which=1):
        ar, cap, aoff = arenas[which]
        n = int(np.prod(shape[1:]))
        nb = n * (1 if dt == BF16 else 2)
        ap = ar.t[:, aoff[0]:aoff[0] + nb]
        aoff[0] += nb
        assert aoff[0] <= cap
        if dt != BF16:
            ap = ap.bitcast(F32)
        if len(shape) == 3:
            ap = ap.rearrange("p (a b) -> p a b", a=shape[1])
        return View(ap, name)

    def barrier():
        for en, e in P.E.items():
            waits = [(k_, e2.count) for k_, e2 in P.E.items() if k_ != en and e2.count > 0]
            waits += [(k_, v_) for k_, v_ in P.dsem.items() if not k_.startswith("wc")]
            e.ops.append((waits, None, None))
            for k_, v_ in waits:
                e.known[k_] = max(e.known.get(k_, 0), v_)

    def B(xs):
        return [x.b if hasattr(x, "b") else x for x in xs]

    def ACT(out, in_, func, r, w, bias=0.0, scale=1.0, accum=None):
        if accum is None:
            P.op("act", lambda e: e.activation(out=out, in_=in_, func=func, bias=bias, scale=scale), B(r), B(w))
        else:
            P.op("act", lambda e: e.activation(out=out, in_=in_, func=func, bias=bias, scale=scale, accum_out=accum), B(r), B(w))

    def TS(eng, out, in0, s1, s2, op0, op1, r, w):
        if op1 is None:
            P.op(eng, lambda e: e.tensor_scalar(out=out, in0=in0, scalar1=s1, scalar2=None, op0=op0), B(r), B(w))
        else:
            P.op(eng, lambda e: e.tensor_scalar(out=out, in0=in0, scalar1=s1, scalar2=s2, op0=op0, op1=op1), B(r), B(w))

    def STT(eng, out, in0, scalar, in1, op0, op1, r, w):
        P.op(eng, lambda e: e.scalar_tensor_tensor(out=out, in0=in0, scalar=scalar, in1=in1, op0=op0, op1=op1), B(r), B(w))

    def TT(eng, out, in0, in1, op, r, w):
        P.op(eng, lambda e: e.tensor_tensor(out=out, in0=in0, in1=in1, op=op), B(r), B(w))

    def CP(eng, out, in_, r, w):
        if eng == "act":
            P.op("act", lambda e: e.copy(out=out, in_=in_), B(r), B(w))
        else:
            P.op(eng, lambda e: e.tensor_copy(out=out, in_=in_), B(r), B(w))

    def MM(mms, r, w):
        def fn(e):
            ins = None
            for (o, l, rh, st, sp) in mms:
                ins = e.matmul(o, l, rh, start=st, stop=sp)
            return ins
        P.op("pe", fn, B(r), B(w))

    def TR(trs, r, w):
        def fn(e):
            ins = None
            for (o, i, idn) in trs:
                ins = e.transpose(o, i, idn)
            return ins
        P.op("pe", fn, B(r), B(w))

    def DMA(queue, out, in_, key, r, w):
        q = {"sp": "sp", "pool": "pool", "act": "act"}[queue]
        P.dma(q, lambda e: e.dma_start(out=out, in_=in_), key, B(r), B(w))

    cst = T("cst", [128, 5, 128])
    cstb = T("cstb", [128, 128], BF16)
    cstb2 = T("cstb2", [128, 2, 128], BF16)
    gphls = [T(f"gphl{i}", [128, 2, 32], BF16) for i in range(2)]
    gmixB = T("gmixB", [128, D])
    gffnB = T("gffnB", [128, D])
    gfinB = T("gfinB", [128, D])
    cwA = T("cwA", [128, 8, 3])
    cwG = T("cwG", [128, 24, 4])
    cwF = T("cwF", [128, 44, 3])
    negA = T("negA", [128, 32])
    dtb = T("dtb", [128, 32])
    ngP = T("ngP", [128, 1])
    wab = T("wab", [128, 8, 16], BF16)
    IDF = cst[:, 0, :]
    UTI = cst[:, 1, :]
    NEGSU = cst[:, 2, :]
    POSSL = cst[:, 3, :]
    ONES = cst[:, 4, :]
    IDB = cstb[:, :]

    xt = [T(f"xt{i}", [128, D]) for i in range(4)]
    xn = T("xn", [128, D], BF16)
    junk = T("junk", [128, D], BF16)
    st_small = [T(f"st{i}", [128, 4]) for i in range(2)]
    hT = T("hT", [128, 8, 512], BF16)
    h2T = hT
    wslot = [T(f"wslot{i}", [128, 8, 512], BF16) for i in range(3)]
    qkvw = [T(f"qkvw{i}", [128, 8, 384], BF16) for i in range(2)]
    pre = [T(f"pre{i}", [128, 515]) for i in range(2)]
    cacc = T("cacc", [128, 512])
    post = T("post", [128, 512])
    sqb = T("sqb", [128, 512])
    rr = T("rr", [128, 512])
    NSETS = 5
    qTs = [T(f"qT{i}", [128, 512], BF16) for i in range(NSETS)]
    kTs = [T(f"kT{i}", [128, 512], BF16) for i in range(NSETS)]
    vTs = [T(f"vT{i}", [128, 512], BF16) for i in range(NSETS)]
    oTs = [T(f"oT{i}", [128, 512]) for i in range(2)] + [AV("oT2", [128, 512], F32, 2)]
    oT = oTs[0]
    carG = T("carG", [128, 24, 3])
    carA = T("carA", [128, 8, 2])
    carF = T("carF", [128, 44, 2])
    betaPs = [T(f"betaP{i}", [128, 32]) for i in range(2)]
    gPs = [T(f"gP{i}", [128, 32]) for i in range(2)]
    GPs = [T(f"GP{i}", [128, 32]) for i in range(2)]
    kegP = T("kegP", [128, 32])
    s1Ps = [T(f"s1P{i}", [128, 32]) for i in range(2)]
    tmpP = T("tmpP", [128, 32])
    NPIPE = 3
    def PT(p, name, shape, dt=F32):
        return T(name, shape, dt) if p == 0 else AV(name, shape, dt, p)

    Fs = [[PT(p, f"F{p}_{j}", [128, 512]) for j in range(3)] for p in range(NPIPE)]
    gm0_ = T("gm0", [128, 4, 4, 128], BF16)
    gm = [gm0_ for p in range(NPIPE)]
    Vb = [[PT(p, f"V{p}_{j}", [128, 512], BF16) for j in range(2)] for p in range(NPIPE)]
    VTb = [[PT(p, f"VT{p}_{j}", [128, 512], BF16) for j in range(2)] for p in range(NPIPE)]
    Rb = [[PT(p, f"R{p}_{j}", [128, 512], BF16) for j in range(2)] for p in range(NPIPE)]
    ATb = [PT(p, f"AT{p}", [128, 512], BF16) for p in range(NPIPE)]
    QATb = [PT(p, f"QAT{p}", [128, 512], BF16) for p in range(NPIPE)]
    vbb = [PT(p, f"vb{p}", [128, 4, 128], BF16) for p in range(NPIPE)]
    kbgb = [PT(p, f"kbg{p}", [128, 4, 128], BF16) for p in range(NPIPE)]
    kdecb = [PT(p, f"kdec{p}", [128, 4, 128], BF16) for p in range(NPIPE)]
    ub = [PT(p, f"u{p}", [128, 512], BF16) for p in range(NPIPE)]
    wb = [PT(p, f"w{p}", [128, 512], BF16) for p in range(NPIPE)]
    nKWTb = [PT(p, f"nKWT{p}", [128, 512], BF16) for p in range(NPIPE)]
    kbTs = [PT(p, f"kbT{p}", [128, 512], BF16) for p in range(NPIPE)]
    sm4 = [T(f"sm4_{p}", [128, 16]) for p in range(NPIPE)]
    bphls = [T(f"bphl{i}", [128, 2, 32], BF16) for i in range(2)]
    Sf = [T(f"Sf{h}", [128, 128]) for h in range(NH)]
    Sb = [[T(f"Sb{h}_{j}", [128, 128], BF16) for j in range(2)] for h in range(NH)]
    sb_par = [0] * NH
    ob = T("ob", [128, 8, 512], BF16)
    yap = View(arena2.t[:, 0:4096].rearrange("p (f n) -> p f n", f=8), "yap")
    sz = View(arena2.t[:, 4096:8192].rearrange("p (f n) -> p f n", f=8), "sz")
    mixT = View(arena2.t[:, 8192:12288].rearrange("p (f n) -> p f n", f=8), "mixT")
    sga = rr
    sgb = oT
    cgs = cacc
    uw = pre
    actT = View(arena.t[:, :].rearrange("p (f n) -> p f n", f=22), "actT")
    cvg = post
    cvv = sqb
    yout = [T(f"yout{i}", [128, D]) for i in range(1)]

    class Slot:
        def __init__(self, ap, b):
            self.ap = ap
            self.b = b

        def __getitem__(self, k):
            return self.ap[k]

    ps_proj = [PS(f"ps_proj{i}", [128, 512]) for i in range(2)]
    ps_small = ps_proj[0]
    ps_gb = [Slot(ps_small[:, 128:384], ps_small.b) for i in range(2)]
    ps_mm_t = [PS(f"ps_mm{i}", [128, 512]) for i in range(6)]


    mm_slots = [Slot(ps_mm_t[i][:, :], ps_mm_t[i].b) for i in range(6)]
    bank_free = list(mm_slots)
    mm_i = [0]

    def mslot():
        s = mm_slots[mm_i[0] % 3]
        mm_i[0] += 1
        return s

    proj_i = [0]

    in_phase2 = [False]

    def pslot():
        banks = (ps_proj + mm_slots) if in_phase2[0] else ps_proj
        s = banks[proj_i[0] % len(banks)]
        proj_i[0] += 1
        return s

    cl = [(cst, cst_d), (gmixB, gmixB_d), (gffnB, gffnB_d), (gfinB, gfinB_d), (cwA, cwA_d), (cwG, cwG_d),
          (cwF, cwF_d), (negA, alog_d), (dtb, dtb_d), (ngP, ng_d)]
    for t, d_ in cl:
        DMA("sp", t[:], d_, "const", [], [t])
    for t, d_ in cl:
        t.b.w = ("const", P.dsem["const"])
    CP("dve", cstb[:, :], cst[:, 0, :], [cst], [cstb])
    CP("dve", cstb2[:, 0, :], cst[:, 1, :], [cst], [cstb2])
    CP("dve", cstb2[:, 1, :], cst[:, 4, :], [cst], [cstb2])
    ACT(negA[:, :], negA[:, :], AF.Exp, [negA], [negA])
    TS("dve", negA[:, :], negA[:, :], -1.0, None, ALU.mult, None, [negA], [negA])
    for t in (carG, carA, carF):
        P.op("pool", lambda e, t=t: e.memset(t[:], 0.0), [], [t.b])
    for h in range(NH):
        P.op("pool", lambda e, h=h: e.memset(Sf[h][:, :], 0.0), [], [Sf[h].b])
        P.op("pool", lambda e, h=h: e.memset(Sb[h][0][:, :], 0.0), [], [Sb[h][0].b])

    wsc_b = [Buf(f"wsc{u}") for u in range(8 + NUNIT2)]
    stopped = [False]

    uext = {}

    def cast_piece(u, col0, src, r0, nk, c0, width, key="wcast"):
        uext[u] = (nk, max(uext.get(u, (0, 0))[1], col0 + width))
        s = src[r0:r0 + nk * 128, c0:c0 + width].rearrange("(k p) n -> p k n", p=128)
        cast_dma(wsc[u, :, 0:nk, col0:col0 + width], s)

    cast_n = [0]
    dummy = T("dummy", [128, 4])

    lazy_casts = []
    lazy_mode = [False]

    def cast_dma(out, in_):
        if "nocast" in flags:
            return
        if lazy_mode[0]:
            lazy_casts.append((out, in_))
            return
        key = f"wc{cast_n[0] % 6}"
        cast_n[0] += 1
        if P.dsem.get(key, 0) > 0:
            P.E["pool"].ops.append(([(key, P.dsem[key])], None, None))
        DMA("pool", out, in_, key, [], [])

    def emit_lazy_casts(k):
        lazy_mode[0] = False
        for _ in range(min(k, len(lazy_casts))):
            o_, i_ = lazy_casts.pop(0)
            cast_dma(o_, i_)
        if not lazy_casts and not casts_done[0]:
            casts_done[0] = True
            cast_barrier([wsc_b[uu] for uu in range(8, 8 + NUNIT2)])

    casts_done = [False]

    def cast_barrier(bufs):
        fake = []
        for j in range(6):
            key = f"wc{j}"
            if P.dsem.get(key, 0) > 0:
                fb = Buf("fk")
                fb.w = (key, P.dsem[key])
                fake.append(fb)
        P.op("pool", lambda e: e.memset(dummy[:, :], 0.0), fake, [dummy.b])
        for b_ in bufs:
            b_.w = dummy.b.w

    cast_dma(wab[:, :, :], w_in[:, C_A:C_A + 16].rearrange("(k p) n -> p k n", p=128))
    for h in range(NH):
        for j, cb in enumerate((C_Q, C_K, C_V)):
            cast_piece(h, j * 128, w_in, 0, 8, cb + h * 128, 128, key="wcast0")
    cast_barrier([wab.b] + [wsc_b[h] for h in range(NH)])
    lazy_mode[0] = True
    units = []
    u = 8
    U_A = []
    for ct in range(8):
        for j, cb in enumerate((C_BG, C_CG, C_XV)):
            cast_piece(u, j * 128, w_in, 0, 8, cb + ct * 128, 128)
        U_A.append(u); u += 1
    U_Z = []
    for i in range(2):
        cast_piece(u, 0, w_in, 0, 8, C_Z + i * 512, 512)
        U_Z.append(u); u += 1
    U_G = []
    for nt in range(8):
        cast_piece(u, 0, w_in, 0, 8, C_GA + nt * 128, 128)
        cast_piece(u, 128, w_in, 0, 8, C_GB + nt * 128, 128)
        cast_piece(u, 256, w_a, 0, 8, nt * 128, 128)
        cast_piece(u, 384, w_b, 0, 8, nt * 128, 128)
        U_G.append(u); u += 1
    U_O = []
    for i in range(2):
        cast_piece(u, 0, w_o, 0, 8, i * 512, 512)
        U_O.append(u); u += 1
    U_UP = []
    for fp in range(11):
        for j in range(2):
            ft = fp * 2 + j
            cast_piece(u, j * 256, w_up, 0, 8, ft * 128, 128)
            cast_piece(u, j * 256 + 128, w_up, 0, 8, DFF + ft * 128, 128)
        U_UP.append(u); u += 1
    U_D = []
    for nh in range(2):
        row = []
        for kg, (k0, nk) in enumerate(((0, 8), (8, 8), (16, 6))):
            cast_piece(u, 0, w_dn, k0 * 128, nk, nh * 512, 512)
            row.append((u, nk)); u += 1
        U_D.append(row)
    assert u == 8 + NUNIT2
    lazy_mode[0] = False

    ws_state = {"q": [], "n": 0}

    def ws_issue(unit):
        i = ws_state["n"] % 3
        ws_state["n"] += 1
        nk, ncol = uext[unit]
        DMA("sp", wslot[i][:, 0:nk, 0:ncol], wsc[unit, :, 0:nk, 0:ncol], f"ws{i}", [wsc_b[unit]], [wslot[i]])
        return wslot[i]

    class WStream:
        def __init__(self, seq, ahead=2):
            self.seq = list(seq)
            self.loaded = []
            self.pos = 0
            self.ahead = ahead

        def get(self, ahead=None):
            ahead = self.ahead if ahead is None else ahead
            while len(self.loaded) < min(len(self.seq), self.pos + 1 + ahead):
                self.loaded.append(ws_issue(self.seq[len(self.loaded)]))
            s = self.loaded[self.pos]
            self.pos += 1
            return s

    sidx = [0]

    def norm_to_T(src_tile, gB, dstT, sub, bank=None):
        bank = bank if bank is not None else bank_free[0]
        ps_tr = Slot(bank[:, 0:512].bitcast(BF16), bank.b)
        st = st_small[sidx[0] % 2]
        sidx[0] += 1
        ACT(junk[:, :], src_tile[:, :], AF.Square, [src_tile], [junk, st], accum=st[:, 0:1])
        ACT(st[:, 1:2], st[:, 0:1], AF.Ln, [st], [st], bias=EPS, scale=1.0 / D)
        ACT(st[:, 2:3], st[:, 1:2], AF.Exp, [st], [st], scale=-0.5)
        STT("dve", xn[:, :], src_tile[:, :], st[:, 2:3], gB[:, :], ALU.mult, ALU.mult, [src_tile, st, gB], [xn])
        TR([(ps_tr[:, k * 128:(k + 1) * 128], xn[:, k * 128:(k + 1) * 128], IDB) for k in range(8)], [xn, cstb], [ps_tr])
        CP("act", dstT[:, :, sub * 128:(sub + 1) * 128], ps_tr[:, :].rearrange("p (k n) -> p k n", k=8), [ps_tr], [dstT])
        return st

    def dump(name, ap, r):
        if name in dbg_out:
            DMA("sp", dbg_out[name], ap, "dbg", r, [])

    def dwconv(work, carry_ap, K, wts, n, outp, r_w, w_out, car_t):
        CP("pool", work[:, 0:K - 1], carry_ap, [car_t], [work])
        P.op("act", lambda e: e.mul(out=outp, in_=work[:, 0:n], mul=wts(0)), B([work] + r_w), B(w_out))
        for j in range(1, K):
            STT("dve", outp, work[:, j:j + n], wts(j), outp, ALU.mult, ALU.add, [work] + r_w + w_out, w_out)
        CP("pool", carry_ap, work[:, n:n + K - 1], [work], [car_t])

    cb_i = [0]
    bc_i = lambda ap4: ap4.unsqueeze(2).broadcast_to([128, 4, 128])
    bc_c = lambda ap: ap.unsqueeze(1).broadcast_to([128, 4, 128])
    v4 = lambda ap: ap.rearrange("p (c i) -> p c i", c=4)
    UTIB = cstb2[:, 0, :]
    ONESB = cstb2[:, 1, :]

    def acquire(n):
        while len(bank_free) < n:
            yield
        return [bank_free.pop(0) for _ in range(n)]

    def release(*bs):
        for b_ in bs:
            bank_free.append(b_)

    def stageA(tt, h, wq, qs):
        qT, kT, vT = qTs[qs], kTs[qs], vTs[qs]
        cc = [cacc, post, sqb]
        for j in range(3):
            ps = pslot()
            MM([(ps[:, :], wq[:, k, j * 128:(j + 1) * 128], hT[:, k, :], k == 0, k == 7) for k in range(8)], [wq, hT], [ps])
            pw = pre[(cb_i[0]) % 2]
            cb_i[0] += 1
            CP("act", pw[:, 3:515], ps[:, :], [ps], [pw])
            ch = j * 8 + h
            dwconv(pw, carG[:, ch, :], 4, lambda jj, ch=ch: cwG[:, ch, jj:jj + 1], 512, cc[j][:, :], [cwG], [cc[j]], carG)
            yield
        ACT(cc[0][:, :], cc[0][:, :], AF.Silu, [cc[0]], [cc[0]])
        ACT(cc[1][:, :], cc[1][:, :], AF.Silu, [cc[1]], [cc[1]])
        ACT(vT[:, :], cc[2][:, :], AF.Silu, [cc[2]], [vT])
        yield
        for src, dst, scale in ((cc[0], qT, 128 ** -0.5), (cc[1], kT, 1.0)):
            ACT(junk[:, 0:512], src[:, :], AF.Square, [src], [junk])
            ps2 = pslot()
            MM([(ps2[:, :], ONESB, junk[:, 0:512], True, True)], [junk, cstb2], [ps2])
            ACT(rr[:, :], ps2[:, :], AF.Ln, [ps2], [rr], bias=EPS)
            ACT(rr[:, :], rr[:, :], AF.Exp, [rr], [rr], scale=-0.5)
            STT("dve", dst[:, :], src[:, :], scale, rr[:, :], ALU.mult, ALU.mult, [src, rr], [dst])
            yield

    def stageC(tt, h, qs, p=0):
        qT, kT, vT = qTs[qs], kTs[qs], vTs[qs]
        sl = slice(h * 4, h * 4 + 4)
        F0, F1, F2 = Fs[p]
        G, V, VT, R = gm[p], Vb[p], VTb[p], Rb[p]
        AT, QAT, vb, kbg, kdec, u_, w_, nK, sm = ATb[p], QATb[p], vbb[p], kbgb[p], kdecb[p], ub[p], wb[p], nKWTb[p], sm4[p]
        kbT, oT = kbTs[p], oTs[p]
        tp = tt % 2
        betaP, GP, s1P, gphl, bphl = betaPs[tp], GPs[tp], s1Ps[tp], gphls[tp], bphls[tp]
        f2 = lambda t3: t3.rearrange("p c i -> p (c i)")
        cs = lambda c: slice(c * 128, (c + 1) * 128)
        bG, bB = yield from acquire(2)
        TT("pool", G[:, 0], bc_c(UTIB), bc_i(gphl[:, 0, sl]), ALU.mult, [cstb2, gphl], [G])
        TT("pool", G[:, 1], bc_c(UTIB), bc_i(gphl[:, 1, sl]), ALU.mult, [cstb2, gphl], [G])
        TT("pool", G[:, 2], bc_c(IDB), bc_i(bphl[:, 0, sl]), ALU.mult, [cstb, bphl], [G])
        TT("pool", G[:, 3], bc_c(IDB), bc_i(bphl[:, 1, sl]), ALU.mult, [cstb, bphl], [G])
        MM([(bG[:, 0:512], ONESB, f2(G[:, 0]), True, False), (bG[:, 0:512], ONESB, f2(G[:, 1]), False, True)], [G, cstb2], [bG])
        MM([(bB[:, 0:512], ONESB, f2(G[:, 2]), True, False), (bB[:, 0:512], ONESB, f2(G[:, 3]), False, True)], [G, cstb2], [bB])
        yield
        TT("dve", v4(F0[:, :]), v4(bG[:, 0:512]), bc_i(GP[:, sl]), ALU.subtract, [bG, GP], [F0])
        ACT(F2[:, :], bG[:, 0:512], AF.Exp, [bG], [F2])
        CP("act", sm[:, 0:4], v4(bG[:, 0:512])[:, :, 127], [bG], [sm])
        TT("dve", kbT[:, :], kT[:, :], bB[:, 0:512], ALU.mult, [kT, bB], [kbT])
        release(bG, bB)
        yield
        TT("dve", v4(F1[:, :]), v4(F0[:, :]), bc_c(POSSL), ALU.add, [F0, cst], [F1])
        TT("pool", v4(F0[:, :]), v4(F0[:, :]), bc_c(NEGSU), ALU.add, [F0, cst], [F0])
        TT("dve", sm[:, 4:8], sm[:, 0:4], GP[:, sl], ALU.subtract, [sm, GP], [sm])
        yield
        ACT(F1[:, :], F1[:, :], AF.Exp, [F1], [F1], scale=-1.0)
        ACT(F0[:, :], F0[:, :], AF.Exp, [F0], [F0])
        ACT(sm[:, 8:12], sm[:, 4:8], AF.Exp, [sm], [sm])
        ACT(sm[:, 12:16], sm[:, 0:4], AF.Exp, [sm], [sm])
        TT("pool", F2[:, :], qT[:, :], F2[:, :], ALU.mult, [qT, F2], [F2])
        bL, bU, bA = yield from acquire(3)
        MM([(bL[:, cs(c)], kbT[:, cs(c)], kT[:, cs(c)], True, True) for c in range(4)], [kbT, kT], [bL])
        MM([(bU[:, cs(c)], kT[:, cs(c)], kbT[:, cs(c)], True, True) for c in range(4)], [kbT, kT], [bU])
        MM([(bA[:, cs(c)], kT[:, cs(c)], qT[:, cs(c)], True, True) for c in range(4)], [qT, kT], [bA])
        yield
        STT("dve", VT[0][:, :], bL[:, 0:512], -1.0, F1[:, :], ALU.mult, ALU.mult, [bL, F1], [VT[0]])
        STT("dve", V[0][:, :], bU[:, 0:512], -1.0, F0[:, :], ALU.mult, ALU.mult, [bU, F0], [V[0]])
        TT("pool", v4(F0[:, :]), v4(F0[:, :]), bc_c(IDF), ALU.add, [F0, cst], [F0])
        TT("dve", AT[:, :], bA[:, 0:512], F0[:, :], ALU.mult, [bA, F0], [AT])
        TT("pool", v4(R[0][:, :]), v4(V[0][:, :]), bc_c(IDB), ALU.add, [V[0], cstb], [R[0]])
        release(bL, bU, bA)
        (bX,) = yield from acquire(1)
        bXb = bX[:, 0:512].bitcast(BF16)
        TR([(bXb[:, cs(c)], kT[:, cs(c)], IDB) for c in range(4)] +
           [(bXb[:, 512 + c * 128:512 + (c + 1) * 128], vT[:, cs(c)], IDB) for c in range(4)], [kT, vT, cstb], [bX])
        bT, bV = yield from acquire(2)
        MM([(bT[:, cs(c)], V[0][:, cs(c)], VT[0][:, cs(c)], True, True) for c in range(4)], [V[0], VT[0]], [bT])
        MM([(bV[:, cs(c)], VT[0][:, cs(c)], V[0][:, cs(c)], True, True) for c in range(4)], [V[0], VT[0]], [bV])
        yield
        TT("dve", kbg[:, :, :], v4(bXb[:, 0:512]), bc_i(s1P[:, sl]), ALU.mult, [bX, s1P], [kbg])
        TT("dve", kdec[:, :, :], v4(bXb[:, 0:512]), bc_i(sm[:, 8:12]), ALU.mult, [bX, sm], [kdec])
        TT("dve", vb[:, :, :], v4(bXb[:, 512:1024]), bc_i(betaP[:, sl]), ALU.mult, [bX, betaP], [vb])
        release(bX)
        cur = 0
        for m in range(1, 7):
            nxt = 1 - cur
            CP("act", VT[nxt][:, :], bT[:, 0:512], [bT], [VT[nxt]])
            release(bT)
            if m < 6:
                CP("act", V[nxt][:, :], bV[:, 0:512], [bV], [V[nxt]])
                release(bV)
            if m > 1:
                CP("act", R[cur][:, :], bR[:, 0:512], [bR], [R[cur]])
                release(bR)
            yield
            (bR,) = yield from acquire(1)
            mmr = []
            for c in range(4):
                mmr.append((bR[:, cs(c)], IDB, R[cur][:, cs(c)], True, False))
                mmr.append((bR[:, cs(c)], VT[nxt][:, cs(c)], R[cur][:, cs(c)], False, True))
            MM(mmr, [cstb, R[cur], VT[nxt]], [bR])
            if m < 5:
                bT, bV = yield from acquire(2)
            elif m == 5:
                (bT,) = yield from acquire(1)
            if m < 6:
                MM([(bT[:, cs(c)], V[nxt][:, cs(c)], VT[nxt][:, cs(c)], True, True) for c in range(4)], [V[nxt], VT[nxt]], [bT])
            if m < 5:
                MM([(bV[:, cs(c)], VT[nxt][:, cs(c)], V[nxt][:, cs(c)], True, True) for c in range(4)], [V[nxt], VT[nxt]], [bV])
            cur = nxt
            yield
        CP("act", R[cur][:, :], bR[:, 0:512], [bR], [R[cur]])
        release(bR)
        Rf = R[cur]
        yield
        b0, b1 = yield from acquire(2)
        MM([(b0[:, cs(c)], Rf[:, cs(c)], vb[:, c, :], True, True) for c in range(4)], [Rf, vb], [b0])
        MM([(b1[:, cs(c)], Rf[:, cs(c)], kbg[:, c, :], True, True) for c in range(4)], [Rf, kbg], [b1])
        yield
        CP("act", u_[:, :], b0[:, 0:512], [b0], [u_])
        CP("act", w_[:, :], b1[:, 0:512], [b1], [w_])
        release(b0, b1)
        yield
        bK, bQ = yield from acquire(2)
        MM([(bK[:, cs(c)], w_[:, cs(c)], kdec[:, c, :], True, True) for c in range(4)], [w_, kdec], [bK])
        MM([(bQ[:, cs(c)], w_[:, cs(c)], AT[:, cs(c)], True, True) for c in range(4)], [w_, AT], [bQ])
        yield
        P.op("act", lambda e: e.mul(out=nK[:, :], in_=bK[:, 0:512], mul=-1.0), B([bK]), B([nK]))
        TT("dve", QAT[:, :], F2[:, :], bQ[:, 0:512], ALU.subtract, [F2, bQ], [QAT])
        release(bK, bQ)
        yield
        for c in range(4):
            so = sb_par[h]
            S_old = Sb[h][so]
            S_new = Sb[h][1 - so]
            sb_par[h] = 1 - so
            (bS,) = yield from acquire(1)
            MM([(bS[:, 128:256], kdec[:, c, :], u_[:, cs(c)], True, False),
                (bS[:, 128:256], nK[:, cs(c)], S_old[:, :], False, True),
                (bS[:, 0:128], S_old[:, :], QAT[:, cs(c)], True, False),
                (bS[:, 0:128], u_[:, cs(c)], AT[:, cs(c)], False, True)],
               [S_old, QAT, u_, AT, kdec, nK], [bS])
            yield
            gam = sm[:, 12 + c:13 + c]
            STT("dve", S_new[:, :], Sf[h][:, :], gam, bS[:, 128:256], ALU.mult, ALU.add, [Sf[h], sm, bS], [S_new])
            STT("dve", Sf[h][:, :], Sf[h][:, :], gam, bS[:, 128:256], ALU.mult, ALU.add, [Sf[h], sm, bS], [Sf[h]])
            CP("act", oT[:, cs(c)], bS[:, 0:128], [bS], [oT])
            release(bS)
            yield
        ACT(V[0][:, :], oT[:, :], AF.Square, [oT], [V[0]])
        yield
        (b2,) = yield from acquire(1)
        MM([(b2[:, 0:512], ONESB, V[0][:, :], True, True)], [V[0], cstb2], [b2])
        yield
        ACT(F1[:, :], b2[:, 0:512], AF.Ln, [b2], [F1], bias=EPS, scale=1.0 / 128)
        release(b2)
        ACT(F1[:, :], F1[:, :], AF.Exp, [F1], [F1], scale=-0.5)
        STT("dve", ob[:, h, :], oT[:, :], ngP[:, 0:1], F1[:, :], ALU.mult, ALU.mult, [oT, F1, ngP], [ob])
        yield

    def run_il(gens, prio=1):
        gens = [g for g in gens if g is not None]
        first = gens[0] if gens else None
        while gens:
            for g in list(gens):
                for _ in range(prio if g is first else 1):
                    try:
                        next(g)
                    except StopIteration:
                        gens.remove(g)
                        break

    def load_wq(h):
        wq = qkvw[h % 2]
        DMA("sp", wq[:, :, :], wsc[h, :, :, 0:384], f"qw{h % 2}", [wsc_b[h]], [wq])
        return wq

    x_loaded = set()

    def load_x(tt):
        if tt in x_loaded or tt >= NT:
            return
        x_loaded.add(tt)
        for sub in range(4):
            r0 = tt * 512 + sub * 128
            DMA("sp", xt[sub][:, :], xw[r0:r0 + 128, :], f"xl{sub}", [], [xt[sub]])

    def tile_start(tt):
        tp = tt % 2
        betaP, gP, GP, s1P, gphl, bphl = betaPs[tp], gPs[tp], GPs[tp], s1Ps[tp], gphls[tp], bphls[tp]
        load_x(tt)
        for sub in range(4):
            (bk,) = yield from acquire(1)
            norm_to_T(xt[sub], gmixB, hT, sub, bk)
            release(bk)
            yield
        if tt < T_OWN0 - 1:
            load_x(tt + 1)
        MM([(ps_small[:, c * 16:(c + 1) * 16], hT[:, k, c * 128:(c + 1) * 128], wab[:, k, :], k == 0, k == 7)
            for c in range(4) for k in range(8)], [hT, wab], [ps_small])
        pa = ps_small[:, 0:64].rearrange("p (c x) -> p c x", c=4)
        vch = lambda t: t[:, :].rearrange("p (c h) -> p c h", c=4)
        vhc = lambda t: t[:, :].rearrange("p (h c) -> p c h", c=4)
        ACT(vhc(betaP), pa[:, :, 8:16], AF.Exp, [ps_small], [betaP], scale=-1.0)
        ACT(betaP[:, :], betaP[:, :], AF.Ln, [betaP], [betaP], bias=1.0)
        ACT(betaP[:, :], betaP[:, :], AF.Exp, [betaP], [betaP], scale=-1.0)
        TT("dve", vch(tmpP), pa[:, :, 0:8], vch(dtb), ALU.add, [ps_small, dtb], [tmpP])
        ACT(tmpP[:, :], tmpP[:, :], AF.Exp, [tmpP], [tmpP])
        ACT(tmpP[:, :], tmpP[:, :], AF.Ln, [tmpP], [tmpP], bias=1.0)
        TT("dve", vhc(gP), vch(tmpP), vch(negA), ALU.mult, [tmpP, negA], [gP])
        CP("pool", gphl[:, 0, :], gP[:, :], [gP], [gphl])
        TT("pool", tmpP[:, :], gP[:, :], gphl[:, 0, :], ALU.subtract, [gP, gphl], [tmpP])
        CP("pool", gphl[:, 1, :], tmpP[:, :], [tmpP], [gphl])
        CP("pool", bphl[:, 0, :], betaP[:, :], [betaP], [bphl])
        TT("pool", tmpP[:, :], betaP[:, :], bphl[:, 0, :], ALU.subtract, [betaP, bphl], [tmpP])
        CP("pool", bphl[:, 1, :], tmpP[:, :], [tmpP], [bphl])
        MM([(ps_small[:, 64:96], cstb2[:, 0, :], gphl[:, 0, :], True, False), (ps_small[:, 64:96], cstb2[:, 0, :], gphl[:, 1, :], False, True)], [cstb2, gphl], [ps_small])
        CP("act", GP[:, :], ps_small[:, 64:96], [ps_small], [GP])
        ACT(kegP[:, :], ps_small[:, 64:96], AF.Exp, [ps_small], [kegP])
        TT("dve", s1P[:, :], kegP[:, :], betaP[:, :], ALU.mult, [kegP, betaP], [s1P])
        yield

    def gdn_tiles(tts):
        tts = list(tts)
        gl = [(tt, h) for tt in tts for h in range(NH)]
        doneA = set()
        doneC = set()

        def chainA():
            for gi, (tt, h) in enumerate(gl):
                while gi >= NSETS and gl[gi - NSETS] not in doneC:
                    yield
                if h == 0:
                    while any((tt - 2, hh) in gl and (tt - 2, hh) not in doneC for hh in range(NH)):
                        yield
                    yield from tile_start(tt)
                emit_lazy_casts(2)
                yield from stageA(tt, h, load_wq(h), gi % NSETS)
                doneA.add((tt, h))

        def chainC(p, delay):
            for _ in range(delay):
                yield
            for gi, (tt, h) in enumerate(gl):
                if gi % NPIPE != p:
                    continue
                while (tt, h) not in doneA:
                    yield
                while gi >= NH and gl[gi - NH] not in doneC:
                    yield
                yield from stageC(tt, h, gi % NSETS, p)
                doneC.add((tt, h))

        run_il([chainA()] + [chainC(p, p * STAGGER) for p in range(NPIPE)], prio=A_PRIO)

    out_sem_total = [0]

    def phase2(tt, c0, c1, store):
        emit_lazy_casts(10 ** 6)
        in_phase2[0] = True
        try:
            _phase2(tt, c0, c1, store)
        finally:
            in_phase2[0] = False

    def _phase2(tt, c0, c1, store):
        n = c1 - c0
        subs = list(range(c0 // 128, c1 // 128))
        seq = U_A + U_Z + U_G + U_O + U_UP + [uu for row in U_D for (uu, _) in row]
        wsm = WStream(seq)
        hs = hT[:, :, c0:c1]
        for ct in range(8):
            wu = wsm.get()
            psc = pslot()
            MM([(psc[:, 0:n], wu[:, k, 128:256], hT[:, k, c0:c1], k == 0, k == 7) for k in range(8)], [wu, hT], [psc])
            CP("act", cgs[:, 0:n], psc[:, 0:n], [psc], [cgs])
            psx = pslot()
            MM([(psx[:, 0:n], wu[:, k, 256:384], hT[:, k, c0:c1], k == 0, k == 7) for k in range(8)], [wu, hT], [psx])
            w_ = uw[ct % 2]
            TT("dve", w_[:, 2:2 + n], psx[:, 0:n], cgs[:, 0:n], ALU.mult, [psx, cgs], [w_])
            dwconv(w_, carA[:, ct, :], 3, lambda jj, ct=ct: cwA[:, ct, jj:jj + 1], n, cvg[:, 0:n], [cwA], [cvg], carA)
            psb = pslot()
            MM([(psb[:, 0:n], wu[:, k, 0:128], hT[:, k, c0:c1], k == 0, k == 7) for k in range(8)], [wu, hT], [psb])
            TT("dve", yap[:, ct, 0:n], psb[:, 0:n], cvg[:, 0:n], ALU.mult, [psb, cvg], [yap])
        for i2 in range(2):
            wu = wsm.get()
            for j in range(4):
                ct = i2 * 4 + j
                ps = pslot()
                MM([(ps[:, 0:n], wu[:, k, j * 128:(j + 1) * 128], hT[:, k, c0:c1], k == 0, k == 7) for k in range(8)], [wu, hT], [ps])
                ACT(sz[:, ct, 0:n], ps[:, 0:n], AF.Silu, [ps], [sz])
                TT("pool", sz[:, ct, 0:n], sz[:, ct, 0:n], ob[:, ct, c0:c1], ALU.mult, [sz, ob], [sz])
        for nt in range(8):
            wu = wsm.get()
            ps = pslot()
            MM([(ps[:, 0:n], wu[:, k, 0:128], hT[:, k, c0:c1], k == 0, k == 7) for k in range(8)], [wu, hT], [ps])
            ACT(sga[:, 0:n], ps[:, 0:n], AF.Sigmoid, [ps], [sga])
            ps = pslot()
            MM([(ps[:, 0:n], wu[:, k, 128:256], hT[:, k, c0:c1], k == 0, k == 7) for k in range(8)], [wu, hT], [ps])
            ACT(sgb[:, 0:n], ps[:, 0:n], AF.Sigmoid, [ps], [sgb])
            ps = pslot()
            MM([(ps[:, 0:n], wu[:, k, 256:384], yap[:, k, 0:n], k == 0, k == 7) for k in range(8)], [wu, yap], [ps])
            TT("dve", sga[:, 0:n], sga[:, 0:n], ps[:, 0:n], ALU.mult, [sga, ps], [sga])
            ps = pslot()
            MM([(ps[:, 0:n], wu[:, k, 384:512], sz[:, k, 0:n], k == 0, k == 7) for k in range(8)], [wu, sz], [ps])
            TT("dve", sgb[:, 0:n], sgb[:, 0:n], ps[:, 0:n], ALU.mult, [sgb, ps], [sgb])
            TT("pool", mixT[:, nt, 0:n], sga[:, 0:n], sgb[:, 0:n], ALU.add, [sga, sgb], [mixT])
        for nh in range(2):
            wu = wsm.get()
            for si, sub in enumerate(subs):
                ps = pslot()
                MM([(ps[:, :], mixT[:, k, si * 128:(si + 1) * 128], wu[:, k, :], k == 0, k == 7) for k in range(8)], [wu, mixT], [ps])
                TT("dve", xt[sub][:, nh * 512:(nh + 1) * 512], xt[sub][:, nh * 512:(nh + 1) * 512], ps[:, :], ALU.add, [xt[sub], ps], [xt[sub]])
        for si, sub in enumerate(subs):
            norm_to_T(xt[sub], gffnB, h2T, si)
        for fp in range(11):
            wu = wsm.get()
            for j in range(2):
                ft = fp * 2 + j
                psg = pslot()
                MM([(psg[:, 0:n], wu[:, k, j * 256:j * 256 + 128], h2T[:, k, 0:n], k == 0, k == 7) for k in range(8)], [wu, h2T], [psg])
                w0 = uw[0]
                CP("act", w0[:, 2:2 + n], psg[:, 0:n], [psg], [w0])
                dwconv(w0, carF[:, ft, :], 3, lambda jj, ft=ft: cwF[:, ft, jj:jj + 1], n, cvg[:, 0:n], [cwF], [cvg], carF)
                psv = pslot()
                MM([(psv[:, 0:n], wu[:, k, j * 256 + 128:j * 256 + 256], h2T[:, k, 0:n], k == 0, k == 7) for k in range(8)], [wu, h2T], [psv])
                w1 = uw[1]
                CP("act", w1[:, 2:2 + n], psv[:, 0:n], [psv], [w1])
                dwconv(w1, carF[:, 22 + ft, :], 3, lambda jj, ft=ft: cwF[:, 22 + ft, jj:jj + 1], n, cvv[:, 0:n], [cwF], [cvv], carF)
                ACT(cvg[:, 0:n], cvg[:, 0:n], AF.Silu, [cvg], [cvg])
                TT("dve", actT[:, ft, 0:n], cvg[:, 0:n], cvv[:, 0:n], ALU.mult, [cvg, cvv], [actT])
        for nh in range(2):
            wus = [(wsm.get(ahead=0), nk) for (_, nk) in U_D[nh]]
            if not store:
                continue
            for si, sub in enumerate(subs):
                ps = pslot()
                mms = []
                for kg, (wu, nk) in enumerate(wus):
                    for k in range(nk):
                        kk = kg * 8 + k
                        mms.append((ps[:, :], actT[:, kk, si * 128:(si + 1) * 128], wu[:, k, :], kk == 0, kk == 21))
                MM(mms, [actT] + [w for w, _ in wus], [ps])
                TT("dve", xt[sub][:, nh * 512:(nh + 1) * 512], xt[sub][:, nh * 512:(nh + 1) * 512], ps[:, :], ALU.add, [xt[sub], ps], [xt[sub]])
        if store:
            for si, sub in enumerate(subs):
                st = st_small[sidx[0] % 2]
                sidx[0] += 1
                yo = yout[0]
                ACT(junk[:, :], xt[sub][:, :], AF.Square, [xt[sub]], [junk, st], accum=st[:, 0:1])
                ACT(st[:, 1:2], st[:, 0:1], AF.Ln, [st], [st], bias=EPS, scale=1.0 / D)
                ACT(st[:, 2:3], st[:, 1:2], AF.Exp, [st], [st], scale=-0.5)
                STT("dve", yo[:, :], xt[sub][:, :], st[:, 2:3], gfinB[:, :], ALU.mult, ALU.mult, [xt[sub], st, gfinB], [yo])
                row0 = (tt - T_OWN0) * 512 + sub * 128
                DMA("sp", out_d[row0:row0 + 128, :], yo[:, :], "store0", [yo], [])

    def main_loop():
        ck(10)
        n_cont = max(T_OWN0 - 1, 0)
        if n_cont > 0:
            gdn_tiles(range(0, n_cont))
        for tt in range(n_cont, NT):
            gdn_tiles([tt])
            ck(70)
            if tt == T_OWN0 - 1:
                barrier()
                phase2(tt, 384, 512, False)
                barrier()
                ck(80)
            elif tt >= T_OWN0:
                barrier()
                phase2(tt, 0, 512, True)
                barrier()

    try:
        main_loop()
    except _Stop:
        pass

    fw = [(k_, v_) for k_, v_ in P.dsem.items() if k_.startswith("store")]
    if "dbg" in P.dsem:
        fw.append(("dbg", P.dsem["dbg"]))
    for k_ in P.dsem:
        if k_.startswith("xl") or k_.startswith("qw") or k_.startswith("ws") or k_ == "const":
            fw.append((k_, P.dsem[k_]))
    P.final_wait("sp", fw)
    P.final_wait("sp", [(k_, e_.count) for k_, e_ in P.E.items() if e_.count > 0])

    keys = list(P.E.keys()) + list(P.dsem.keys())
    sems = {k: es.enter_context(nc.semaphore("s_" + k)) for k in keys}
    with nc.Block() as block:
        def replay(name):
            def run(eng):
                for waits, fn, inc in P.E[name].ops:
                    for k, v in waits:
                        eng.wait_ge(sems[k], v)
                    if fn is None:
                        continue
                    if isinstance(fn, tuple):
                        ins = fn[0](eng)
                        ins.annotate(fn[1])
                    else:
                        ins = fn(eng)
                    ins.then_inc(sems[inc[0]], inc[1])
            return run
        block.tensor(replay("pe"))
        block.scalar(replay("act"))
        block.vector(replay("dve"))
        block.gpsimd(replay("pool"))
        block.sync(replay("sp"))
    es.close()
    return nc


def consts_np():
    i = np.arange(128)
    ident = np.eye(128, dtype=np.float32)
    uti = (i[:, None] <= i[None, :]).astype(np.float32)
    negsu = np.where(i[:, None] < i[None, :], 0.0, -30000.0).astype(np.float32)
    possl = np.where(i[:, None] > i[None, :], 0.0, 30000.0).astype(np.float32)
    ones = np.ones((128, 128), np.float32)
    return np.ascontiguousarray(np.stack([ident, uti, negsu, possl, ones], axis=1))


def make_in_maps(inputs, W, OWN, core_tokens):
    f = lambda a: np.ascontiguousarray(np.asarray(a, dtype=np.float32))
    x = f(inputs["x"])
    bc = lambda v: np.ascontiguousarray(np.broadcast_to(f(v).reshape(1, -1), (128, f(v).size)))
    cw = lambda w, nt: np.ascontiguousarray(f(w).reshape(w.shape[-2], nt, 128).transpose(2, 1, 0))
    shared = {
        "w_in": f(inputs["w_in"][0]), "w_a": f(inputs["w_a_out"][0]), "w_b": f(inputs["w_b_out"][0]),
        "w_o": f(inputs["w_o"][0]), "w_up": f(inputs["w_up"][0]), "w_dn": f(inputs["w_down"][0]),
        "gmixB": bc(inputs["norm_mix_g"][0]), "gffnB": bc(inputs["norm_ffn_g"][0]), "gfinB": bc(inputs["norm_final_g"]),
        "cwA": cw(np.asarray(inputs["conv_a_w"][0]), 8), "cwG": cw(np.asarray(inputs["gdn_conv_w"][0]), 24),
        "cwF": cw(np.asarray(inputs["ffn_conv_w"][0]), 44),
        "alogB": np.ascontiguousarray(np.tile(bc(inputs["gdn_A_log"][0]), (1, 4))),
        "dtbB": np.ascontiguousarray(np.tile(bc(inputs["gdn_dt_bias"][0]), (1, 4))),
        "ngP": f(inputs["gdn_norm_g"][0]).reshape(128, 1),
        "cst": consts_np(),
    }
    maps = []
    for (b, s) in core_tokens:
        end = s + OWN
        xwin = np.zeros((W, D), np.float32)
        lo = max(0, end - W)
        xwin[W - (end - lo):] = x[b, lo:end]
        m = dict(shared)
        m["xw"] = xwin
        maps.append(m)
    return maps


def kernel(**inputs):
    x = np.asarray(inputs["x"])
    Bsz, S, _ = x.shape
    core_tokens = [(c // 4, (c % 4) * OWN_FULL) for c in range(NCORES)]
    nc = build(SEQ, OWN_FULL)
    maps = make_in_maps(inputs, SEQ, OWN_FULL, core_tokens)
    res = run_bass_kernel_spmd(nc, maps, core_ids=list(range(NCORES)))
    out = np.zeros((Bsz, S, D), np.float32)
    for c, (b, s) in enumerate(core_tokens):
        out[b, s:s + OWN_FULL] = np.asarray(res.results[c]["out"])
    return out
```

```python
import numpy as np
from contextlib import ExitStack
import concourse.bass as bass
import concourse.mybir as mybir
from concourse.bass_utils import run_bass_kernel_spmd

F32 = mybir.dt.float32
BF16 = mybir.dt.bfloat16
AF = mybir.ActivationFunctionType
ALU = mybir.AluOpType

D = 1024
NH = 8
DFF = 2816
EPS = 1e-6
NCORES = 8
SEQ = 8192
OWN_FULL = 2048
SAME_SYNC = True
STAGGER = 11
A_PRIO = 8
DEBUG_TAGS = False

C_BG, C_CG, C_XV, C_Q, C_K, C_V, C_Z, C_A, C_B, C_GA, C_GB = 0, 1024, 2048, 3072, 4096, 5120, 6144, 7168, 7176, 7184, 8208


class Buf:
    __slots__ = ("name", "w", "rs", "psum")

    def __init__(self, name, psum=False):
        self.name = name
        self.w = None
        self.rs = []
        self.psum = psum


class Eng:
    def __init__(self, name):
        self.name = name
        self.ops = []
        self.count = 0
        self.known = {}


class Prog:
    def __init__(self):
        self.E = {n: Eng(n) for n in ("pe", "act", "dve", "pool", "sp")}
        self.dsem = {}

    def _deps(self, e, reads, writes):
        deps = {}

        def add(dep):
            if dep is None:
                return
            k, v = dep
            if deps.get(k, 0) < v:
                deps[k] = v

        for b in reads:
            add(b.w)
            if b.psum:
                for r in b.rs:
                    if r[0] != e.name:
                        add(r)
        for b in writes:
            add(b.w)
            for r in b.rs:
                add(r)
        waits = []
        for k, v in deps.items():
            if k == e.name and (k == "pe" or not SAME_SYNC):
                continue
            if e.known.get(k, 0) >= v:
                continue
            e.known[k] = v
            waits.append((k, v))
        return waits

    def op(self, eng, fn, reads=(), writes=()):
        e = self.E[eng]
        waits = self._deps(e, reads, writes)
        e.count += 1
        n = e.count
        if DEBUG_TAGS:
            import sys
            f = sys._getframe(1)
            tag = []
            for _ in range(4):
                if f is None:
                    break
                tag.append(str(f.f_lineno))
                f = f.f_back
            fn = (fn, "L" + "<".join(tag))
        e.ops.append((waits, fn, (eng, 1)))
        for b in writes:
            b.w = (eng, n)
            b.rs = []
        for b in reads:
            b.rs.append((eng, n))

    def dma(self, queue, fn, key, reads=(), writes=()):
        e = self.E[queue]
        waits = self._deps(e, reads, writes)
        self.dsem[key] = self.dsem.get(key, 0) + 16
        n = self.dsem[key]
        e.ops.append((waits, fn, (key, 16)))
        for b in writes:
            b.w = (key, n)
            b.rs = []
        for b in reads:
            b.rs.append((key, n))

    def final_wait(self, eng, deps):
        self.E[eng].ops.append((list(deps), None, None))


class _Stop(Exception):
    pass


def build(W=SEQ, OWN=OWN_FULL, dbg=None, stage=None, flags=()):
    def ck(n):
        if stage is not None and stage == n:
            raise _Stop()

    NT = W // 512
    T_OWN0 = (W - OWN) // 512
    NOWN = OWN // 512
    nc = bass.Bass("TRN2", target_bir_lowering=False)
    P = Prog()
    dbg = dbg or []
    dbg_out = {}

    def din(name, shape, dt=F32):
        return nc.dram_tensor(name, shape, dt, kind="ExternalInput").ap()

    xw = din("xw", [W, D])
    w_in = din("w_in", [D, 9232])
    w_a = din("w_a", [D, D])
    w_b = din("w_b", [D, D])
    w_o = din("w_o", [D, D])
    w_up = din("w_up", [D, 2 * DFF])
    w_dn = din("w_dn", [DFF, D])
    gmixB_d = din("gmixB", [128, D])
    gffnB_d = din("gffnB", [128, D])
    gfinB_d = din("gfinB", [128, D])
    cwA_d = din("cwA", [128, 8, 3])
    cwG_d = din("cwG", [128, 24, 4])
    cwF_d = din("cwF", [128, 44, 3])
    alog_d = din("alogB", [128, 32])
    dtb_d = din("dtbB", [128, 32])
    ng_d = din("ngP", [128, 1])
    cst_d = din("cst", [128, 5, 128])
    out_d = nc.dram_tensor("out", [OWN, D], F32, kind="ExternalOutput").ap()
    NUNIT2 = 37
    wsc = nc.dram_tensor("wsc", [8 + NUNIT2, 128, 8, 512], BF16, kind="Internal").ap()
    for name, shape in dbg:
        dbg_out[name] = nc.dram_tensor("dbg_" + name, shape, F32, kind="ExternalOutput").ap()

    es = ExitStack()
    SB_ACC = [0, []]
    build.sb_acc = SB_ACC

    class T:
        def __init__(self, name, shape, dt=F32):
            self.t = es.enter_context(nc.sbuf_tensor("sb_" + name, shape, dt))
            self.b = Buf(name)
            SB_ACC[0] += int(np.prod(shape[1:])) * (2 if dt == BF16 else 4)
            SB_ACC[1].append((name, int(np.prod(shape[1:])) * (2 if dt == BF16 else 4)))

        def __getitem__(self, k):
            return self.t[k]

    class PS:
        def __init__(self, name, shape, dt=F32):
            self.t = es.enter_context(nc.psum_tensor(name, shape, dt))
            self.b = Buf(name, psum=True)

        def __getitem__(self, k):
            return self.t[k]

    class View:
        def __init__(self, ap, name):
            self.ap = ap
            self.b = Buf(name)

        def __getitem__(self, k):
            return self.ap[k]

    arena = T("arena", [128, 11264], BF16)
    arena2 = T("arena2", [128, 12288], BF16)
    arenas = {1: (arena, 11264, [0]), 2: (arena2, 12288, [0])}

    def AV(name, shape, dt=F32, which=1):
        ar, cap, aoff = arenas[which]
        n = int(np.prod(shape[1:]))
        nb = n * (1 if dt == BF16 else 2)
        ap = ar.t[:, aoff[0]:aoff[0] + nb]
        aoff[0] += nb
        assert aoff[0] <= cap
        if dt != BF16:
            ap = ap.bitcast(F32)
        if len(shape) == 3:
            ap = ap.rearrange("p (a b) -> p a b", a=shape[1])
        return View(ap, name)

    def barrier():
        for en, e in P.E.items():
            waits = [(k_, e2.count) for k_, e2 in P.E.items() if k_ != en and e2.count > 0]
            waits += [(k_, v_) for k_, v_ in P.dsem.items() if not k_.startswith("wc")]
            e.ops.append((waits, None, None))
            for k_, v_ in waits:
                e.known[k_] = max(e.known.get(k_, 0), v_)

    def B(xs):
        return [x.b if hasattr(x, "b") else x for x in xs]

    def ACT(out, in_, func, r, w, bias=0.0, scale=1.0, accum=None):
        if accum is None:
            P.op("act", lambda e: e.activation(out=out, in_=in_, func=func, bias=bias, scale=scale), B(r), B(w))
        else:
            P.op("act", lambda e: e.activation(out=out, in_=in_, func=func, bias=bias, scale=scale, accum_out=accum), B(r), B(w))

    def TS(eng, out, in0, s1, s2, op0, op1, r, w):
        if op1 is None:
            P.op(eng, lambda e: e.tensor_scalar(out=out, in0=in0, scalar1=s1, scalar2=None, op0=op0), B(r), B(w))
        else:
            P.op(eng, lambda e: e.tensor_scalar(out=out, in0=in0, scalar1=s1, scalar2=s2, op0=op0, op1=op1), B(r), B(w))

    def STT(eng, out, in0, scalar, in1, op0, op1, r, w):
        P.op(eng, lambda e: e.scalar_tensor_tensor(out=out, in0=in0, scalar=scalar, in1=in1, op0=op0, op1=op1), B(r), B(w))

    def TT(eng, out, in0, in1, op, r, w):
        P.op(eng, lambda e: e.tensor_tensor(out=out, in0=in0, in1=in1, op=op), B(r), B(w))

    def CP(eng, out, in_, r, w):
        if eng == "act":
            P.op("act", lambda e: e.copy(out=out, in_=in_), B(r), B(w))
        else:
            P.op(eng, lambda e: e.tensor_copy(out=out, in_=in_), B(r), B(w))

    def MM(mms, r, w):
        def fn(e):
            ins = None
            for (o, l, rh, st, sp) in mms:
                ins = e.matmul(o, l, rh, start=st, stop=sp)
            return ins
        P.op("pe", fn, B(r), B(w))

    def TR(trs, r, w):
        def fn(e):
            ins = None
            for (o, i, idn) in trs:
                ins = e.transpose(o, i, idn)
            return ins
        P.op("pe", fn, B(r), B(w))

    def DMA(queue, out, in_, key, r, w):
        q = {"sp": "sp", "pool": "pool", "act": "act"}[queue]
        P.dma(q, lambda e: e.dma_start(out=out, in_=in_), key, B(r), B(w))

    cst = T("cst", [128, 5, 128])
    cstb = T("cstb", [128, 128], BF16)
    cstb2 = T("cstb2", [128, 2, 128], BF16)
    gphls = [T(f"gphl{i}", [128, 2, 32], BF16) for i in range(2)]
    gmixB = T("gmixB", [128, D])
    gffnB = T("gffnB", [128, D])
    gfinB = T("gfinB", [128, D])
    cwA = T("cwA", [128, 8, 3])
    cwG = T("cwG", [128, 24, 4])
    cwF = T("cwF", [128, 44, 3])
    negA = T("negA", [128, 32])
    dtb = T("dtb", [128, 32])
    ngP = T("ngP", [128, 1])
    wab = T("wab", [128, 8, 16], BF16)
    IDF = cst[:, 0, :]
    UTI = cst[:, 1, :]
    NEGSU = cst[:, 2, :]
    POSSL = cst[:, 3, :]
    ONES = cst[:, 4, :]
    IDB = cstb[:, :]

    xt = [T(f"xt{i}", [128, D]) for i in range(4)]
    xn = T("xn", [128, D], BF16)
    junk = T("junk", [128, D], BF16)
    st_small = [T(f"st{i}", [128, 4]) for i in range(2)]
    hT = T("hT", [128, 8, 512], BF16)
    h2T = hT
    wslot = [T(f"wslot{i}", [128, 8, 512], BF16) for i in range(3)]
    qkvw = [T(f"qkvw{i}", [128, 8, 384], BF16) for i in range(2)]
    pre = [T(f"pre{i}", [128, 515]) for i in range(2)]
    cacc = T("cacc", [128, 512])
    post = T("post", [128, 512])
    sqb = T("sqb", [128, 512])
    rr = T("rr", [128, 512])
    NSETS = 5
    qTs = [T(f"qT{i}", [128, 512], BF16) for i in range(NSETS)]
    kTs = [T(f"kT{i}", [128, 512], BF16) for i in range(NSETS)]
    vTs = [T(f"vT{i}", [128, 512], BF16) for i in range(NSETS)]
    oTs = [T(f"oT{i}", [128, 512]) for i in range(2)] + [AV("oT2", [128, 512], F32, 2)]
    oT = oTs[0]
    carG = T("carG", [128, 24, 3])
    carA = T("carA", [128, 8, 2])
    carF = T("carF", [128, 44, 2])
    betaPs = [T(f"betaP{i}", [128, 32]) for i in range(2)]
    gPs = [T(f"gP{i}", [128, 32]) for i in range(2)]
    GPs = [T(f"GP{i}", [128, 32]) for i in range(2)]
    kegP = T("kegP", [128, 32])
    s1Ps = [T(f"s1P{i}", [128, 32]) for i in range(2)]
    tmpP = T("tmpP", [128, 32])
    NPIPE = 3
    def PT(p, name, shape, dt=F32):
        return T(name, shape, dt) if p == 0 else AV(name, shape, dt, p)

    Fs = [[PT(p, f"F{p}_{j}", [128, 512]) for j in range(3)] for p in range(NPIPE)]
    gm0_ = T("gm0", [128, 4, 4, 128], BF16)
    gm = [gm0_ for p in range(NPIPE)]
    Vb = [[PT(p, f"V{p}_{j}", [128, 512], BF16) for j in range(2)] for p in range(NPIPE)]
    VTb = [[PT(p, f"VT{p}_{j}", [128, 512], BF16) for j in range(2)] for p in range(NPIPE)]
    Rb = [[PT(p, f"R{p}_{j}", [128, 512], BF16) for j in range(2)] for p in range(NPIPE)]
    ATb = [PT(p, f"AT{p}", [128, 512], BF16) for p in range(NPIPE)]
    QATb = [PT(p, f"QAT{p}", [128, 512], BF16) for p in range(NPIPE)]
    vbb = [PT(p, f"vb{p}", [128, 4, 128], BF16) for p in range(NPIPE)]
    kbgb = [PT(p, f"kbg{p}", [128, 4, 128], BF16) for p in range(NPIPE)]
    kdecb = [PT(p, f"kdec{p}", [128, 4, 128], BF16) for p in range(NPIPE)]
    ub = [PT(p, f"u{p}", [128, 512], BF16) for p in range(NPIPE)]
    wb = [PT(p, f"w{p}", [128, 512], BF16) for p in range(NPIPE)]
    nKWTb = [PT(p, f"nKWT{p}", [128, 512], BF16) for p in range(NPIPE)]
    kbTs = [PT(p, f"kbT{p}", [128, 512], BF16) for p in range(NPIPE)]
    sm4 = [T(f"sm4_{p}", [128, 16]) for p in range(NPIPE)]
    bphls = [T(f"bphl{i}", [128, 2, 32], BF16) for i in range(2)]
    Sf = [T(f"Sf{h}", [128, 128]) for h in range(NH)]
    Sb = [[T(f"Sb{h}_{j}", [128, 128], BF16) for j in range(2)] for h in range(NH)]
    sb_par = [0] * NH
    ob = T("ob", [128, 8, 512], BF16)
    yap = View(arena2.t[:, 0:4096].rearrange("p (f n) -> p f n", f=8), "yap")
    sz = View(arena2.t[:, 4096:8192].rearrange("p (f n) -> p f n", f=8), "sz")
    mixT = View(arena2.t[:, 8192:12288].rearrange("p (f n) -> p f n", f=8), "mixT")
    sga = rr
    sgb = oT
    cgs = cacc
    uw = pre
    actT = View(arena.t[:, :].rearrange("p (f n) -> p f n", f=22), "actT")
    cvg = post
    cvv = sqb
    yout = [T(f"yout{i}", [128, D]) for i in range(1)]

    class Slot:
        def __init__(self, ap, b):
            self.ap = ap
            self.b = b

        def __getitem__(self, k):
            return self.ap[k]

    ps_proj = [PS(f"ps_proj{i}", [128, 512]) for i in range(2)]
    ps_small = ps_proj[0]
    ps_gb = [Slot(ps_small[:, 128:384], ps_small.b) for i in range(2)]
    ps_mm_t = [PS(f"ps_mm{i}", [128, 512]) for i in range(6)]


    mm_slots = [Slot(ps_mm_t[i][:, :], ps_mm_t[i].b) for i in range(6)]
    bank_free = list(mm_slots)
    mm_i = [0]

    def mslot():
        s = mm_slots[mm_i[0] % 3]
        mm_i[0] += 1
        return s

    proj_i = [0]

    in_phase2 = [False]

    def pslot():
        banks = (ps_proj + mm_slots) if in_phase2[0] else ps_proj
        s = banks[proj_i[0] % len(banks)]
        proj_i[0] += 1
        return s

    cl = [(cst, cst_d), (gmixB, gmixB_d), (gffnB, gffnB_d), (gfinB, gfinB_d), (cwA, cwA_d), (cwG, cwG_d),
          (cwF, cwF_d), (negA, alog_d), (dtb, dtb_d), (ngP, ng_d)]
    for t, d_ in cl:
        DMA("sp", t[:], d_, "const", [], [t])
    for t, d_ in cl:
        t.b.w = ("const", P.dsem["const"])
    CP("dve", cstb[:, :], cst[:, 0, :], [cst], [cstb])
    CP("dve", cstb2[:, 0, :], cst[:, 1, :], [cst], [cstb2])
    CP("dve", cstb2[:, 1, :], cst[:, 4, :], [cst], [cstb2])
    ACT(negA[:, :], negA[:, :], AF.Exp, [negA], [negA])
    TS("dve", negA[:, :], negA[:, :], -1.0, None, ALU.mult, None, [negA], [negA])
    for t in (carG, carA, carF):
        P.op("pool", lambda e, t=t: e.memset(t[:], 0.0), [], [t.b])
    for h in range(NH):
        P.op("pool", lambda e, h=h: e.memset(Sf[h][:, :], 0.0), [], [Sf[h].b])
        P.op("pool", lambda e, h=h: e.memset(Sb[h][0][:, :], 0.0), [], [Sb[h][0].b])

    wsc_b = [Buf(f"wsc{u}") for u in range(8 + NUNIT2)]
    stopped = [False]

    uext = {}

    def cast_piece(u, col0, src, r0, nk, c0, width, key="wcast"):
        uext[u] = (nk, max(uext.get(u, (0, 0))[1], col0 + width))
        s = src[r0:r0 + nk * 128, c0:c0 + width].rearrange("(k p) n -> p k n", p=128)
        cast_dma(wsc[u, :, 0:nk, col0:col0 + width], s)

    cast_n = [0]
    dummy = T("dummy", [128, 4])

    lazy_casts = []
    lazy_mode = [False]

    def cast_dma(out, in_):
        if "nocast" in flags:
            return
        if lazy_mode[0]:
            lazy_casts.append((out, in_))
            return
        key = f"wc{cast_n[0] % 6}"
        cast_n[0] += 1
        if P.dsem.get(key, 0) > 0:
            P.E["pool"].ops.append(([(key, P.dsem[key])], None, None))
        DMA("pool", out, in_, key, [], [])

    def emit_lazy_casts(k):
        lazy_mode[0] = False
        for _ in range(min(k, len(lazy_casts))):
            o_, i_ = lazy_casts.pop(0)
            cast_dma(o_, i_)
        if not lazy_casts and not casts_done[0]:
            casts_done[0] = True
            cast_barrier([wsc_b[uu] for uu in range(8, 8 + NUNIT2)])

    casts_done = [False]

    def cast_barrier(bufs):
        fake = []
        for j in range(6):
            key = f"wc{j}"
            if P.dsem.get(key, 0) > 0:
                fb = Buf("fk")
                fb.w = (key, P.dsem[key])
                fake.append(fb)
        P.op("pool", lambda e: e.memset(dummy[:, :], 0.0), fake, [dummy.b])
        for b_ in bufs:
            b_.w = dummy.b.w

    cast_dma(wab[:, :, :], w_in[:, C_A:C_A + 16].rearrange("(k p) n -> p k n", p=128))
    for h in range(NH):
        for j, cb in enumerate((C_Q, C_K, C_V)):
            cast_piece(h, j * 128, w_in, 0, 8, cb + h * 128, 128, key="wcast0")
    cast_barrier([wab.b] + [wsc_b[h] for h in range(NH)])
    lazy_mode[0] = True
    units = []
    u = 8
    U_A = []
    for ct in range(8):
        for j, cb in enumerate((C_BG, C_CG, C_XV)):
            cast_piece(u, j * 128, w_in, 0, 8, cb + ct * 128, 128)
        U_A.append(u); u += 1
    U_Z = []
    for i in range(2):
        cast_piece(u, 0, w_in, 0, 8, C_Z + i * 512, 512)
        U_Z.append(u); u += 1
    U_G = []
    for nt in range(8):
        cast_piece(u, 0, w_in, 0, 8, C_GA + nt * 128, 128)
        cast_piece(u, 128, w_in, 0, 8, C_GB + nt * 128, 128)
        cast_piece(u, 256, w_a, 0, 8, nt * 128, 128)
        cast_piece(u, 384, w_b, 0, 8, nt * 128, 128)
        U_G.append(u); u += 1
    U_O = []
    for i in range(2):
        cast_piece(u, 0, w_o, 0, 8, i * 512, 512)
        U_O.append(u); u += 1
    U_UP = []
    for fp in range(11):
        for j in range(2):
            ft = fp * 2 + j
            cast_piece(u, j * 256, w_up, 0, 8, ft * 128, 128)
            cast_piece(u, j * 256 + 128, w_up, 0, 8, DFF + ft * 128, 128)
        U_UP.append(u); u += 1
    U_D = []
    for nh in range(2):
        row = []
        for kg, (k0, nk) in enumerate(((0, 8), (8, 8), (16, 6))):
            cast_piece(u, 0, w_dn, k0 * 128, nk, nh * 512, 512)
            row.append((u, nk)); u += 1
        U_D.append(row)
    assert u == 8 + NUNIT2
    lazy_mode[0] = False

    ws_state = {"q": [], "n": 0}

    def ws_issue(unit):
        i = ws_state["n"] % 3
        ws_state["n"] += 1
        nk, ncol = uext[unit]
        DMA("sp", wslot[i][:, 0:nk, 0:ncol], wsc[unit, :, 0:nk, 0:ncol], f"ws{i}", [wsc_b[unit]], [wslot[i]])
        return wslot[i]

    class WStream:
        def __init__(self, seq, ahead=2):
            self.seq = list(seq)
            self.loaded = []
            self.pos = 0
            self.ahead = ahead

        def get(self, ahead=None):
            ahead = self.ahead if ahead is None else ahead
            while len(self.loaded) < min(len(self.seq), self.pos + 1 + ahead):
                self.loaded.append(ws_issue(self.seq[len(self.loaded)]))
            s = self.loaded[self.pos]
            self.pos += 1
            return s

    sidx = [0]

    def norm_to_T(src_tile, gB, dstT, sub, bank=None):
        bank = bank if bank is not None else bank_free[0]
        ps_tr = Slot(bank[:, 0:512].bitcast(BF16), bank.b)
        st = st_small[sidx[0] % 2]
        sidx[0] += 1
        ACT(junk[:, :], src_tile[:, :], AF.Square, [src_tile], [junk, st], accum=st[:, 0:1])
        ACT(st[:, 1:2], st[:, 0:1], AF.Ln, [st], [st], bias=EPS, scale=1.0 / D)
        ACT(st[:, 2:3], st[:, 1:2], AF.Exp, [st], [st], scale=-0.5)
        STT("dve", xn[:, :], src_tile[:, :], st[:, 2:3], gB[:, :], ALU.mult, ALU.mult, [src_tile, st, gB], [xn])
        TR([(ps_tr[:, k * 128:(k + 1) * 128], xn[:, k * 128:(k + 1) * 128], IDB) for k in range(8)], [xn, cstb], [ps_tr])
        CP("act", dstT[:, :, sub * 128:(sub + 1) * 128], ps_tr[:, :].rearrange("p (k n) -> p k n", k=8), [ps_tr], [dstT])
        return st

    def dump(name, ap, r):
        if name in dbg_out:
            DMA("sp", dbg_out[name], ap, "dbg", r, [])

    def dwconv(work, carry_ap, K, wts, n, outp, r_w, w_out, car_t):
        CP("pool", work[:, 0:K - 1], carry_ap, [car_t], [work])
        P.op("act", lambda e: e.mul(out=outp, in_=work[:, 0:n], mul=wts(0)), B([work] + r_w), B(w_out))
        for j in range(1, K):
            STT("dve", outp, work[:, j:j + n], wts(j), outp, ALU.mult, ALU.add, [work] + r_w + w_out, w_out)
        CP("pool", carry_ap, work[:, n:n + K - 1], [work], [car_t])

    cb_i = [0]
    bc_i = lambda ap4: ap4.unsqueeze(2).broadcast_to([128, 4, 128])
    bc_c = lambda ap: ap.unsqueeze(1).broadcast_to([128, 4, 128])
    v4 = lambda ap: ap.rearrange("p (c i) -> p c i", c=4)
    UTIB = cstb2[:, 0, :]
    ONESB = cstb2[:, 1, :]

    def acquire(n):
        while len(bank_free) < n:
            yield
        return [bank_free.pop(0) for _ in range(n)]

    def release(*bs):
        for b_ in bs:
            bank_free.append(b_)

    def stageA(tt, h, wq, qs):
        qT, kT, vT = qTs[qs], kTs[qs], vTs[qs]
        cc = [cacc, post, sqb]
        for j in range(3):
            ps = pslot()
            MM([(ps[:, :], wq[:, k, j * 128:(j + 1) * 128], hT[:, k, :], k == 0, k == 7) for k in range(8)], [wq, hT], [ps])
            pw = pre[(cb_i[0]) % 2]
            cb_i[0] += 1
            CP("act", pw[:, 3:515], ps[:, :], [ps], [pw])
            ch = j * 8 + h
            dwconv(pw, carG[:, ch, :], 4, lambda jj, ch=ch: cwG[:, ch, jj:jj + 1], 512, cc[j][:, :], [cwG], [cc[j]], carG)
            yield
        ACT(cc[0][:, :], cc[0][:, :], AF.Silu, [cc[0]], [cc[0]])
        ACT(cc[1][:, :], cc[1][:, :], AF.Silu, [cc[1]], [cc[1]])
        ACT(vT[:, :], cc[2][:, :], AF.Silu, [cc[2]], [vT])
        yield
        for src, dst, scale in ((cc[0], qT, 128 ** -0.5), (cc[1], kT, 1.0)):
            ACT(junk[:, 0:512], src[:, :], AF.Square, [src], [junk])
            ps2 = pslot()
            MM([(ps2[:, :], ONESB, junk[:, 0:512], True, True)], [junk, cstb2], [ps2])
            ACT(rr[:, :], ps2[:, :], AF.Ln, [ps2], [rr], bias=EPS)
            ACT(rr[:, :], rr[:, :], AF.Exp, [rr], [rr], scale=-0.5)
            STT("dve", dst[:, :], src[:, :], scale, rr[:, :], ALU.mult, ALU.mult, [src, rr], [dst])
            yield

    def stageC(tt, h, qs, p=0):
        qT, kT, vT = qTs[qs], kTs[qs], vTs[qs]
        sl = slice(h * 4, h * 4 + 4)
        F0, F1, F2 = Fs[p]
        G, V, VT, R = gm[p], Vb[p], VTb[p], Rb[p]
        AT, QAT, vb, kbg, kdec, u_, w_, nK, sm = ATb[p], QATb[p], vbb[p], kbgb[p], kdecb[p], ub[p], wb[p], nKWTb[p], sm4[p]
        kbT, oT = kbTs[p], oTs[p]
        tp = tt % 2
        betaP, GP, s1P, gphl, bphl = betaPs[tp], GPs[tp], s1Ps[tp], gphls[tp], bphls[tp]
        f2 = lambda t3: t3.rearrange("p c i -> p (c i)")
        cs = lambda c: slice(c * 128, (c + 1) * 128)
        bG, bB = yield from acquire(2)
        TT("pool", G[:, 0], bc_c(UTIB), bc_i(gphl[:, 0, sl]), ALU.mult, [cstb2, gphl], [G])
        TT("pool", G[:, 1], bc_c(UTIB), bc_i(gphl[:, 1, sl]), ALU.mult, [cstb2, gphl], [G])
        TT("pool", G[:, 2], bc_c(IDB), bc_i(bphl[:, 0, sl]), ALU.mult, [cstb, bphl], [G])
        TT("pool", G[:, 3], bc_c(IDB), bc_i(bphl[:, 1, sl]), ALU.mult, [cstb, bphl], [G])
        MM([(bG[:, 0:512], ONESB, f2(G[:, 0]), True, False), (bG[:, 0:512], ONESB, f2(G[:, 1]), False, True)], [G, cstb2], [bG])
        MM([(bB[:, 0:512], ONESB, f2(G[:, 2]), True, False), (bB[:, 0:512], ONESB, f2(G[:, 3]), False, True)], [G, cstb2], [bB])
        yield
        TT("dve", v4(F0[:, :]), v4(bG[:, 0:512]), bc_i(GP[:, sl]), ALU.subtract, [bG, GP], [F0])
        ACT(F2[:, :], bG[:, 0:512], AF.Exp, [bG], [F2])
        CP("act", sm[:, 0:4], v4(bG[:, 0:512])[:, :, 127], [bG], [sm])
        TT("dve", kbT[:, :], kT[:, :], bB[:, 0:512], ALU.mult, [kT, bB], [kbT])
        release(bG, bB)
        yield
        TT("dve", v4(F1[:, :]), v4(F0[:, :]), bc_c(POSSL), ALU.add, [F0, cst], [F1])
        TT("pool", v4(F0[:, :]), v4(F0[:, :]), bc_c(NEGSU), ALU.add, [F0, cst], [F0])
        TT("dve", sm[:, 4:8], sm[:, 0:4], GP[:, sl], ALU.subtract, [sm, GP], [sm])
        yield
        ACT(F1[:, :], F1[:, :], AF.Exp, [F1], [F1], scale=-1.0)
        ACT(F0[:, :], F0[:, :], AF.Exp, [F0], [F0])
        ACT(sm[:, 8:12], sm[:, 4:8], AF.Exp, [sm], [sm])
        ACT(sm[:, 12:16], sm[:, 0:4], AF.Exp, [sm], [sm])
        TT("pool", F2[:, :], qT[:, :], F2[:, :], ALU.mult, [qT, F2], [F2])
        bL, bU, bA = yield from acquire(3)
        MM([(bL[:, cs(c)], kbT[:, cs(c)], kT[:, cs(c)], True, True) for c in range(4)], [kbT, kT], [bL])
        MM([(bU[:, cs(c)], kT[:, cs(c)], kbT[:, cs(c)], True, True) for c in range(4)], [kbT, kT], [bU])
        MM([(bA[:, cs(c)], kT[:, cs(c)], qT[:, cs(c)], True, True) for c in range(4)], [qT, kT], [bA])
        yield
        STT("dve", VT[0][:, :], bL[:, 0:512], -1.0, F1[:, :], ALU.mult, ALU.mult, [bL, F1], [VT[0]])
        STT("dve", V[0][:, :], bU[:, 0:512], -1.0, F0[:, :], ALU.mult, ALU.mult, [bU, F0], [V[0]])
        TT("pool", v4(F0[:, :]), v4(F0[:, :]), bc_c(IDF), ALU.add, [F0, cst], [F0])
        TT("dve", AT[:, :], bA[:, 0:512], F0[:, :], ALU.mult, [bA, F0], [AT])
        TT("pool", v4(R[0][:, :]), v4(V[0][:, :]), bc_c(IDB), ALU.add, [V[0], cstb], [R[0]])
        release(bL, bU, bA)
        (bX,) = yield from acquire(1)
        bXb = bX[:, 0:512].bitcast(BF16)
        TR([(bXb[:, cs(c)], kT[:, cs(c)], IDB) for c in range(4)] +
           [(bXb[:, 512 + c * 128:512 + (c + 1) * 128], vT[:, cs(c)], IDB) for c in range(4)], [kT, vT, cstb], [bX])
        bT, bV = yield from acquire(2)
        MM([(bT[:, cs(c)], V[0][:, cs(c)], VT[0][:, cs(c)], True, True) for c in range(4)], [V[0], VT[0]], [bT])
        MM([(bV[:, cs(c)], VT[0][:, cs(c)], V[0][:, cs(c)], True, True) for c in range(4)], [V[0], VT[0]], [bV])
        yield
        TT("dve", kbg[:, :, :], v4(bXb[:, 0:512]), bc_i(s1P[:, sl]), ALU.mult, [bX, s1P], [kbg])
        TT("dve", kdec[:, :, :], v4(bXb[:, 0:512]), bc_i(sm[:, 8:12]), ALU.mult, [bX, sm], [kdec])
        TT("dve", vb[:, :, :], v4(bXb[:, 512:1024]), bc_i(betaP[:, sl]), ALU.mult, [bX, betaP], [vb])
        release(bX)
        cur = 0
        for m in range(1, 7):
            nxt = 1 - cur
            CP("act", VT[nxt][:, :], bT[:, 0:512], [bT], [VT[nxt]])
            release(bT)
            if m < 6:
                CP("act", V[nxt][:, :], bV[:, 0:512], [bV], [V[nxt]])
                release(bV)
            if m > 1:
                CP("act", R[cur][:, :], bR[:, 0:512], [bR], [R[cur]])
                release(bR)
            yield
            (bR,) = yield from acquire(1)
            mmr = []
            for c in range(4):
                mmr.append((bR[:, cs(c)], IDB, R[cur][:, cs(c)], True, False))
                mmr.append((bR[:, cs(c)], VT[nxt][:, cs(c)], R[cur][:, cs(c)], False, True))
            MM(mmr, [cstb, R[cur], VT[nxt]], [bR])
            if m < 5:
                bT, bV = yield from acquire(2)
            elif m == 5:
                (bT,) = yield from acquire(1)
            if m < 6:
                MM([(bT[:, cs(c)], V[nxt][:, cs(c)], VT[nxt][:, cs(c)], True, True) for c in range(4)], [V[nxt], VT[nxt]], [bT])
            if m < 5:
                MM([(bV[:, cs(c)], VT[nxt][:, cs(c)], V[nxt][:, cs(c)], True, True) for c in range(4)], [V[nxt], VT[nxt]], [bV])
            cur = nxt
            yield
        CP("act", R[cur][:, :], bR[:, 0:512], [bR], [R[cur]])
        release(bR)
        Rf = R[cur]
        yield
        b0, b1 = yield from acquire(2)
        MM([(b0[:, cs(c)], Rf[:, cs(c)], vb[:, c, :], True, True) for c in range(4)], [Rf, vb], [b0])
        MM([(b1[:, cs(c)], Rf[:, cs(c)], kbg[:, c, :], True, True) for c in range(4)], [Rf, kbg], [b1])
        yield
        CP("act", u_[:, :], b0[:, 0:512], [b0], [u_])
        CP("act", w_[:, :], b1[:, 0:512], [b1], [w_])
        release(b0, b1)
        yield
        bK, bQ = yield from acquire(2)
        MM([(bK[:, cs(c)], w_[:, cs(c)], kdec[:, c, :], True, True) for c in range(4)], [w_, kdec], [bK])
        MM([(bQ[:, cs(c)], w_[:, cs(c)], AT[:, cs(c)], True, True) for c in range(4)], [w_, AT], [bQ])
        yield
        P.op("act", lambda e: e.mul(out=nK[:, :], in_=bK[:, 0:512], mul=-1.0), B([bK]), B([nK]))
        TT("dve", QAT[:, :], F2[:, :], bQ[:, 0:512], ALU.subtract, [F2, bQ], [QAT])
        release(bK, bQ)
        yield
        for c in range(4):
            so = sb_par[h]
            S_old = Sb[h][so]
            S_new = Sb[h][1 - so]
            sb_par[h] = 1 - so
            (bS,) = yield from acquire(1)
            MM([(bS[:, 128:256], kdec[:, c, :], u_[:, cs(c)], True, False),
                (bS[:, 128:256], nK[:, cs(c)], S_old[:, :], False, True),
                (bS[:, 0:128], S_old[:, :], QAT[:, cs(c)], True, False),
                (bS[:, 0:128], u_[:, cs(c)], AT[:, cs(c)], False, True)],
               [S_old, QAT, u_, AT, kdec, nK], [bS])
            yield
            gam = sm[:, 12 + c:13 + c]
            STT("dve", S_new[:, :], Sf[h][:, :], gam, bS[:, 128:256], ALU.mult, ALU.add, [Sf[h], sm, bS], [S_new])
            STT("dve", Sf[h][:, :], Sf[h][:, :], gam, bS[:, 128:256], ALU.mult, ALU.add, [Sf[h], sm, bS], [Sf[h]])
            CP("act", oT[:, cs(c)], bS[:, 0:128], [bS], [oT])
            release(bS)
            yield
        ACT(V[0][:, :], oT[:, :], AF.Square, [oT], [V[0]])
        yield
        (b2,) = yield from acquire(1)
        MM([(b2[:, 0:512], ONESB, V[0][:, :], True, True)], [V[0], cstb2], [b2])
        yield
        ACT(F1[:, :], b2[:, 0:512], AF.Ln, [b2], [F1], bias=EPS, scale=1.0 / 128)
        release(b2)
        ACT(F1[:, :], F1[:, :], AF.Exp, [F1], [F1], scale=-0.5)
        STT("dve", ob[:, h, :], oT[:, :], ngP[:, 0:1], F1[:, :], ALU.mult, ALU.mult, [oT, F1, ngP], [ob])
        yield

    def run_il(gens, prio=1):
        gens = [g for g in gens if g is not None]
        first = gens[0] if gens else None
        while gens:
            for g in list(gens):
                for _ in range(prio if g is first else 1):
                    try:
                        next(g)
                    except StopIteration:
                        gens.remove(g)
                        break

    def load_wq(h):
        wq = qkvw[h % 2]
        DMA("sp", wq[:, :, :], wsc[h, :, :, 0:384], f"qw{h % 2}", [wsc_b[h]], [wq])
        return wq

    x_loaded = set()

    def load_x(tt):
        if tt in x_loaded or tt >= NT:
            return
        x_loaded.add(tt)
        for sub in range(4):
            r0 = tt * 512 + sub * 128
            DMA("sp", xt[sub][:, :], xw[r0:r0 + 128, :], f"xl{sub}", [], [xt[sub]])

    def tile_start(tt):
        tp = tt % 2
        betaP, gP, GP, s1P, gphl, bphl = betaPs[tp], gPs[tp], GPs[tp], s1Ps[tp], gphls[tp], bphls[tp]
        load_x(tt)
        for sub in range(4):
            (bk,) = yield from acquire(1)
            norm_to_T(xt[sub], gmixB, hT, sub, bk)
            release(bk)
            yield
        if tt < T_OWN0 - 1:
            load_x(tt + 1)
        MM([(ps_small[:, c * 16:(c + 1) * 16], hT[:, k, c * 128:(c + 1) * 128], wab[:, k, :], k == 0, k == 7)
            for c in range(4) for k in range(8)], [hT, wab], [ps_small])
        pa = ps_small[:, 0:64].rearrange("p (c x) -> p c x", c=4)
        vch = lambda t: t[:, :].rearrange("p (c h) -> p c h", c=4)
        vhc = lambda t: t[:, :].rearrange("p (h c) -> p c h", c=4)
        ACT(vhc(betaP), pa[:, :, 8:16], AF.Exp, [ps_small], [betaP], scale=-1.0)
        ACT(betaP[:, :], betaP[:, :], AF.Ln, [betaP], [betaP], bias=1.0)
        ACT(betaP[:, :], betaP[:, :], AF.Exp, [betaP], [betaP], scale=-1.0)
        TT("dve", vch(tmpP), pa[:, :, 0:8], vch(dtb), ALU.add, [ps_small, dtb], [tmpP])
        ACT(tmpP[:, :], tmpP[:, :], AF.Exp, [tmpP], [tmpP])
        ACT(tmpP[:, :], tmpP[:, :], AF.Ln, [tmpP], [tmpP], bias=1.0)
        TT("dve", vhc(gP), vch(tmpP), vch(negA), ALU.mult, [tmpP, negA], [gP])
        CP("pool", gphl[:, 0, :], gP[:, :], [gP], [gphl])
        TT("pool", tmpP[:, :], gP[:, :], gphl[:, 0, :], ALU.subtract, [gP, gphl], [tmpP])
        CP("pool", gphl[:, 1, :], tmpP[:, :], [tmpP], [gphl])
        CP("pool", bphl[:, 0, :], betaP[:, :], [betaP], [bphl])
        TT("pool", tmpP[:, :], betaP[:, :], bphl[:, 0, :], ALU.subtract, [betaP, bphl], [tmpP])
        CP("pool", bphl[:, 1, :], tmpP[:, :], [tmpP], [bphl])
        MM([(ps_small[:, 64:96], cstb2[:, 0, :], gphl[:, 0, :], True, False), (ps_small[:, 64:96], cstb2[:, 0, :], gphl[:, 1, :], False, True)], [cstb2, gphl], [ps_small])
        CP("act", GP[:, :], ps_small[:, 64:96], [ps_small], [GP])
        ACT(kegP[:, :], ps_small[:, 64:96], AF.Exp, [ps_small], [kegP])
        TT("dve", s1P[:, :], kegP[:, :], betaP[:, :], ALU.mult, [kegP, betaP], [s1P])
        yield

    def gdn_tiles(tts):
        tts = list(tts)
        gl = [(tt, h) for tt in tts for h in range(NH)]
        doneA = set()
        doneC = set()

        def chainA():
            for gi, (tt, h) in enumerate(gl):
                while gi >= NSETS and gl[gi - NSETS] not in doneC:
                    yield
                if h == 0:
                    while any((tt - 2, hh) in gl and (tt - 2, hh) not in doneC for hh in range(NH)):
                        yield
                    yield from tile_start(tt)
                emit_lazy_casts(2)
                yield from stageA(tt, h, load_wq(h), gi % NSETS)
                doneA.add((tt, h))

        def chainC(p, delay):
            for _ in range(delay):
                yield
            for gi, (tt, h) in enumerate(gl):
                if gi % NPIPE != p:
                    continue
                while (tt, h) not in doneA:
                    yield
                while gi >= NH and gl[gi - NH] not in doneC:
                    yield
                yield from stageC(tt, h, gi % NSETS, p)
                doneC.add((tt, h))

        run_il([chainA()] + [chainC(p, p * STAGGER) for p in range(NPIPE)], prio=A_PRIO)

    out_sem_total = [0]

    def phase2(tt, c0, c1, store):
        emit_lazy_casts(10 ** 6)
        in_phase2[0] = True
        try:
            _phase2(tt, c0, c1, store)
        finally:
            in_phase2[0] = False

    def _phase2(tt, c0, c1, store):
        n = c1 - c0
        subs = list(range(c0 // 128, c1 // 128))
        seq = U_A + U_Z + U_G + U_O + U_UP + [uu for row in U_D for (uu, _) in row]
        wsm = WStream(seq)
        hs = hT[:, :, c0:c1]
        for ct in range(8):
            wu = wsm.get()
            psc = pslot()
            MM([(psc[:, 0:n], wu[:, k, 128:256], hT[:, k, c0:c1], k == 0, k == 7) for k in range(8)], [wu, hT], [psc])
            CP("act", cgs[:, 0:n], psc[:, 0:n], [psc], [cgs])
            psx = pslot()
            MM([(psx[:, 0:n], wu[:, k, 256:384], hT[:, k, c0:c1], k == 0, k == 7) for k in range(8)], [wu, hT], [psx])
            w_ = uw[ct % 2]
            TT("dve", w_[:, 2:2 + n], psx[:, 0:n], cgs[:, 0:n], ALU.mult, [psx, cgs], [w_])
            dwconv(w_, carA[:, ct, :], 3, lambda jj, ct=ct: cwA[:, ct, jj:jj + 1], n, cvg[:, 0:n], [cwA], [cvg], carA)
            psb = pslot()
            MM([(psb[:, 0:n], wu[:, k, 0:128], hT[:, k, c0:c1], k == 0, k == 7) for k in range(8)], [wu, hT], [psb])
            TT("dve", yap[:, ct, 0:n], psb[:, 0:n], cvg[:, 0:n], ALU.mult, [psb, cvg], [yap])
        for i2 in range(2):
            wu = wsm.get()
            for j in range(4):
                ct = i2 * 4 + j
                ps = pslot()
                MM([(ps[:, 0:n], wu[:, k, j * 128:(j + 1) * 128], hT[:, k, c0:c1], k == 0, k == 7) for k in range(8)], [wu, hT], [ps])
                ACT(sz[:, ct, 0:n], ps[:, 0:n], AF.Silu, [ps], [sz])
                TT("pool", sz[:, ct, 0:n], sz[:, ct, 0:n], ob[:, ct, c0:c1], ALU.mult, [sz, ob], [sz])
        for nt in range(8):
            wu = wsm.get()
            ps = pslot()
            MM([(ps[:, 0:n], wu[:, k, 0:128], hT[:, k, c0:c1], k == 0, k == 7) for k in range(8)], [wu, hT], [ps])
            ACT(sga[:, 0:n], ps[:, 0:n], AF.Sigmoid, [ps], [sga])
            ps = pslot()
            MM([(ps[:, 0:n], wu[:, k, 128:256], hT[:, k, c0:c1], k == 0, k == 7) for k in range(8)], [wu, hT], [ps])
            ACT(sgb[:, 0:n], ps[:, 0:n], AF.Sigmoid, [ps], [sgb])
            ps = pslot()
            MM([(ps[:, 0:n], wu[:, k, 256:384], yap[:, k, 0:n], k == 0, k == 7) for k in range(8)], [wu, yap], [ps])
            TT("dve", sga[:, 0:n], sga[:, 0:n], ps[:, 0:n], ALU.mult, [sga, ps], [sga])
            ps = pslot()
            MM([(ps[:, 0:n], wu[:, k, 384:512], sz[:, k, 0:n], k == 0, k == 7) for k in range(8)], [wu, sz], [ps])
            TT("dve", sgb[:, 0:n], sgb[:, 0:n], ps[:, 0:n], ALU.mult, [sgb, ps], [sgb])
            TT("pool", mixT[:, nt, 0:n], sga[:, 0:n], sgb[:, 0:n], ALU.add, [sga, sgb], [mixT])
        for nh in range(2):
            wu = wsm.get()
            for si, sub in enumerate(subs):
                ps = pslot()
                MM([(ps[:, :], mixT[:, k, si * 128:(si + 1) * 128], wu[:, k, :], k == 0, k == 7) for k in range(8)], [wu, mixT], [ps])
                TT("dve", xt[sub][:, nh * 512:(nh + 1) * 512], xt[sub][:, nh * 512:(nh + 1) * 512], ps[:, :], ALU.add, [xt[sub], ps], [xt[sub]])
        for si, sub in enumerate(subs):
            norm_to_T(xt[sub], gffnB, h2T, si)
        for fp in range(11):
            wu = wsm.get()
            for j in range(2):
                ft = fp * 2 + j
                psg = pslot()
                MM([(psg[:, 0:n], wu[:, k, j * 256:j * 256 + 128], h2T[:, k, 0:n], k == 0, k == 7) for k in range(8)], [wu, h2T], [psg])
                w0 = uw[0]
                CP("act", w0[:, 2:2 + n], psg[:, 0:n], [psg], [w0])
                dwconv(w0, carF[:, ft, :], 3, lambda jj, ft=ft: cwF[:, ft, jj:jj + 1], n, cvg[:, 0:n], [cwF], [cvg], carF)
                psv = pslot()
                MM([(psv[:, 0:n], wu[:, k, j * 256 + 128:j * 256 + 256], h2T[:, k, 0:n], k == 0, k == 7) for k in range(8)], [wu, h2T], [psv])
                w1 = uw[1]
                CP("act", w1[:, 2:2 + n], psv[:, 0:n], [psv], [w1])
                dwconv(w1, carF[:, 22 + ft, :], 3, lambda jj, ft=ft: cwF[:, 22 + ft, jj:jj + 1], n, cvv[:, 0:n], [cwF], [cvv], carF)
                ACT(cvg[:, 0:n], cvg[:, 0:n], AF.Silu, [cvg], [cvg])
                TT("dve", actT[:, ft, 0:n], cvg[:, 0:n], cvv[:, 0:n], ALU.mult, [cvg, cvv], [actT])
        for nh in range(2):
            wus = [(wsm.get(ahead=0), nk) for (_, nk) in U_D[nh]]
            if not store:
                continue
            for si, sub in enumerate(subs):
                ps = pslot()
                mms = []
                for kg, (wu, nk) in enumerate(wus):
                    for k in range(nk):
                        kk = kg * 8 + k
                        mms.append((ps[:, :], actT[:, kk, si * 128:(si + 1) * 128], wu[:, k, :], kk == 0, kk == 21))
                MM(mms, [actT] + [w for w, _ in wus], [ps])
                TT("dve", xt[sub][:, nh * 512:(nh + 1) * 512], xt[sub][:, nh * 512:(nh + 1) * 512], ps[:, :], ALU.add, [xt[sub], ps], [xt[sub]])
        if store:
            for si, sub in enumerate(subs):
                st = st_small[sidx[0] % 2]
                sidx[0] += 1
                yo = yout[0]
                ACT(junk[:, :], xt[sub][:, :], AF.Square, [xt[sub]], [junk, st], accum=st[:, 0:1])
                ACT(st[:, 1:2], st[:, 0:1], AF.Ln, [st], [st], bias=EPS, scale=1.0 / D)
                ACT(st[:, 2:3], st[:, 1:2], AF.Exp, [st], [st], scale=-0.5)
                STT("dve", yo[:, :], xt[sub][:, :], st[:, 2:3], gfinB[:, :], ALU.mult, ALU.mult, [xt[sub], st, gfinB], [yo])
                row0 = (tt - T_OWN0) * 512 + sub * 128
                DMA("sp", out_d[row0:row0 + 128, :], yo[:, :], "store0", [yo], [])

    def main_loop():
        ck(10)
        n_cont = max(T_OWN0 - 1, 0)
        if n_cont > 0:
            gdn_tiles(range(0, n_cont))
        for tt in range(n_cont, NT):
            gdn_tiles([tt])
            ck(70)
            if tt == T_OWN0 - 1:
                barrier()
                phase2(tt, 384, 512, False)
                barrier()
                ck(80)
            elif tt >= T_OWN0:
                barrier()
                phase2(tt, 0, 512, True)
                barrier()

    try:
        main_loop()
    except _Stop:
        pass

    fw = [(k_, v_) for k_, v_ in P.dsem.items() if k_.startswith("store")]
    if "dbg" in P.dsem:
        fw.append(("dbg", P.dsem["dbg"]))
    for k_ in P.dsem:
        if k_.startswith("xl") or k_.startswith("qw") or k_.startswith("ws") or k_ == "const":
            fw.append((k_, P.dsem[k_]))
    P.final_wait("sp", fw)
    P.final_wait("sp", [(k_, e_.count) for k_, e_ in P.E.items() if e_.count > 0])

    keys = list(P.E.keys()) + list(P.dsem.keys())
    sems = {k: es.enter_context(nc.semaphore("s_" + k)) for k in keys}
    with nc.Block() as block:
        def replay(name):
            def run(eng):
                for waits, fn, inc in P.E[name].ops:
                    for k, v in waits:
                        eng.wait_ge(sems[k], v)
                    if fn is None:
                        continue
                    if isinstance(fn, tuple):
                        ins = fn[0](eng)
                        ins.annotate(fn[1])
                    else:
                        ins = fn(eng)
                    ins.then_inc(sems[inc[0]], inc[1])
            return run
        block.tensor(replay("pe"))
        block.scalar(replay("act"))
        block.vector(replay("dve"))
        block.gpsimd(replay("pool"))
        block.sync(replay("sp"))
    es.close()
    return nc


def consts_np():
    i = np.arange(128)
    ident = np.eye(128, dtype=np.float32)
    uti = (i[:, None] <= i[None, :]).astype(np.float32)
    negsu = np.where(i[:, None] < i[None, :], 0.0, -30000.0).astype(np.float32)
    possl = np.where(i[:, None] > i[None, :], 0.0, 30000.0).astype(np.float32)
    ones = np.ones((128, 128), np.float32)
    return np.ascontiguousarray(np.stack([ident, uti, negsu, possl, ones], axis=1))


def make_in_maps(inputs, W, OWN, core_tokens):
    f = lambda a: np.ascontiguousarray(np.asarray(a, dtype=np.float32))
    x = f(inputs["x"])
    bc = lambda v: np.ascontiguousarray(np.broadcast_to(f(v).reshape(1, -1), (128, f(v).size)))
    cw = lambda w, nt: np.ascontiguousarray(f(w).reshape(w.shape[-2], nt, 128).transpose(2, 1, 0))
    shared = {
        "w_in": f(inputs["w_in"][0]), "w_a": f(inputs["w_a_out"][0]), "w_b": f(inputs["w_b_out"][0]),
        "w_o": f(inputs["w_o"][0]), "w_up": f(inputs["w_up"][0]), "w_dn": f(inputs["w_down"][0]),
        "gmixB": bc(inputs["norm_mix_g"][0]), "gffnB": bc(inputs["norm_ffn_g"][0]), "gfinB": bc(inputs["norm_final_g"]),
        "cwA": cw(np.asarray(inputs["conv_a_w"][0]), 8), "cwG": cw(np.asarray(inputs["gdn_conv_w"][0]), 24),
        "cwF": cw(np.asarray(inputs["ffn_conv_w"][0]), 44),
        "alogB": np.ascontiguousarray(np.tile(bc(inputs["gdn_A_log"][0]), (1, 4))),
        "dtbB": np.ascontiguousarray(np.tile(bc(inputs["gdn_dt_bias"][0]), (1, 4))),
        "ngP": f(inputs["gdn_norm_g"][0]).reshape(128, 1),
        "cst": consts_np(),
    }
    maps = []
    for (b, s) in core_tokens:
        end = s + OWN
        xwin = np.zeros((W, D), np.float32)
        lo = max(0, end - W)
        xwin[W - (end - lo):] = x[b, lo:end]
        m = dict(shared)
        m["xw"] = xwin
        maps.append(m)
    return maps


def kernel(**inputs):
    x = np.asarray(inputs["x"])
    Bsz, S, _ = x.shape
    core_tokens = [(c // 4, (c % 4) * OWN_FULL) for c in range(NCORES)]
    nc = build(SEQ, OWN_FULL)
    maps = make_in_maps(inputs, SEQ, OWN_FULL, core_tokens)
    res = run_bass_kernel_spmd(nc, maps, core_ids=list(range(NCORES)))
    out = np.zeros((Bsz, S, D), np.float32)
    for c, (b, s) in enumerate(core_tokens):
        out[b, s:s + OWN_FULL] = np.asarray(res.results[c]["out"])
    return out
```
